# Optimizing a Trainium2 kernel written in Bass

```python
import math
import jax, jax.numpy as jnp
from jax import lax
import numpy as np

D_MODEL = 1024
BATCH = 8
SEQ = 4096
DEPTH = 2

HG_HEAD_DIM = 128
HG_HEADS = D_MODEL // HG_HEAD_DIM
HG_DIM = HG_HEADS * HG_HEAD_DIM
GLA_HEADS = 4
GLA_KEY_DIM = D_MODEL // 2
GLA_VALUE_DIM = D_MODEL
GLA_HEAD_K = GLA_KEY_DIM // GLA_HEADS
GLA_HEAD_V = GLA_VALUE_DIM // GLA_HEADS
GLA_GATE_RANK = 16
GLA_GATE_NORMALIZER = 16.0
GLA_IN_DIM = 2 * GLA_KEY_DIM + 2 * GLA_VALUE_DIM + GLA_GATE_RANK
D_FF = 2816
CONV_WIDTH = 3
CHUNK = 64
EPS = 1e-6
N_HG_LAYERS = (DEPTH + 1) // 2
N_GLA_LAYERS = DEPTH // 2

kernel_name = "hybrid_hgrn2_gla_convglu"


def rmsnorm(x, w):
    xf = x.astype(jnp.float32)
    xf = xf * lax.rsqrt(jnp.mean(xf * xf, axis=-1, keepdims=True) + EPS)
    return (xf * w.astype(jnp.float32)).astype(x.dtype)


def chunk_gated_linear_attention(q, k, v, log_f):
    B, T, H, dk = q.shape
    dv = v.shape[-1]
    N = T // CHUNK

    def to_chunks(a):
        return a.astype(jnp.float32).reshape(B, N, CHUNK, H, a.shape[-1]).transpose(1, 0, 3, 2, 4)

    qc = to_chunks(q) * (dk ** -0.5)
    kc, vc, gc = to_chunks(k), to_chunks(v), to_chunks(log_f)
    causal = jnp.tril(jnp.ones((CHUNK, CHUNK), dtype=bool))

    def step(S, chunk):
        q_, k_, v_, g_ = chunk
        b = jnp.cumsum(g_, axis=2)
        diff = b[:, :, :, None, :] - b[:, :, None, :, :]
        decay = jnp.exp(jnp.where(causal[:, :, None], diff, -jnp.inf))
        scores = jnp.einsum('bhtd,bhtsd,bhsd->bhts', q_, decay, k_)
        o = (jnp.einsum('bhts,bhsv->bhtv', scores, v_)
             + jnp.einsum('bhtd,bhdv->bhtv', q_ * jnp.exp(b), S))
        b_last = b[:, :, -1:, :]
        S = (jnp.exp(b_last[:, :, 0, :])[..., None] * S
             + jnp.einsum('bhsd,bhsv->bhdv', k_ * jnp.exp(b_last - b), v_))
        return S, o

    S0 = jnp.zeros((B, H, dk, dv), jnp.float32)
    _, o = lax.scan(step, S0, (qc, kc, vc, gc))
    return o.transpose(1, 0, 3, 2, 4).reshape(B, T, H, dv)


def gated_head_norm(o, g, norm_w, out_dtype):
    B, T, H, dv = o.shape
    o = o * lax.rsqrt(jnp.mean(o * o, axis=-1, keepdims=True) + EPS) * norm_w.astype(jnp.float32)
    o = o.reshape(B, T, H * dv) * jax.nn.silu(g.astype(jnp.float32))
    return o.astype(out_dtype)


def hgrn2_mixer(h, w_in, lb, norm_w, w_out):
    B, T, _ = h.shape
    proj = h @ w_in
    q, f, i, g = jnp.split(proj, 4, axis=-1)
    q = jax.nn.silu(q)
    forget = lb + (1.0 - lb) * jax.nn.sigmoid(f.astype(jnp.float32))
    k = 1.0 - forget
    log_f = jnp.log(forget)
    heads = lambda a: a.reshape(B, T, HG_HEADS, HG_HEAD_DIM)
    o = chunk_gated_linear_attention(heads(q), heads(k), heads(i), heads(log_f))
    return gated_head_norm(o, g, norm_w, h.dtype) @ w_out


def gla_mixer(h, w_in, w_gk_up, b_gk_up, norm_w, w_out):
    B, T, _ = h.shape
    proj = h @ w_in
    splits = np.cumsum([GLA_KEY_DIM, GLA_KEY_DIM, GLA_VALUE_DIM, GLA_VALUE_DIM]).tolist()
    q, k, v, g, gate_lr = jnp.split(proj, splits, axis=-1)
    gk = (gate_lr @ w_gk_up + b_gk_up).astype(jnp.float32)
    log_f = jax.nn.log_sigmoid(gk) / GLA_GATE_NORMALIZER
    hk = lambda a: a.reshape(B, T, GLA_HEADS, GLA_HEAD_K)
    o = chunk_gated_linear_attention(hk(q), hk(k), v.reshape(B, T, GLA_HEADS, GLA_HEAD_V), hk(log_f))
    return gated_head_norm(o, g, norm_w, h.dtype) @ w_out


def conv_glu_ffn(h, w_up, conv_w, conv_b, w_down):
    T = h.shape[1]
    u = h @ w_up
    up = jnp.pad(u, ((0, 0), (CONV_WIDTH - 1, 0), (0, 0)))
    u = conv_b + sum(conv_w[j] * up[:, j:j + T] for j in range(CONV_WIDTH))
    a, gate = jnp.split(u, 2, axis=-1)
    return (jax.nn.silu(gate) * a) @ w_down


def setup_inputs(seed: int = 0) -> dict:
    key = jax.random.key(seed)
    ks = jax.random.split(key, 20)
    nrm = lambda k, shape, fan_in: jax.random.normal(k, shape, jnp.float32) * (fan_in ** -0.5)
    gain = lambda k, shape: 1.0 + 0.02 * jax.random.normal(k, shape, jnp.float32)
    return {
        "x": jax.random.normal(ks[0], (BATCH, SEQ, D_MODEL), jnp.float32),
        "lb_logits": 0.1 * jax.random.normal(ks[1], (DEPTH + 1, HG_DIM), jnp.float32),
        "hg_w_in": nrm(ks[2], (N_HG_LAYERS, D_MODEL, 4 * HG_DIM), D_MODEL),
        "hg_norm_w": gain(ks[3], (N_HG_LAYERS, HG_HEAD_DIM)),
        "hg_w_out": nrm(ks[4], (N_HG_LAYERS, HG_DIM, D_MODEL), HG_DIM),
        "gla_w_in": nrm(ks[5], (N_GLA_LAYERS, D_MODEL, GLA_IN_DIM), D_MODEL),
        "gla_w_gk_up": nrm(ks[6], (N_GLA_LAYERS, GLA_GATE_RANK, GLA_KEY_DIM), GLA_GATE_RANK),
        "gla_b_gk_up": 0.1 * jax.random.normal(ks[7], (N_GLA_LAYERS, GLA_KEY_DIM), jnp.float32),
        "gla_norm_w": gain(ks[8], (N_GLA_LAYERS, GLA_HEAD_V)),
        "gla_w_out": nrm(ks[9], (N_GLA_LAYERS, GLA_VALUE_DIM, D_MODEL), GLA_VALUE_DIM),
        "norm_mixer_w": gain(ks[10], (DEPTH, D_MODEL)),
        "norm_ffn_w": gain(ks[11], (DEPTH, D_MODEL)),
        "ffn_w_up": nrm(ks[12], (DEPTH, D_MODEL, 2 * D_FF), D_MODEL),
        "ffn_conv_w": nrm(ks[13], (DEPTH, CONV_WIDTH, 2 * D_FF), CONV_WIDTH),
        "ffn_conv_b": 0.02 * jax.random.normal(ks[14], (DEPTH, 2 * D_FF), jnp.float32),
        "ffn_w_down": nrm(ks[15], (DEPTH, D_FF, D_MODEL), D_FF),
        "norm_final_w": gain(ks[16], (D_MODEL,)),
    }


def reference(x, lb_logits, hg_w_in, hg_norm_w, hg_w_out, gla_w_in, gla_w_gk_up, gla_b_gk_up,
              gla_norm_w, gla_w_out, norm_mixer_w, norm_ffn_w, ffn_w_up, ffn_conv_w, ffn_conv_b,
              ffn_w_down, norm_final_w):
    lb_table = jnp.cumsum(jax.nn.softmax(lb_logits.astype(jnp.float32), axis=0), axis=0)
    h = x
    for i in range(DEPTH):
        y = rmsnorm(h, norm_mixer_w[i])
        j = i // 2
        if i % 2 == 0:
            mix = hgrn2_mixer(y, hg_w_in[j], lb_table[i], hg_norm_w[j], hg_w_out[j])
        else:
            mix = gla_mixer(y, gla_w_in[j], gla_w_gk_up[j], gla_b_gk_up[j], gla_norm_w[j], gla_w_out[j])
        h = h + mix
        y = rmsnorm(h, norm_ffn_w[i])
        h = h + conv_glu_ffn(y, ffn_w_up[i], ffn_conv_w[i], ffn_conv_b[i], ffn_w_down[i])
    return rmsnorm(h, norm_final_w)
```

```python
import contextlib
import numpy as np
import concourse.bass as bass
import concourse.mybir as mybir
from concourse.bass_utils import run_bass_kernel_spmd

F32 = mybir.dt.float32
F32R = mybir.dt.float32r
BF16 = mybir.dt.bfloat16
AF = mybir.ActivationFunctionType
ALU = mybir.AluOpType

D = 1024
T = 4096
TT = 512
DFF = 2816
NCT = 22
NSLOT = 7
ENGS = ("pe", "act", "dve", "pool", "sp")


class Buf:
    __slots__ = ("name", "last_w", "readers")

    def __init__(self, name=""):
        self.name = name
        self.last_w = None
        self.readers = {}


class Rec:
    def __init__(self, same_engine_sync=("act", "dve", "pool")):
        self.ops = {e: [] for e in ENGS}
        self.cnt = {e: 0 for e in ENGS}
        self.waited = {}
        self.same_sync = set(same_engine_sync)
        self.dma_sems = []
        self.limit = None

    def new_dma_sem(self, name):
        self.cnt[name] = 0
        self.dma_sems.append(name)
        return name

    def op(self, eng, issue, reads=(), writes=(), signal=True, dma_sem=None, force=False):
        self.total = getattr(self, "total", 0) + 1
        if self.limit is not None and self.total > self.limit and not force:
            return None
        need = {}

        def add(tok):
            if tok is None:
                return
            k, v = tok
            if need.get(k, 0) < v:
                need[k] = v

        for b in reads:
            add(b.last_w)
        for b in writes:
            add(b.last_w)
            for k, v in b.readers.items():
                add((k, v))
        waits = []
        for k, v in need.items():
            if k == eng and eng not in self.same_sync:
                continue
            if k == eng and v > self.cnt[eng]:
                continue
            if self.waited.get((eng, k), 0) >= v:
                continue
            self.waited[(eng, k)] = v
            waits.append((k, v))
        if dma_sem is not None:
            self.cnt[dma_sem] += 16
            tok = (dma_sem, self.cnt[dma_sem])
            inc = (dma_sem, 16)
        elif signal:
            self.cnt[eng] += 1
            tok = (eng, self.cnt[eng])
            inc = (eng, 1)
        else:
            tok = (eng, self.cnt[eng] + 1)
            inc = None
        for b in reads:
            if b.readers.get(tok[0], 0) < tok[1]:
                b.readers[tok[0]] = tok[1]
        for b in writes:
            b.last_w = tok
            b.readers = {}
        self.ops[eng].append((waits, issue, inc))
        return tok

    def final_wait(self, eng, toks):
        self.ops[eng].append(([(k, v) for (k, v) in toks], None, None))

    def emit(self, nc):
        names = [e for e in ENGS if e != "sp"] + self.dma_sems
        with contextlib.ExitStack() as st:
            sems = {n: st.enter_context(nc.semaphore("s_" + n)) for n in names}
            block = st.enter_context(nc.Block())

            def replay(name, e):
                for waits, issue, inc in self.ops[name]:
                    for k, v in waits:
                        e.wait_ge(sems[k], v)
                    if issue is None:
                        continue
                    ins = issue(e)
                    if inc is not None:
                        ins.then_inc(sems[inc[0]], inc[1])

            @block.tensor
            def _(e):
                replay("pe", e)

            @block.scalar
            def _(e):
                replay("act", e)

            @block.vector
            def _(e):
                replay("dve", e)

            @block.gpsimd
            def _(e):
                replay("pool", e)

            @block.sync
            def _(e):
                replay("sp", e)


class V:
    __slots__ = ("ap", "bufs")

    def __init__(self, ap, bufs):
        self.ap = ap
        self.bufs = list(bufs)

    def __getitem__(self, k):
        return V(self.ap[k], self.bufs)

    def bitcast(self, dt):
        return V(self.ap.bitcast(dt), self.bufs)

    def rearrange(self, s, **kw):
        return V(self.ap.rearrange(s, **kw), self.bufs)

    def bcast3(self, n):
        return V(self.ap.unsqueeze(2).broadcast_to(list(self.ap.shape) + [n]), self.bufs)


def plan(layers):
    ch = []
    for l in layers:
        if l == 0:
            for hp in range(4):
                for k in ("q", "f", "i", "g"):
                    ch.append(("hg_in", k, hp))
            for hp in range(4):
                ch.append(("hg_out", hp))
        else:
            ch.append(("gla_lr",))
            for hd in range(4):
                ch.append(("gla_qk", hd))
                ch.append(("gla_v", hd))
                ch.append(("gla_g", hd))
            for hd in range(4):
                ch.append(("gla_out", hd))
        for g in range(2):
            cs = list(range(g * 11, (g + 1) * 11))
            for c in cs:
                ch.append(("up", l, c))
            for half in range(2):
                for part in (cs[0:4], cs[4:8], cs[8:11]):
                    ch.append(("down", l, half, tuple(part)))
    return ch


def _cols(W, cols):
    sub = W[:, cols].reshape(8, 128, len(cols)).transpose(1, 0, 2)
    return sub.reshape(128, -1)


def pack_wstream(inp, layers):
    chunks = plan(layers)
    out = np.zeros((len(chunks), 128, 2048), np.float32)
    hg_in = inp["hg_w_in"][0]
    hg_out = inp["hg_w_out"][0]
    gl_in = inp["gla_w_in"][0]
    gl_out = inp["gla_w_out"][0]
    for i, c in enumerate(chunks):
        kind = c[0]
        if kind == "hg_in":
            base = {"q": 0, "f": 1024, "i": 2048, "g": 3072}[c[1]] + c[2] * 256
            out[i] = _cols(hg_in, np.arange(base, base + 256))
        elif kind == "hg_out":
            r0 = c[1] * 256
            out[i] = hg_out[r0:r0 + 256].reshape(2, 128, 1024).transpose(1, 0, 2).reshape(128, 2048)
        elif kind == "gla_lr":
            out[i, :, :1024] = _cols(gl_in, np.arange(2960, 3088))
        elif kind == "gla_qk":
            hd = c[1]
            cols = np.concatenate([np.arange(hd * 128, hd * 128 + 128), np.arange(512 + hd * 128, 512 + hd * 128 + 128)])
            out[i] = _cols(gl_in, cols)
        elif kind == "gla_v":
            out[i] = _cols(gl_in, np.arange(1024 + c[1] * 256, 1024 + c[1] * 256 + 256))
        elif kind == "gla_g":
            out[i] = _cols(gl_in, np.arange(2048 + c[1] * 256, 2048 + c[1] * 256 + 256))
        elif kind == "gla_out":
            r0 = c[1] * 256
            out[i] = gl_out[r0:r0 + 256].reshape(2, 128, 1024).transpose(1, 0, 2).reshape(128, 2048)
        elif kind == "up":
            l, ct = c[1], c[2]
            cols = np.concatenate([np.arange(ct * 128, ct * 128 + 128), np.arange(DFF + ct * 128, DFF + ct * 128 + 128)])
            out[i] = _cols(inp["ffn_w_up"][l], cols)
        elif kind == "down":
            l, half, part = c[1], c[2], c[3]
            wd = inp["ffn_w_down"][l]
            for j, ct in enumerate(part):
                out[i, :, j * 512:(j + 1) * 512] = wd[ct * 128:(ct + 1) * 128, half * 512:(half + 1) * 512]
    return out


_CO = {}
_off = 0
for _n, _w in (("nm", 16), ("nf", 16), ("nfin", 8), ("lbl", 24), ("hgnw", 1), ("bgk", 4), ("glanw", 2),
               ("cw", 264), ("cb", 88), ("eps", 1), ("one", 1), ("lnc", 1), ("ident", 128), ("cmask", 128),
               ("smask", 512), ("w2", 512)):
    _CO[_n] = _off
    _off += _w
NCONST = _off


def pack_consts(inp):
    c = np.zeros((128, NCONST), np.float32)

    def put(name, arr):
        arr = np.asarray(arr, np.float32)
        c[:, _CO[name]:_CO[name] + arr.shape[1]] = arr

    pd = lambda v: np.asarray(v, np.float32).reshape(-1, 128).T
    put("nm", np.concatenate([pd(inp["norm_mixer_w"][l]) for l in range(2)], axis=1))
    put("nf", np.concatenate([pd(inp["norm_ffn_w"][l]) for l in range(2)], axis=1))
    put("nfin", pd(inp["norm_final_w"]))
    put("lbl", np.concatenate([pd(inp["lb_logits"][k]) for k in range(3)], axis=1))
    put("hgnw", pd(inp["hg_norm_w"][0]))
    put("bgk", pd(inp["gla_b_gk_up"][0]))
    put("glanw", pd(inp["gla_norm_w"][0]))
    cw = np.asarray(inp["ffn_conv_w"], np.float32)
    put("cw", np.concatenate([pd(cw[l, j]) for l in range(2) for j in range(3)], axis=1))
    cb = np.asarray(inp["ffn_conv_b"], np.float32)
    put("cb", np.concatenate([pd(cb[l]) for l in range(2)], axis=1))
    put("eps", np.full((128, 1), 1e-6, np.float32))
    put("one", np.ones((128, 1), np.float32))
    put("lnc", np.full((128, 1), np.log(128.0 ** -0.5), np.float32))
    put("ident", np.eye(128, dtype=np.float32))
    s = np.arange(128)[:, None]
    t = np.arange(128)[None, :]
    put("cmask", ((s // 64 == t // 64) & (s <= t)).astype(np.float32))
    sm = np.ones((128, 512), np.float32)
    sm[:, ::64] = 0.0
    put("smask", sm)
    w2 = np.zeros((128, 512), np.float32)
    w2[112:128, :] = np.asarray(inp["gla_w_gk_up"][0], np.float32)
    put("w2", w2)
    return c


def build(NT, layers=(0, 1), final_norm=True, same_sync=("act", "dve", "pool")):
    nc = bass.Bass("TRN2", target_bir_lowering=False)
    nc.dge_precook = False
    chunks = plan(layers)
    NCH = len(chunks)
    xT_d = nc.dram_tensor("xT", [D, NT * TT], F32, kind="ExternalInput").ap()
    ws_d = nc.dram_tensor("wstream", [NCH, 128, 2048], F32R, kind="ExternalInput").ap()
    cs_d = nc.dram_tensor("consts", [128, NCONST], F32, kind="ExternalInput").ap()
    out_d = nc.dram_tensor("outT", [D, NT * TT], F32, kind="ExternalOutput").ap()

    R = Rec(same_engine_sync=same_sync)
    import os as _os
    if _os.environ.get("KLIMIT"):
        R.limit = int(_os.environ["KLIMIT"])
    with contextlib.ExitStack() as st:
        def sbt(name, shape, dt=F32, nbuf=1):
            t = st.enter_context(nc.sbuf_tensor(name, shape, dt))
            return t, [Buf(f"{name}{i}") for i in range(nbuf)]

        def tile(name, shape, dt=F32):
            t, b = sbt(name, shape, dt)
            return V(t[:], b)

        hT_t, hT_b = sbt("hT", [128, 8, TT], F32, 8)
        yT_t, yT_b = sbt("yT", [128, 8, TT], F32R, 8)
        oT_t, oT_b = sbt("oT", [128, 8, TT], F32R, 8)
        aT_t, aT_b = sbt("aT", [128, 11, TT], F32R, 11)
        hT = [V(hT_t[:, i, :], [hT_b[i]]) for i in range(8)]
        yT = [V(yT_t[:, i, :], [yT_b[i]]) for i in range(8)]
        oT = [V(oT_t[:, i, :], [oT_b[i]]) for i in range(8)]
        aT = [V(aT_t[:, i, :], [aT_b[i]]) for i in range(11)]
        hT_all = V(hT_t[:], hT_b)
        CS = tile("consts_sb", [128, NCONST])
        slots = [tile(f"slot{i}", [128, 2048], F32R) for i in range(NSLOT)]
        slot_sems = [R.new_dma_sem(f"ws{i}") for i in range(NSLOT)]

        def cst(name, j=0, w=1):
            o = _CO[name] + j
            return CS[:, o:o + w]

        ones1024 = tile("ones1024", [128, 128], F32R)
        ones128 = tile("ones128", [128, 128], F32R)
        ones256 = tile("ones256", [128, 128], F32R)
        ident_bf = tile("ident_bf", [128, 128], BF16)
        lb = tile("lb", [128, 8])
        clb = tile("clb", [128, 8])
        lbe = tile("lbe", [128, 24])
        nbgk = tile("nbgk", [128, 4])
        W2r = tile("W2r", [128, 512], F32R)
        G_sb = tile("G_sb", [128, TT], F32R)
        lnv = tile("lnv", [128, TT])
        rstd = tile("rstd", [128, TT])
        S32hg = [tile(f"S32hg{h}", [128, 128]) for h in range(8)]
        Sbfhg = [tile(f"Sbfhg{h}", [128, 128], BF16) for h in range(8)]
        S32gl = [tile(f"S32gl{h}", [128, 256]) for h in range(4)]
        Sbfgl = [tile(f"Sbfgl{h}", [128, 256], BF16) for h in range(4)]
        tails_t = st.enter_context(nc.sbuf_tensor("tails", [128, 2, 44, 2], F32))
        tails = [[V(tails_t[:, l, ct, :], [Buf(f"tail{l}_{ct}")]) for ct in range(44)] for l in range(2)]
        TM = []
        for s in range(2):
            d = {}
            for n in ("C", "D", "SG1", "E"):
                d[n] = tile(f"t{n}{s}", [128, TT])
            d["XA"] = tile(f"tXA{s}", [128, TT + 2])
            d["XG"] = tile(f"tXG{s}", [128, TT + 2])
            d["A"] = d["XA"][:, 0:TT]
            d["B"] = d["XG"][:, 0:TT]
            d["UA"] = d["C"]
            d["UG"] = d["D"]
            d["OSQ0"] = tile(f"tOSQ0{s}", [128, TT], F32R)
            d["OSQ1"] = tile(f"tOSQ1{s}", [128, TT], F32R)
            for n in ("kT", "kkT", "qT"):
                d[n] = tile(f"t{n}{s}", [128, TT], BF16)
            d["kktok"] = tile(f"tkktok{s}", [128, 4, 128], BF16)
            d["vtok"] = tile(f"tvtok{s}", [128, 4, 256], BF16)
            d["scT"] = [tile(f"tscT{s}{i}", [128, 128], BF16) for i in range(2)]
            d["ebl"] = tile(f"tebl{s}", [128, 8])
            TM.append(d)
        banks = []
        for i in range(8):
            t = st.enter_context(nc.psum_tensor(f"bank{i}", [128, TT], F32))
            banks.append((t, [Buf(f"bk{i}a"), Buf(f"bk{i}b")]))
        pstate = {"big": 0, "sc": 0, "tr": 0, "nbig": 6}

        def pbank():
            n = pstate["nbig"]
            i = pstate["big"] % n
            pstate["big"] += 1
            t, b = banks[i]
            return V(t[:], b)

        def psc_region():
            h = pstate["sc"] % 2
            pstate["sc"] += 1
            t, b = banks[6]
            return V(t[:, h * 128:(h + 1) * 128], [b[0]])

        def pU_region(nv):
            t, b = banks[6]
            return V(t[:, 256:256 + nv], [b[1]])

        def ptrans():
            h = pstate["tr"] % 2
            pstate["tr"] += 1
            t, b = banks[7]
            return V(t[:, h * 256:(h + 1) * 256].bitcast(BF16), [b[h]])

        def bufs_of(*vs):
            r = []
            for v in vs:
                if isinstance(v, V):
                    r += v.bufs
            return r

        def apof(v):
            return v.ap if isinstance(v, V) else v

        def mm(out, lhsT, rhs, start=True, stop=True, signal=None):
            sig = stop if signal is None else signal
            R.op("pe", lambda e: e.matmul(out.ap, lhsT.ap, rhs.ap, start=start, stop=stop),
                 reads=lhsT.bufs + rhs.bufs, writes=out.bufs, signal=sig)

        def tr(out, in_):
            R.op("pe", lambda e: e.transpose(out.ap, in_.ap, ident_bf.ap),
                 reads=in_.bufs + ident_bf.bufs, writes=out.bufs, signal=True)

        def ACT(out, in_, func, bias=None, scale=None):
            kw = {}
            if bias is not None:
                kw["bias"] = apof(bias)
            if scale is not None:
                kw["scale"] = apof(scale)
            R.op("act", lambda e: e.activation(out=out.ap, in_=in_.ap, func=func, **kw),
                 reads=bufs_of(in_, bias, scale), writes=out.bufs)

        def TTo(eng, out, in0, in1, op):
            R.op(eng, lambda e: e.tensor_tensor(out=out.ap, in0=in0.ap, in1=in1.ap, op=op),
                 reads=bufs_of(in0, in1), writes=out.bufs)

        def TS(eng, out, in0, s1, op0, s2=None, op1=None):
            kw = {}
            if op1 is None and eng == "pool":
                if op0 == ALU.add:
                    s2, op1 = 1.0, ALU.mult
                elif op0 == ALU.mult:
                    s2, op1 = 0.0, ALU.add
            if op1 is not None:
                kw["op1"] = op1
            R.op(eng, lambda e: e.tensor_scalar(out=out.ap, in0=in0.ap, scalar1=apof(s1), scalar2=apof(s2), op0=op0, **kw),
                 reads=bufs_of(in0, s1, s2), writes=out.bufs)

        def STT(eng, out, in0, scalar, in1, op0, op1):
            R.op(eng, lambda e: e.scalar_tensor_tensor(out=out.ap, in0=in0.ap, scalar=apof(scalar), in1=in1.ap, op0=op0, op1=op1),
                 reads=bufs_of(in0, scalar, in1), writes=out.bufs)

        def SCAN(out, d0, d1):
            R.op("dve", lambda e: e.tensor_tensor_scan(out=out.ap, data0=d0.ap, data1=d1.ap, initial=0.0, op0=ALU.mult, op1=ALU.add),
                 reads=bufs_of(d0, d1), writes=out.bufs)

        def RECIP(eng, out, in_):
            R.op(eng, lambda e: e.reciprocal(out=out.ap, in_=in_.ap), reads=in_.bufs, writes=out.bufs)

        def CP(eng, out, in_):
            if eng == "act":
                R.op("act", lambda e: e.copy(out=out.ap, in_=in_.ap), reads=in_.bufs, writes=out.bufs)
            else:
                R.op(eng, lambda e: e.tensor_copy(out=out.ap, in_=in_.ap), reads=in_.bufs, writes=out.bufs)

        def MEMSET(eng, out, val):
            R.op(eng, lambda e: e.memset(out.ap, val), writes=out.bufs)

        wst = {"next_dma": 0, "cur": 0}
        total_chunks = NCH * NT

        def ws_get(kind, lag=1):
            j = wst["cur"]
            wst["cur"] += 1
            assert chunks[j % NCH][0] == kind, (chunks[j % NCH], kind)
            upto = min(total_chunks - 1, j + NSLOT - lag)
            while wst["next_dma"] <= upto:
                m = wst["next_dma"]
                wst["next_dma"] += 1
                k = m % NCH
                sl = slots[m % NSLOT]
                if chunks[k][0] == "gla_lr":
                    R.op("sp", lambda e, sl=sl, k=k: e.dma_start(out=sl.ap[:, 0:1024], in_=ws_d[k, :, 0:1024]),
                         writes=sl.bufs, dma_sem=slot_sems[m % NSLOT])
                else:
                    R.op("sp", lambda e, sl=sl, k=k: e.dma_start(out=sl.ap, in_=ws_d[k]),
                         writes=sl.bufs, dma_sem=slot_sems[m % NSLOT])
            return slots[j % NSLOT]

        s_c = R.new_dma_sem("ld_c")
        s_h = R.new_dma_sem("ld_h")
        s_o = R.new_dma_sem("st_o")
        R.op("sp", lambda e: e.dma_start(out=CS.ap, in_=cs_d), writes=CS.bufs, dma_sem=s_c)
        onesf = tile("onesf", [128, 128])
        for ot, val in ((ones1024, 1.0 / 1024), (ones128, 1.0 / 128), (ones256, 1.0 / 256)):
            MEMSET("pool", onesf, val)
            CP("act", ot, onesf)
        for h in range(8):
            MEMSET("pool", S32hg[h], 0.0)
            MEMSET("pool", Sbfhg[h], 0.0)
        for h in range(4):
            MEMSET("pool", S32gl[h], 0.0)
            MEMSET("pool", Sbfgl[h], 0.0)
        tails_all = V(tails_t[:], [tails[l][ct].bufs[0] for l in range(2) for ct in range(44)])
        MEMSET("pool", tails_all, 0.0)
        CP("act", ident_bf, cst("ident", 0, 128))
        CP("act", W2r, cst("w2", 0, 512))
        ACT(lbe, cst("lbl", 0, 24), AF.Exp)
        TTo("pool", lb, lbe[:, 0:8], lbe[:, 8:16], ALU.add)
        TTo("pool", lb, lb, lbe[:, 16:24], ALU.add)
        RECIP("dve", lb, lb)
        TTo("dve", lb, lb, lbe[:, 0:8], ALU.mult)
        ACT(clb, lb, AF.Ln, scale=-1.0, bias=cst("one"))
        TS("pool", nbgk, cst("bgk", 0, 4), -1.0, ALU.mult)
        one = cst("one")
        eps = cst("eps")
        cmask = cst("cmask", 0, 128)
        smask = cst("smask", 0, 512)

        def rmsnorm(wname, woff, dst):
            for dt in range(8):
                ACT(yT[dt], hT[dt], AF.Square)
            pb = pbank()
            for dt in range(8):
                mm(pb, ones1024, yT[dt], start=(dt == 0), stop=(dt == 7))
            ACT(lnv, pb, AF.Ln, bias=eps)
            ACT(rstd, lnv, AF.Exp, scale=-0.5)
            for dt in range(8):
                if dt % 2 == 0:
                    STT("dve", dst[dt], hT[dt], cst(wname, woff + dt), rstd, ALU.mult, ALU.mult)
                else:
                    ACT(dst[dt], hT[dt], AF.Identity, scale=cst(wname, woff + dt))
                    TTo("pool", dst[dt], dst[dt], rstd, ALU.mult)

        def out_proj(kind):
            so = [ws_get(kind, lag=i + 1) for i in range(4)]
            for ft in range(8):
                pw = pbank()
                for h in range(8):
                    w = so[h // 2].rearrange("p (a f) -> p a f", a=2)[:, h % 2, ft * 128:(ft + 1) * 128]
                    mm(pw, w, oT[h], start=(h == 0), stop=(h == 7))
                TTo("dve", hT[ft], hT[ft], pw, ALU.add)

        def recurrence(tm, a_cols, n_vt, Sbf, S32, vcol0):
            kT, kkT, qT, kktok, vtok, ebl = tm["kT"], tm["kkT"], tm["qT"], tm["kktok"], tm["vtok"], tm["ebl"]
            po = [pbank() for _ in range(n_vt)]
            nv = 128 * n_vt
            for blk in range(4):
                cols = slice(blk * 128, (blk + 1) * 128)
                psc = psc_region()
                mm(psc, kT[:, cols], qT[:, cols])
                scT = tm["scT"][blk % 2]
                TTo("dve", scT, psc, cmask, ALU.mult)
                for vt in range(n_vt):
                    mm(po[vt][:, cols], vtok[:, blk, vcol0 + vt * 128:vcol0 + (vt + 1) * 128], scT,
                       start=True, stop=False, signal=False)
                for ci in range(2):
                    c = blk * 2 + ci
                    ccols = slice(c * 64, (c + 1) * 64)
                    rows = slice(ci * 64, (ci + 1) * 64)
                    for vt in range(n_vt):
                        mm(po[vt][:, ccols], Sbf[:, vt * 128:(vt + 1) * 128], qT[:, ccols],
                           start=False, stop=True, signal=(vt == n_vt - 1))
                    pU = pU_region(nv)
                    mm(pU, kktok[rows, blk, :], vtok[rows, blk, vcol0:vcol0 + nv])
                    STT("dve", S32, S32, ebl[:, c:c + 1], pU, ALU.mult, ALU.add)
                    CP("act", Sbf, S32)
            return po

        def kk_transposes(tm):
            ptr = ptrans()
            for blk in range(4):
                tr(ptr[:, blk * 128:(blk + 1) * 128], tm["kkT"][:, blk * 128:(blk + 1) * 128])
            CP("act", tm["kktok"].rearrange("p a b -> p (a b)"), ptr)

        def silu_gate(dst, pg):
            ACT(dst, pg, AF.Exp, scale=-1.0)
            TS("pool", dst, dst, 1.0, ALU.add)
            RECIP("dve", dst, dst)
            TTo("dve", dst, pg, dst, ALU.mult)

        def mixer_hg():
            rmsnorm("nm", 0, yT)
            for hp in range(4):
                sq = ws_get("hg_in", 1).rearrange("p (dt f) -> p dt f", dt=8)
                sf = ws_get("hg_in", 2).rearrange("p (dt f) -> p dt f", dt=8)
                si = ws_get("hg_in", 3).rearrange("p (dt f) -> p dt f", dt=8)
                sg = ws_get("hg_in", 4).rearrange("p (dt f) -> p dt f", dt=8)
                tmv = TM[hp % 2]
                for blk in range(4):
                    pv = pbank()[:, 0:256]
                    for dt in range(8):
                        mm(pv, yT[dt][:, blk * 128:(blk + 1) * 128], si[:, dt, :], start=(dt == 0), stop=(dt == 7))
                    CP("act", tmv["vtok"][:, blk, :], pv)
                for a in range(2):
                    h = 2 * hp + a
                    tm = dict(TM[h % 2])
                    tm["vtok"] = tmv["vtok"]
                    A_, B_, C_, D_ = tm["A"], tm["B"], tm["C"], tm["D"]
                    fc = slice(a * 128, (a + 1) * 128)
                    pf = pbank()
                    for dt in range(8):
                        mm(pf, sf[:, dt, fc], yT[dt], start=(dt == 0), stop=(dt == 7))
                    pq = pbank()
                    for dt in range(8):
                        mm(pq, sq[:, dt, fc], yT[dt], start=(dt == 0), stop=(dt == 7))
                    pg = pbank()
                    for dt in range(8):
                        mm(pg, sg[:, dt, fc], yT[dt], start=(dt == 0), stop=(dt == 7))
                    ACT(A_, pf, AF.Exp, scale=-1.0)
                    ACT(B_, A_, AF.Ln, bias=one)
                    ACT(C_, A_, AF.Ln, scale=lb[:, h:h + 1], bias=one)
                    TTo("pool", C_, C_, B_, ALU.subtract)
                    SCAN(D_, smask, C_)
                    TTo("dve", A_, pf, B_, ALU.add)
                    TTo("pool", A_, A_, D_, ALU.add)
                    ACT(tm["kT"], A_, AF.Exp, scale=-1.0, bias=clb[:, h:h + 1])
                    ACT(tm["ebl"], D_[:, 63::64], AF.Exp)
                    TTo("pool", tm["kkT"].rearrange("p (c k) -> p c k", k=64),
                        tm["kT"].rearrange("p (c k) -> p c k", k=64), tm["ebl"].bcast3(64), ALU.mult)
                    kk_transposes(tm)
                    ACT(C_, pq, AF.Exp, scale=-1.0)
                    TS("pool", C_, C_, 1.0, ALU.add)
                    RECIP("dve", C_, C_)
                    ACT(B_, D_, AF.Exp, bias=cst("lnc"))
                    TTo("dve", C_, pq, C_, ALU.mult)
                    TTo("pool", tm["qT"], C_, B_, ALU.mult)
                    silu_gate(A_, pg)
                    po = recurrence(tm, None, 1, Sbfhg[h], S32hg[h], a * 128)[0]
                    ACT(tm["OSQ0"], po, AF.Square)
                    pss = pbank()
                    mm(pss, ones128, tm["OSQ0"])
                    ACT(C_, pss, AF.Ln, bias=eps)
                    ACT(C_, C_, AF.Exp, scale=-0.5)
                    STT("dve", D_, po, cst("hgnw"), C_, ALU.mult, ALU.mult)
                    TTo("pool", oT[h], D_, A_, ALU.mult)
            out_proj("hg_out")

        def mixer_gla():
            rmsnorm("nm", 8, yT)
            slr = ws_get("gla_lr", 1)[:, 0:1024].rearrange("p (dt f) -> p dt f", dt=8)
            pG = pbank()
            for dt in range(8):
                mm(pG, slr[:, dt, :], yT[dt], start=(dt == 0), stop=(dt == 7))
            CP("act", G_sb, pG)
            for hd in range(4):
                sqk = ws_get("gla_qk", 1).rearrange("p (dt two f) -> p dt two f", dt=8, two=2)
                sv = ws_get("gla_v", 2).rearrange("p (dt f) -> p dt f", dt=8)
                sg = ws_get("gla_g", 3).rearrange("p (dt f) -> p dt f", dt=8)
                tm = TM[hd % 2]
                A_, B_, C_, D_ = tm["A"], tm["B"], tm["C"], tm["D"]
                SG = [tm["E"], tm["SG1"]]
                for vt in range(2):
                    pg = pbank()
                    for dt in range(8):
                        mm(pg, sg[:, dt, vt * 128:(vt + 1) * 128], yT[dt], start=(dt == 0), stop=(dt == 7))
                    silu_gate(SG[vt], pg)
                for blk in range(4):
                    pv = pbank()[:, 0:256]
                    for dt in range(8):
                        mm(pv, yT[dt][:, blk * 128:(blk + 1) * 128], sv[:, dt, :], start=(dt == 0), stop=(dt == 7))
                    CP("act", tm["vtok"][:, blk, :], pv)
                pgk = pbank()
                mm(pgk, W2r[:, hd * 128:(hd + 1) * 128], G_sb)
                ACT(A_, pgk, AF.Exp, scale=-1.0, bias=nbgk[:, hd:hd + 1])
                ACT(A_, A_, AF.Ln, bias=one)
                SCAN(D_, smask, A_)
                ACT(B_, D_, AF.Exp, scale=-1.0 / 16)
                ACT(C_, D_, AF.Exp, scale=1.0 / 16)
                ACT(tm["ebl"], D_[:, 63::64], AF.Exp, scale=-1.0 / 16)
                pq = pbank()
                for dt in range(8):
                    mm(pq, sqk[:, dt, 0, :], yT[dt], start=(dt == 0), stop=(dt == 7))
                pk = pbank()
                for dt in range(8):
                    mm(pk, sqk[:, dt, 1, :], yT[dt], start=(dt == 0), stop=(dt == 7))
                STT("dve", tm["qT"], pq, 128 ** -0.5, B_, ALU.mult, ALU.mult)
                TTo("dve", tm["kT"], pk, C_, ALU.mult)
                TTo("pool", tm["kkT"].rearrange("p (c k) -> p c k", k=64),
                    tm["kT"].rearrange("p (c k) -> p c k", k=64), tm["ebl"].bcast3(64), ALU.mult)
                kk_transposes(tm)
                po = recurrence(tm, None, 2, Sbfgl[hd], S32gl[hd], 0)
                OSQ = [tm["OSQ0"], tm["OSQ1"]]
                for vt in range(2):
                    ACT(OSQ[vt], po[vt], AF.Square)
                pss = pbank()
                mm(pss, ones256, OSQ[0], start=True, stop=False)
                mm(pss, ones256, OSQ[1], start=False, stop=True)
                ACT(C_, pss, AF.Ln, bias=eps)
                ACT(C_, C_, AF.Exp, scale=-0.5)
                for vt in range(2):
                    STT("dve", D_, po[vt], cst("glanw", vt), C_, ALU.mult, ALU.mult)
                    TTo("pool", oT[hd * 2 + vt], D_, SG[vt], ALU.mult)
            out_proj("gla_out")

        def ffn(l):
            rmsnorm("nf", 8 * l, yT)
            pstate["nbig"] = 8
            for g in range(2):
                for ci in range(11):
                    c = g * 11 + ci
                    su = ws_get("up", 1).rearrange("p (dt two f) -> p dt two f", dt=8, two=2)
                    tm = TM[ci % 2]
                    pa = pbank()
                    for dt in range(8):
                        mm(pa, su[:, dt, 0, :], yT[dt], start=(dt == 0), stop=(dt == 7))
                    pg = pbank()
                    for dt in range(8):
                        mm(pg, su[:, dt, 1, :], yT[dt], start=(dt == 0), stop=(dt == 7))
                    for (ps, ct, xs, u, e1, e2) in ((pa, c, tm["XA"], tm["UA"], "dve", "dve"),
                                                     (pg, NCT + c, tm["XG"], tm["UG"], "dve", "dve")):
                        cw = lambda j, ct=ct: cst("cw", (l * 3 + j) * 44 + ct)
                        CP("act", xs[:, 2:TT + 2], ps)
                        CP("pool", xs[:, 0:2], tails[l][ct])
                        ACT(u, xs[:, 0:TT], AF.Identity, scale=cw(0), bias=cst("cb", l * 44 + ct))
                        STT(e1, u, xs[:, 1:TT + 1], cw(1), u, ALU.mult, ALU.add)
                        STT(e2, u, xs[:, 2:TT + 2], cw(2), u, ALU.mult, ALU.add)
                        CP("pool", tails[l][ct], xs[:, TT:TT + 2])
                    E = tm["E"]
                    ACT(E, tm["UG"], AF.Exp, scale=-1.0)
                    TS("pool", E, E, 1.0, ALU.add)
                    RECIP("dve", E, E)
                    TTo("pool", tm["UG"], tm["UG"], E, ALU.mult)
                    TTo("pool", aT[ci], tm["UG"], tm["UA"], ALU.mult)
                for half in range(2):
                    accs = [pbank() for _ in range(4)]
                    for part in (range(0, 4), range(4, 8), range(8, 11)):
                        sd = ws_get("down", 1).rearrange("p (i f) -> p i f", i=4)
                        for i, ci in enumerate(part):
                            for j in range(4):
                                mm(accs[j], sd[:, i, j * 128:(j + 1) * 128], aT[ci], start=(ci == 0), stop=(ci == 10))
                    for j in range(4):
                        ft = half * 4 + j
                        TTo("dve", hT[ft], hT[ft], accs[j], ALU.add)
            pstate["nbig"] = 6
            pstate["big"] = 0

        out_toks = []
        for tI in range(NT):
            t0 = tI * TT
            R.op("sp", lambda e, t0=t0: e.dma_start(out=hT_t[:], in_=xT_d[:, t0:t0 + TT].rearrange("(dt p) t -> p dt t", p=128)),
                 writes=hT_b, dma_sem=s_h)
            for l in layers:
                if l == 0:
                    mixer_hg()
                else:
                    mixer_gla()
                ffn(l)
            if final_norm:
                rmsnorm("nfin", 0, hT)
            tok = R.op("sp", lambda e, t0=t0: e.dma_start(out=out_d[:, t0:t0 + TT].rearrange("(dt p) t -> p dt t", p=128), in_=hT_t[:]),
                       reads=hT_b, dma_sem=s_o, force=True)
            out_toks.append(tok)
        R.final_wait("sp", [out_toks[-1]])
        assert wst["cur"] == total_chunks
        print("total ops", R.total, {k: v for k, v in R.cnt.items()})
        R.emit(nc)
    return nc


_PROGS = {}


def _prog(NT, layers, final_norm):
    key = (NT, tuple(layers), final_norm)
    if key not in _PROGS:
        _PROGS[key] = build(NT, layers, final_norm)
    return _PROGS[key]


FUSED = False


def kernel(**inputs):
    inp = {k: np.asarray(v) for k, v in inputs.items()}
    x = inp["x"].astype(np.float32, copy=False)
    B = x.shape[0]
    consts = pack_consts(inp)
    xT = [np.ascontiguousarray(x[b].T) for b in range(B)]
    if FUSED:
        stages = [((0, 1), True)]
    else:
        stages = [((0,), False), ((1,), True)]
    cur = xT
    for layers, fin in stages:
        ws = pack_wstream(inp, layers)
        nc = _prog(T // TT, layers, fin)
        in_maps = [{"xT": cur[b], "wstream": ws, "consts": consts} for b in range(B)]
        res = run_bass_kernel_spmd(nc, in_maps, core_ids=list(range(B)))
        cur = [np.asarray(res.results[b]["outT"]) for b in range(B)]
    out = np.stack([np.ascontiguousarray(cur[b].T) for b in range(B)], axis=0)
    return out.astype(np.float32, copy=False)
```

```python
import contextlib
import numpy as np
import concourse.bass as bass
import concourse.mybir as mybir
from concourse.bass_utils import run_bass_kernel_spmd

F32 = mybir.dt.float32
F32R = mybir.dt.float32r
BF16 = mybir.dt.bfloat16
AF = mybir.ActivationFunctionType
ALU = mybir.AluOpType

D = 1024
T = 4096
TT = 512
DFF = 2816
NCT = 22
NSLOT = 7
ENGS = ("pe", "act", "dve", "pool", "sp")


class Buf:
    __slots__ = ("name", "last_w", "readers")

    def __init__(self, name=""):
        self.name = name
        self.last_w = None
        self.readers = {}


class Rec:
    def __init__(self, same_engine_sync=("act", "dve", "pool")):
        self.ops = {e: [] for e in ENGS}
        self.cnt = {e: 0 for e in ENGS}
        self.waited = {}
        self.same_sync = set(same_engine_sync)
        self.dma_sems = []
        self.limit = None

    def new_dma_sem(self, name):
        self.cnt[name] = 0
        self.dma_sems.append(name)
        return name

    def op(self, eng, issue, reads=(), writes=(), signal=True, dma_sem=None, force=False):
        self.total = getattr(self, "total", 0) + 1
        if self.limit is not None and self.total > self.limit and not force:
            return None
        need = {}

        def add(tok):
            if tok is None:
                return
            k, v = tok
            if need.get(k, 0) < v:
                need[k] = v

        for b in reads:
            add(b.last_w)
        for b in writes:
            add(b.last_w)
            for k, v in b.readers.items():
                add((k, v))
        waits = []
        for k, v in need.items():
            if k == eng and eng not in self.same_sync:
                continue
            if k == eng and v > self.cnt[eng]:
                continue
            if self.waited.get((eng, k), 0) >= v:
                continue
            self.waited[(eng, k)] = v
            waits.append((k, v))
        if dma_sem is not None:
            self.cnt[dma_sem] += 16
            tok = (dma_sem, self.cnt[dma_sem])
            inc = (dma_sem, 16)
        elif signal:
            self.cnt[eng] += 1
            tok = (eng, self.cnt[eng])
            inc = (eng, 1)
        else:
            tok = (eng, self.cnt[eng] + 1)
            inc = None
        for b in reads:
            if b.readers.get(tok[0], 0) < tok[1]:
                b.readers[tok[0]] = tok[1]
        for b in writes:
            b.last_w = tok
            b.readers = {}
        self.ops[eng].append((waits, issue, inc))
        return tok

    def final_wait(self, eng, toks):
        self.ops[eng].append(([(k, v) for (k, v) in toks], None, None))

    def emit(self, nc):
        names = [e for e in ENGS if e != "sp"] + self.dma_sems
        with contextlib.ExitStack() as st:
            sems = {n: st.enter_context(nc.semaphore("s_" + n)) for n in names}
            block = st.enter_context(nc.Block())

            def replay(name, e):
                for waits, issue, inc in self.ops[name]:
                    for k, v in waits:
                        e.wait_ge(sems[k], v)
                    if issue is None:
                        continue
                    ins = issue(e)
                    if inc is not None:
                        ins.then_inc(sems[inc[0]], inc[1])

            @block.tensor
            def _(e):
                replay("pe", e)

            @block.scalar
            def _(e):
                replay("act", e)

            @block.vector
            def _(e):
                replay("dve", e)

            @block.gpsimd
            def _(e):
                replay("pool", e)

            @block.sync
            def _(e):
                replay("sp", e)


class V:
    __slots__ = ("ap", "bufs")

    def __init__(self, ap, bufs):
        self.ap = ap
        self.bufs = list(bufs)

    def __getitem__(self, k):
        return V(self.ap[k], self.bufs)

    def bitcast(self, dt):
        return V(self.ap.bitcast(dt), self.bufs)

    def rearrange(self, s, **kw):
        return V(self.ap.rearrange(s, **kw), self.bufs)

    def bcast3(self, n):
        return V(self.ap.unsqueeze(2).broadcast_to(list(self.ap.shape) + [n]), self.bufs)


def plan(layers):
    ch = []
    for l in layers:
        if l == 0:
            for hp in range(4):
                for k in ("q", "f", "i", "g"):
                    ch.append(("hg_in", k, hp))
            for hp in range(4):
                ch.append(("hg_out", hp))
        else:
            ch.append(("gla_lr",))
            for hd in range(4):
                ch.append(("gla_qk", hd))
                ch.append(("gla_v", hd))
                ch.append(("gla_g", hd))
            for hd in range(4):
                ch.append(("gla_out", hd))
        for g in range(2):
            cs = list(range(g * 11, (g + 1) * 11))
            for c in cs:
                ch.append(("up", l, c))
            for half in range(2):
                for part in (cs[0:4], cs[4:8], cs[8:11]):
                    ch.append(("down", l, half, tuple(part)))
    return ch


def _cols(W, cols):
    sub = W[:, cols].reshape(8, 128, len(cols)).transpose(1, 0, 2)
    return sub.reshape(128, -1)


def pack_wstream(inp, layers):
    chunks = plan(layers)
    out = np.zeros((len(chunks), 128, 2048), np.float32)
    hg_in = inp["hg_w_in"][0]
    hg_out = inp["hg_w_out"][0]
    gl_in = inp["gla_w_in"][0]
    gl_out = inp["gla_w_out"][0]
    for i, c in enumerate(chunks):
        kind = c[0]
        if kind == "hg_in":
            base = {"q": 0, "f": 1024, "i": 2048, "g": 3072}[c[1]] + c[2] * 256
            out[i] = _cols(hg_in, np.arange(base, base + 256))
        elif kind == "hg_out":
            r0 = c[1] * 256
            out[i] = hg_out[r0:r0 + 256].reshape(2, 128, 1024).transpose(1, 0, 2).reshape(128, 2048)
        elif kind == "gla_lr":
            out[i, :, :1024] = _cols(gl_in, np.arange(2960, 3088))
        elif kind == "gla_qk":
            hd = c[1]
            cols = np.concatenate([np.arange(hd * 128, hd * 128 + 128), np.arange(512 + hd * 128, 512 + hd * 128 + 128)])
            out[i] = _cols(gl_in, cols)
        elif kind == "gla_v":
            out[i] = _cols(gl_in, np.arange(1024 + c[1] * 256, 1024 + c[1] * 256 + 256))
        elif kind == "gla_g":
            out[i] = _cols(gl_in, np.arange(2048 + c[1] * 256, 2048 + c[1] * 256 + 256))
        elif kind == "gla_out":
            r0 = c[1] * 256
            out[i] = gl_out[r0:r0 + 256].reshape(2, 128, 1024).transpose(1, 0, 2).reshape(128, 2048)
        elif kind == "up":
            l, ct = c[1], c[2]
            cols = np.concatenate([np.arange(ct * 128, ct * 128 + 128), np.arange(DFF + ct * 128, DFF + ct * 128 + 128)])
            out[i] = _cols(inp["ffn_w_up"][l], cols)
        elif kind == "down":
            l, half, part = c[1], c[2], c[3]
            wd = inp["ffn_w_down"][l]
            for j, ct in enumerate(part):
                out[i, :, j * 512:(j + 1) * 512] = wd[ct * 128:(ct + 1) * 128, half * 512:(half + 1) * 512]
    return out


_CO = {}
_off = 0
for _n, _w in (("nm", 16), ("nf", 16), ("nfin", 8), ("lbl", 24), ("hgnw", 1), ("bgk", 4), ("glanw", 2),
               ("cw", 264), ("cb", 88), ("eps", 1), ("one", 1), ("lnc", 1), ("ident", 128), ("cmask", 128),
               ("smask", 512), ("w2", 512)):
    _CO[_n] = _off
    _off += _w
NCONST = _off


def pack_consts(inp):
    c = np.zeros((128, NCONST), np.float32)

    def put(name, arr):
        arr = np.asarray(arr, np.float32)
        c[:, _CO[name]:_CO[name] + arr.shape[1]] = arr

    pd = lambda v: np.asarray(v, np.float32).reshape(-1, 128).T
    put("nm", np.concatenate([pd(inp["norm_mixer_w"][l]) for l in range(2)], axis=1))
    put("nf", np.concatenate([pd(inp["norm_ffn_w"][l]) for l in range(2)], axis=1))
    put("nfin", pd(inp["norm_final_w"]))
    put("lbl", np.concatenate([pd(inp["lb_logits"][k]) for k in range(3)], axis=1))
    put("hgnw", pd(inp["hg_norm_w"][0]))
    put("bgk", pd(inp["gla_b_gk_up"][0]))
    put("glanw", pd(inp["gla_norm_w"][0]))
    cw = np.asarray(inp["ffn_conv_w"], np.float32)
    put("cw", np.concatenate([pd(cw[l, j]) for l in range(2) for j in range(3)], axis=1))
    cb = np.asarray(inp["ffn_conv_b"], np.float32)
    put("cb", np.concatenate([pd(cb[l]) for l in range(2)], axis=1))
    put("eps", np.full((128, 1), 1e-6, np.float32))
    put("one", np.ones((128, 1), np.float32))
    put("lnc", np.full((128, 1), np.log(128.0 ** -0.5), np.float32))
    put("ident", np.eye(128, dtype=np.float32))
    s = np.arange(128)[:, None]
    t = np.arange(128)[None, :]
    put("cmask", ((s // 64 == t // 64) & (s <= t)).astype(np.float32))
    sm = np.ones((128, 512), np.float32)
    sm[:, ::64] = 0.0
    put("smask", sm)
    w2 = np.zeros((128, 512), np.float32)
    w2[112:128, :] = np.asarray(inp["gla_w_gk_up"][0], np.float32)
    put("w2", w2)
    return c


def build(NT, layers=(0, 1), final_norm=True, same_sync=("act", "dve", "pool")):
    nc = bass.Bass("TRN2", target_bir_lowering=False)
    nc.dge_precook = False
    chunks = plan(layers)
    NCH = len(chunks)
    xT_d = nc.dram_tensor("xT", [D, NT * TT], F32, kind="ExternalInput").ap()
    ws_d = nc.dram_tensor("wstream", [NCH, 128, 2048], F32R, kind="ExternalInput").ap()
    cs_d = nc.dram_tensor("consts", [128, NCONST], F32, kind="ExternalInput").ap()
    out_d = nc.dram_tensor("outT", [D, NT * TT], F32, kind="ExternalOutput").ap()

    R = Rec(same_engine_sync=same_sync)
    import os as _os
    if _os.environ.get("KLIMIT"):
        R.limit = int(_os.environ["KLIMIT"])
    with contextlib.ExitStack() as st:
        def sbt(name, shape, dt=F32, nbuf=1):
            t = st.enter_context(nc.sbuf_tensor(name, shape, dt))
            return t, [Buf(f"{name}{i}") for i in range(nbuf)]

        def tile(name, shape, dt=F32):
            t, b = sbt(name, shape, dt)
            return V(t[:], b)

        hT_t, hT_b = sbt("hT", [128, 8, TT], F32, 8)
        yT_t, yT_b = sbt("yT", [128, 8, TT], F32R, 8)
        oT_t, oT_b = sbt("oT", [128, 8, TT], F32R, 8)
        aT_t, aT_b = sbt("aT", [128, 11, TT], F32R, 11)
        hT = [V(hT_t[:, i, :], [hT_b[i]]) for i in range(8)]
        yT = [V(yT_t[:, i, :], [yT_b[i]]) for i in range(8)]
        oT = [V(oT_t[:, i, :], [oT_b[i]]) for i in range(8)]
        aT = [V(aT_t[:, i, :], [aT_b[i]]) for i in range(11)]
        hT_all = V(hT_t[:], hT_b)
        CS = tile("consts_sb", [128, NCONST])
        slots = [tile(f"slot{i}", [128, 2048], F32R) for i in range(NSLOT)]
        slot_sems = [R.new_dma_sem(f"ws{i}") for i in range(NSLOT)]

        def cst(name, j=0, w=1):
            o = _CO[name] + j
            return CS[:, o:o + w]

        ones1024 = tile("ones1024", [128, 128], F32R)
        ones128 = tile("ones128", [128, 128], F32R)
        ones256 = tile("ones256", [128, 128], F32R)
        ident_bf = tile("ident_bf", [128, 128], BF16)
        lb = tile("lb", [128, 8])
        clb = tile("clb", [128, 8])
        lbe = tile("lbe", [128, 24])
        nbgk = tile("nbgk", [128, 4])
        W2r = tile("W2r", [128, 512], F32R)
        G_sb = tile("G_sb", [128, TT], F32R)
        lnv = tile("lnv", [128, TT])
        rstd = tile("rstd", [128, TT])
        S32hg = [tile(f"S32hg{h}", [128, 128]) for h in range(8)]
        Sbfhg = [tile(f"Sbfhg{h}", [128, 128], BF16) for h in range(8)]
        S32gl = [tile(f"S32gl{h}", [128, 256]) for h in range(4)]
        Sbfgl = [tile(f"Sbfgl{h}", [128, 256], BF16) for h in range(4)]
        tails_t = st.enter_context(nc.sbuf_tensor("tails", [128, 2, 44, 2], F32))
        tails = [[V(tails_t[:, l, ct, :], [Buf(f"tail{l}_{ct}")]) for ct in range(44)] for l in range(2)]
        TM = []
        for s in range(2):
            d = {}
            for n in ("C", "D", "SG1", "E"):
                d[n] = tile(f"t{n}{s}", [128, TT])
            d["XA"] = tile(f"tXA{s}", [128, TT + 2])
            d["XG"] = tile(f"tXG{s}", [128, TT + 2])
            d["A"] = d["XA"][:, 0:TT]
            d["B"] = d["XG"][:, 0:TT]
            d["UA"] = d["C"]
            d["UG"] = d["D"]
            d["OSQ0"] = tile(f"tOSQ0{s}", [128, TT], F32R)
            d["OSQ1"] = tile(f"tOSQ1{s}", [128, TT], F32R)
            for n in ("kT", "kkT", "qT"):
                d[n] = tile(f"t{n}{s}", [128, TT], BF16)
            d["kktok"] = tile(f"tkktok{s}", [128, 4, 128], BF16)
            d["vtok"] = tile(f"tvtok{s}", [128, 4, 256], BF16)
            d["scT"] = [tile(f"tscT{s}{i}", [128, 128], BF16) for i in range(2)]
            d["ebl"] = tile(f"tebl{s}", [128, 8])
            TM.append(d)
        banks = []
        for i in range(8):
            t = st.enter_context(nc.psum_tensor(f"bank{i}", [128, TT], F32))
            banks.append((t, [Buf(f"bk{i}a"), Buf(f"bk{i}b")]))
        pstate = {"big": 0, "sc": 0, "tr": 0, "nbig": 6}

        def pbank():
            n = pstate["nbig"]
            i = pstate["big"] % n
            pstate["big"] += 1
            t, b = banks[i]
            return V(t[:], b)

        def psc_region():
            h = pstate["sc"] % 2
            pstate["sc"] += 1
            t, b = banks[6]
            return V(t[:, h * 128:(h + 1) * 128], [b[0]])

        def pU_region(nv):
            t, b = banks[6]
            return V(t[:, 256:256 + nv], [b[1]])

        def ptrans():
            h = pstate["tr"] % 2
            pstate["tr"] += 1
            t, b = banks[7]
            return V(t[:, h * 256:(h + 1) * 256].bitcast(BF16), [b[h]])

        def bufs_of(*vs):
            r = []
            for v in vs:
                if isinstance(v, V):
                    r += v.bufs
            return r

        def apof(v):
            return v.ap if isinstance(v, V) else v

        def mm(out, lhsT, rhs, start=True, stop=True, signal=None):
            sig = stop if signal is None else signal
            R.op("pe", lambda e: e.matmul(out.ap, lhsT.ap, rhs.ap, start=start, stop=stop),
                 reads=lhsT.bufs + rhs.bufs, writes=out.bufs, signal=sig)

        def tr(out, in_):
            R.op("pe", lambda e: e.transpose(out.ap, in_.ap, ident_bf.ap),
                 reads=in_.bufs + ident_bf.bufs, writes=out.bufs, signal=True)

        def ACT(out, in_, func, bias=None, scale=None):
            kw = {}
            if bias is not None:
                kw["bias"] = apof(bias)
            if scale is not None:
                kw["scale"] = apof(scale)
            R.op("act", lambda e: e.activation(out=out.ap, in_=in_.ap, func=func, **kw),
                 reads=bufs_of(in_, bias, scale), writes=out.bufs)

        def TTo(eng, out, in0, in1, op):
            R.op(eng, lambda e: e.tensor_tensor(out=out.ap, in0=in0.ap, in1=in1.ap, op=op),
                 reads=bufs_of(in0, in1), writes=out.bufs)

        def TS(eng, out, in0, s1, op0, s2=None, op1=None):
            kw = {}
            if op1 is None and eng == "pool":
                if op0 == ALU.add:
                    s2, op1 = 1.0, ALU.mult
                elif op0 == ALU.mult:
                    s2, op1 = 0.0, ALU.add
            if op1 is not None:
                kw["op1"] = op1
            R.op(eng, lambda e: e.tensor_scalar(out=out.ap, in0=in0.ap, scalar1=apof(s1), scalar2=apof(s2), op0=op0, **kw),
                 reads=bufs_of(in0, s1, s2), writes=out.bufs)

        def STT(eng, out, in0, scalar, in1, op0, op1):
            R.op(eng, lambda e: e.scalar_tensor_tensor(out=out.ap, in0=in0.ap, scalar=apof(scalar), in1=in1.ap, op0=op0, op1=op1),
                 reads=bufs_of(in0, scalar, in1), writes=out.bufs)

        def SCAN(out, d0, d1):
            R.op("dve", lambda e: e.tensor_tensor_scan(out=out.ap, data0=d0.ap, data1=d1.ap, initial=0.0, op0=ALU.mult, op1=ALU.add),
                 reads=bufs_of(d0, d1), writes=out.bufs)

        def RECIP(eng, out, in_):
            R.op(eng, lambda e: e.reciprocal(out=out.ap, in_=in_.ap), reads=in_.bufs, writes=out.bufs)

        def CP(eng, out, in_):
            if eng == "act":
                R.op("act", lambda e: e.copy(out=out.ap, in_=in_.ap), reads=in_.bufs, writes=out.bufs)
            else:
                R.op(eng, lambda e: e.tensor_copy(out=out.ap, in_=in_.ap), reads=in_.bufs, writes=out.bufs)

        def MEMSET(eng, out, val):
            R.op(eng, lambda e: e.memset(out.ap, val), writes=out.bufs)

        wst = {"next_dma": 0, "cur": 0}
        total_chunks = NCH * NT

        def ws_get(kind, lag=1):
            j = wst["cur"]
            wst["cur"] += 1
            assert chunks[j % NCH][0] == kind, (chunks[j % NCH], kind)
            upto = min(total_chunks - 1, j + NSLOT - lag)
            while wst["next_dma"] <= upto:
                m = wst["next_dma"]
                wst["next_dma"] += 1
                k = m % NCH
                sl = slots[m % NSLOT]
                if chunks[k][0] == "gla_lr":
                    R.op("sp", lambda e, sl=sl, k=k: e.dma_start(out=sl.ap[:, 0:1024], in_=ws_d[k, :, 0:1024]),
                         writes=sl.bufs, dma_sem=slot_sems[m % NSLOT])
                else:
                    R.op("sp", lambda e, sl=sl, k=k: e.dma_start(out=sl.ap, in_=ws_d[k]),
                         writes=sl.bufs, dma_sem=slot_sems[m % NSLOT])
            return slots[j % NSLOT]

        s_c = R.new_dma_sem("ld_c")
        s_h = R.new_dma_sem("ld_h")
        s_o = R.new_dma_sem("st_o")
        R.op("sp", lambda e: e.dma_start(out=CS.ap, in_=cs_d), writes=CS.bufs, dma_sem=s_c)
        onesf = tile("onesf", [128, 128])
        for ot, val in ((ones1024, 1.0 / 1024), (ones128, 1.0 / 128), (ones256, 1.0 / 256)):
            MEMSET("pool", onesf, val)
            CP("act", ot, onesf)
        for h in range(8):
            MEMSET("pool", S32hg[h], 0.0)
            MEMSET("pool", Sbfhg[h], 0.0)
        for h in range(4):
            MEMSET("pool", S32gl[h], 0.0)
            MEMSET("pool", Sbfgl[h], 0.0)
        tails_all = V(tails_t[:], [tails[l][ct].bufs[0] for l in range(2) for ct in range(44)])
        MEMSET("pool", tails_all, 0.0)
        CP("act", ident_bf, cst("ident", 0, 128))
        CP("act", W2r, cst("w2", 0, 512))
        ACT(lbe, cst("lbl", 0, 24), AF.Exp)
        TTo("pool", lb, lbe[:, 0:8], lbe[:, 8:16], ALU.add)
        TTo("pool", lb, lb, lbe[:, 16:24], ALU.add)
        RECIP("dve", lb, lb)
        TTo("dve", lb, lb, lbe[:, 0:8], ALU.mult)
        ACT(clb, lb, AF.Ln, scale=-1.0, bias=cst("one"))
        TS("pool", nbgk, cst("bgk", 0, 4), -1.0, ALU.mult)
        one = cst("one")
        eps = cst("eps")
        cmask = cst("cmask", 0, 128)
        smask = cst("smask", 0, 512)

        def rmsnorm(wname, woff, dst):
            for dt in range(8):
                ACT(yT[dt], hT[dt], AF.Square)
            pb = pbank()
            for dt in range(8):
                mm(pb, ones1024, yT[dt], start=(dt == 0), stop=(dt == 7))
            ACT(lnv, pb, AF.Ln, bias=eps)
            ACT(rstd, lnv, AF.Exp, scale=-0.5)
            for dt in range(8):
                if dt % 2 == 0:
                    STT("dve", dst[dt], hT[dt], cst(wname, woff + dt), rstd, ALU.mult, ALU.mult)
                else:
                    ACT(dst[dt], hT[dt], AF.Identity, scale=cst(wname, woff + dt))
                    TTo("pool", dst[dt], dst[dt], rstd, ALU.mult)

        def out_proj(kind):
            so = [ws_get(kind, lag=i + 1) for i in range(4)]
            for ft in range(8):
                pw = pbank()
                for h in range(8):
                    w = so[h // 2].rearrange("p (a f) -> p a f", a=2)[:, h % 2, ft * 128:(ft + 1) * 128]
                    mm(pw, w, oT[h], start=(h == 0), stop=(h == 7))
                TTo("dve", hT[ft], hT[ft], pw, ALU.add)

        def recurrence(tm, a_cols, n_vt, Sbf, S32, vcol0):
            kT, kkT, qT, kktok, vtok, ebl = tm["kT"], tm["kkT"], tm["qT"], tm["kktok"], tm["vtok"], tm["ebl"]
            po = [pbank() for _ in range(n_vt)]
            nv = 128 * n_vt
            for blk in range(4):
                cols = slice(blk * 128, (blk + 1) * 128)
                psc = psc_region()
                mm(psc, kT[:, cols], qT[:, cols])
                scT = tm["scT"][blk % 2]
                TTo("dve", scT, psc, cmask, ALU.mult)
                for vt in range(n_vt):
                    mm(po[vt][:, cols], vtok[:, blk, vcol0 + vt * 128:vcol0 + (vt + 1) * 128], scT,
                       start=True, stop=False, signal=False)
                for ci in range(2):
                    c = blk * 2 + ci
                    ccols = slice(c * 64, (c + 1) * 64)
                    rows = slice(ci * 64, (ci + 1) * 64)
                    for vt in range(n_vt):
                        mm(po[vt][:, ccols], Sbf[:, vt * 128:(vt + 1) * 128], qT[:, ccols],
                           start=False, stop=True, signal=(vt == n_vt - 1))
                    pU = pU_region(nv)
                    mm(pU, kktok[rows, blk, :], vtok[rows, blk, vcol0:vcol0 + nv])
                    STT("dve", S32, S32, ebl[:, c:c + 1], pU, ALU.mult, ALU.add)
                    CP("act", Sbf, S32)
            return po

        def kk_transposes(tm):
            ptr = ptrans()
            for blk in range(4):
                tr(ptr[:, blk * 128:(blk + 1) * 128], tm["kkT"][:, blk * 128:(blk + 1) * 128])
            CP("act", tm["kktok"].rearrange("p a b -> p (a b)"), ptr)

        def silu_gate(dst, pg):
            ACT(dst, pg, AF.Exp, scale=-1.0)
            TS("pool", dst, dst, 1.0, ALU.add)
            RECIP("dve", dst, dst)
            TTo("dve", dst, pg, dst, ALU.mult)

        def mixer_hg():
            rmsnorm("nm", 0, yT)
            for hp in range(4):
                sq = ws_get("hg_in", 1).rearrange("p (dt f) -> p dt f", dt=8)
                sf = ws_get("hg_in", 2).rearrange("p (dt f) -> p dt f", dt=8)
                si = ws_get("hg_in", 3).rearrange("p (dt f) -> p dt f", dt=8)
                sg = ws_get("hg_in", 4).rearrange("p (dt f) -> p dt f", dt=8)
                tmv = TM[hp % 2]
                for blk in range(4):
                    pv = pbank()[:, 0:256]
                    for dt in range(8):
                        mm(pv, yT[dt][:, blk * 128:(blk + 1) * 128], si[:, dt, :], start=(dt == 0), stop=(dt == 7))
                    CP("act", tmv["vtok"][:, blk, :], pv)
                for a in range(2):
                    h = 2 * hp + a
                    tm = dict(TM[h % 2])
                    tm["vtok"] = tmv["vtok"]
                    A_, B_, C_, D_ = tm["A"], tm["B"], tm["C"], tm["D"]
                    fc = slice(a * 128, (a + 1) * 128)
                    pf = pbank()
                    for dt in range(8):
                        mm(pf, sf[:, dt, fc], yT[dt], start=(dt == 0), stop=(dt == 7))
                    pq = pbank()
                    for dt in range(8):
                        mm(pq, sq[:, dt, fc], yT[dt], start=(dt == 0), stop=(dt == 7))
                    pg = pbank()
                    for dt in range(8):
                        mm(pg, sg[:, dt, fc], yT[dt], start=(dt == 0), stop=(dt == 7))
                    ACT(A_, pf, AF.Exp, scale=-1.0)
                    ACT(B_, A_, AF.Ln, bias=one)
                    ACT(C_, A_, AF.Ln, scale=lb[:, h:h + 1], bias=one)
                    TTo("pool", C_, C_, B_, ALU.subtract)
                    SCAN(D_, smask, C_)
                    TTo("dve", A_, pf, B_, ALU.add)
                    TTo("pool", A_, A_, D_, ALU.add)
                    ACT(tm["kT"], A_, AF.Exp, scale=-1.0, bias=clb[:, h:h + 1])
                    ACT(tm["ebl"], D_[:, 63::64], AF.Exp)
                    TTo("pool", tm["kkT"].rearrange("p (c k) -> p c k", k=64),
                        tm["kT"].rearrange("p (c k) -> p c k", k=64), tm["ebl"].bcast3(64), ALU.mult)
                    kk_transposes(tm)
                    ACT(C_, pq, AF.Exp, scale=-1.0)
                    TS("pool", C_, C_, 1.0, ALU.add)
                    RECIP("dve", C_, C_)
                    ACT(B_, D_, AF.Exp, bias=cst("lnc"))
                    TTo("dve", C_, pq, C_, ALU.mult)
                    TTo("pool", tm["qT"], C_, B_, ALU.mult)
                    silu_gate(A_, pg)
                    po = recurrence(tm, None, 1, Sbfhg[h], S32hg[h], a * 128)[0]
                    ACT(tm["OSQ0"], po, AF.Square)
                    pss = pbank()
                    mm(pss, ones128, tm["OSQ0"])
                    ACT(C_, pss, AF.Ln, bias=eps)
                    ACT(C_, C_, AF.Exp, scale=-0.5)
                    STT("dve", D_, po, cst("hgnw"), C_, ALU.mult, ALU.mult)
                    TTo("pool", oT[h], D_, A_, ALU.mult)
            out_proj("hg_out")

        def mixer_gla():
            rmsnorm("nm", 8, yT)
            slr = ws_get("gla_lr", 1)[:, 0:1024].rearrange("p (dt f) -> p dt f", dt=8)
            pG = pbank()
            for dt in range(8):
                mm(pG, slr[:, dt, :], yT[dt], start=(dt == 0), stop=(dt == 7))
            CP("act", G_sb, pG)
            for hd in range(4):
                sqk = ws_get("gla_qk", 1).rearrange("p (dt two f) -> p dt two f", dt=8, two=2)
                sv = ws_get("gla_v", 2).rearrange("p (dt f) -> p dt f", dt=8)
                sg = ws_get("gla_g", 3).rearrange("p (dt f) -> p dt f", dt=8)
                tm = TM[hd % 2]
                A_, B_, C_, D_ = tm["A"], tm["B"], tm["C"], tm["D"]
                SG = [tm["E"], tm["SG1"]]
                for vt in range(2):
                    pg = pbank()
                    for dt in range(8):
                        mm(pg, sg[:, dt, vt * 128:(vt + 1) * 128], yT[dt], start=(dt == 0), stop=(dt == 7))
                    silu_gate(SG[vt], pg)
                for blk in range(4):
                    pv = pbank()[:, 0:256]
                    for dt in range(8):
                        mm(pv, yT[dt][:, blk * 128:(blk + 1) * 128], sv[:, dt, :], start=(dt == 0), stop=(dt == 7))
                    CP("act", tm["vtok"][:, blk, :], pv)
                pgk = pbank()
                mm(pgk, W2r[:, hd * 128:(hd + 1) * 128], G_sb)
                ACT(A_, pgk, AF.Exp, scale=-1.0, bias=nbgk[:, hd:hd + 1])
                ACT(A_, A_, AF.Ln, bias=one)
                SCAN(D_, smask, A_)
                ACT(B_, D_, AF.Exp, scale=-1.0 / 16)
                ACT(C_, D_, AF.Exp, scale=1.0 / 16)
                ACT(tm["ebl"], D_[:, 63::64], AF.Exp, scale=-1.0 / 16)
                pq = pbank()
                for dt in range(8):
                    mm(pq, sqk[:, dt, 0, :], yT[dt], start=(dt == 0), stop=(dt == 7))
                pk = pbank()
                for dt in range(8):
                    mm(pk, sqk[:, dt, 1, :], yT[dt], start=(dt == 0), stop=(dt == 7))
                STT("dve", tm["qT"], pq, 128 ** -0.5, B_, ALU.mult, ALU.mult)
                TTo("dve", tm["kT"], pk, C_, ALU.mult)
                TTo("pool", tm["kkT"].rearrange("p (c k) -> p c k", k=64),
                    tm["kT"].rearrange("p (c k) -> p c k", k=64), tm["ebl"].bcast3(64), ALU.mult)
                kk_transposes(tm)
                po = recurrence(tm, None, 2, Sbfgl[hd], S32gl[hd], 0)
                OSQ = [tm["OSQ0"], tm["OSQ1"]]
                for vt in range(2):
                    ACT(OSQ[vt], po[vt], AF.Square)
                pss = pbank()
                mm(pss, ones256, OSQ[0], start=True, stop=False)
                mm(pss, ones256, OSQ[1], start=False, stop=True)
                ACT(C_, pss, AF.Ln, bias=eps)
                ACT(C_, C_, AF.Exp, scale=-0.5)
                for vt in range(2):
                    STT("dve", D_, po[vt], cst("glanw", vt), C_, ALU.mult, ALU.mult)
                    TTo("pool", oT[hd * 2 + vt], D_, SG[vt], ALU.mult)
            out_proj("gla_out")

        def ffn(l):
            rmsnorm("nf", 8 * l, yT)
            pstate["nbig"] = 8
            for g in range(2):
                for ci in range(11):
                    c = g * 11 + ci
                    su = ws_get("up", 1).rearrange("p (dt two f) -> p dt two f", dt=8, two=2)
                    tm = TM[ci % 2]
                    pa = pbank()
                    for dt in range(8):
                        mm(pa, su[:, dt, 0, :], yT[dt], start=(dt == 0), stop=(dt == 7))
                    pg = pbank()
                    for dt in range(8):
                        mm(pg, su[:, dt, 1, :], yT[dt], start=(dt == 0), stop=(dt == 7))
                    for (ps, ct, xs, u, e1, e2) in ((pa, c, tm["XA"], tm["UA"], "dve", "dve"),
                                                     (pg, NCT + c, tm["XG"], tm["UG"], "dve", "dve")):
                        cw = lambda j, ct=ct: cst("cw", (l * 3 + j) * 44 + ct)
                        CP("act", xs[:, 2:TT + 2], ps)
                        CP("pool", xs[:, 0:2], tails[l][ct])
                        ACT(u, xs[:, 0:TT], AF.Identity, scale=cw(0), bias=cst("cb", l * 44 + ct))
                        STT(e1, u, xs[:, 1:TT + 1], cw(1), u, ALU.mult, ALU.add)
                        STT(e2, u, xs[:, 2:TT + 2], cw(2), u, ALU.mult, ALU.add)
                        CP("pool", tails[l][ct], xs[:, TT:TT + 2])
                    E = tm["E"]
                    ACT(E, tm["UG"], AF.Exp, scale=-1.0)
                    TS("pool", E, E, 1.0, ALU.add)
                    RECIP("dve", E, E)
                    TTo("pool", tm["UG"], tm["UG"], E, ALU.mult)
                    TTo("pool", aT[ci], tm["UG"], tm["UA"], ALU.mult)
                for half in range(2):
                    accs = [pbank() for _ in range(4)]
                    for part in (range(0, 4), range(4, 8), range(8, 11)):
                        sd = ws_get("down", 1).rearrange("p (i f) -> p i f", i=4)
                        for i, ci in enumerate(part):
                            for j in range(4):
                                mm(accs[j], sd[:, i, j * 128:(j + 1) * 128], aT[ci], start=(ci == 0), stop=(ci == 10))
                    for j in range(4):
                        ft = half * 4 + j
                        TTo("dve", hT[ft], hT[ft], accs[j], ALU.add)
            pstate["nbig"] = 6
            pstate["big"] = 0

        out_toks = []
        for tI in range(NT):
            t0 = tI * TT
            R.op("sp", lambda e, t0=t0: e.dma_start(out=hT_t[:], in_=xT_d[:, t0:t0 + TT].rearrange("(dt p) t -> p dt t", p=128)),
                 writes=hT_b, dma_sem=s_h)
            for l in layers:
                if l == 0:
                    mixer_hg()
                else:
                    mixer_gla()
                ffn(l)
            if final_norm:
                rmsnorm("nfin", 0, hT)
            tok = R.op("sp", lambda e, t0=t0: e.dma_start(out=out_d[:, t0:t0 + TT].rearrange("(dt p) t -> p dt t", p=128), in_=hT_t[:]),
                       reads=hT_b, dma_sem=s_o, force=True)
            out_toks.append(tok)
        R.final_wait("sp", [out_toks[-1]])
        assert wst["cur"] == total_chunks
        print("total ops", R.total, {k: v for k, v in R.cnt.items()})
        R.emit(nc)
    return nc


_PROGS = {}


def _prog(NT, layers, final_norm):
    key = (NT, tuple(layers), final_norm)
    if key not in _PROGS:
        _PROGS[key] = build(NT, layers, final_norm)
    return _PROGS[key]


FUSED = True


def kernel(**inputs):
    inp = {k: np.asarray(v) for k, v in inputs.items()}
    x = inp["x"].astype(np.float32, copy=False)
    B = x.shape[0]
    consts = pack_consts(inp)
    xT = [np.ascontiguousarray(x[b].T) for b in range(B)]
    if FUSED:
        stages = [((0, 1), True)]
    else:
        stages = [((0,), False), ((1,), True)]
    cur = xT
    for layers, fin in stages:
        ws = pack_wstream(inp, layers)
        nc = _prog(T // TT, layers, fin)
        in_maps = [{"xT": cur[b], "wstream": ws, "consts": consts} for b in range(B)]
        res = run_bass_kernel_spmd(nc, in_maps, core_ids=list(range(B)))
        cur = [np.asarray(res.results[b]["outT"]) for b in range(B)]
    out = np.stack([np.ascontiguousarray(cur[b].T) for b in range(B)], axis=0)
    return out.astype(np.float32, copy=False)
```

```python
import contextlib
import numpy as np
import concourse.bass as bass
import concourse.mybir as mybir
from concourse.bass_utils import run_bass_kernel_spmd

F32 = mybir.dt.float32
F32R = mybir.dt.float32r
BF16 = mybir.dt.bfloat16
AF = mybir.ActivationFunctionType
ALU = mybir.AluOpType

D = 1024
T = 4096
TT = 512
DFF = 2816
NCT = 22
NSLOT = 7
ENGS = ("pe", "act", "dve", "pool", "sp")


class Buf:
    __slots__ = ("name", "last_w", "readers")

    def __init__(self, name=""):
        self.name = name
        self.last_w = None
        self.readers = {}


class Rec:
    def __init__(self, same_engine_sync=("act", "dve", "pool")):
        self.ops = {e: [] for e in ENGS}
        self.cnt = {e: 0 for e in ENGS}
        self.waited = {}
        self.same_sync = set(same_engine_sync)
        self.dma_sems = []
        self.limit = None

    def new_dma_sem(self, name):
        self.cnt[name] = 0
        self.dma_sems.append(name)
        return name

    def op(self, eng, issue, reads=(), writes=(), signal=True, dma_sem=None, force=False):
        self.total = getattr(self, "total", 0) + 1
        if self.limit is not None and self.total > self.limit and not force:
            return None
        need = {}

        def add(tok):
            if tok is None:
                return
            k, v = tok
            if need.get(k, 0) < v:
                need[k] = v

        for b in reads:
            add(b.last_w)
        for b in writes:
            add(b.last_w)
            for k, v in b.readers.items():
                add((k, v))
        waits = []
        for k, v in need.items():
            if k == eng and eng not in self.same_sync:
                continue
            if k == eng and v > self.cnt[eng]:
                continue
            if self.waited.get((eng, k), 0) >= v:
                continue
            self.waited[(eng, k)] = v
            waits.append((k, v))
        if dma_sem is not None:
            self.cnt[dma_sem] += 16
            tok = (dma_sem, self.cnt[dma_sem])
            inc = (dma_sem, 16)
        elif signal:
            self.cnt[eng] += 1
            tok = (eng, self.cnt[eng])
            inc = (eng, 1)
        else:
            tok = (eng, self.cnt[eng] + 1)
            inc = None
        for b in reads:
            if b.readers.get(tok[0], 0) < tok[1]:
                b.readers[tok[0]] = tok[1]
        for b in writes:
            b.last_w = tok
            b.readers = {}
        self.ops[eng].append((waits, issue, inc))
        return tok

    def final_wait(self, eng, toks):
        self.ops[eng].append(([(k, v) for (k, v) in toks], None, None))

    def emit(self, nc):
        names = [e for e in ENGS if e != "sp"] + self.dma_sems
        with contextlib.ExitStack() as st:
            sems = {n: st.enter_context(nc.semaphore("s_" + n)) for n in names}
            block = st.enter_context(nc.Block())

            def replay(name, e):
                for waits, issue, inc in self.ops[name]:
                    for k, v in waits:
                        e.wait_ge(sems[k], v)
                    if issue is None:
                        continue
                    ins = issue(e)
                    if inc is not None:
                        ins.then_inc(sems[inc[0]], inc[1])

            @block.tensor
            def _(e):
                replay("pe", e)

            @block.scalar
            def _(e):
                replay("act", e)

            @block.vector
            def _(e):
                replay("dve", e)

            @block.gpsimd
            def _(e):
                replay("pool", e)

            @block.sync
            def _(e):
                replay("sp", e)


class V:
    __slots__ = ("ap", "bufs")

    def __init__(self, ap, bufs):
        self.ap = ap
        self.bufs = list(bufs)

    def __getitem__(self, k):
        return V(self.ap[k], self.bufs)

    def bitcast(self, dt):
        return V(self.ap.bitcast(dt), self.bufs)

    def rearrange(self, s, **kw):
        return V(self.ap.rearrange(s, **kw), self.bufs)

    def bcast3(self, n):
        return V(self.ap.unsqueeze(2).broadcast_to(list(self.ap.shape) + [n]), self.bufs)


def plan(layers):
    ch = []
    for l in layers:
        if l == 0:
            for hp in range(4):
                for k in ("q", "f", "i", "g"):
                    ch.append(("hg_in", k, hp))
            for hp in range(4):
                ch.append(("hg_out", hp))
        else:
            ch.append(("gla_lr",))
            for hd in range(4):
                ch.append(("gla_qk", hd))
                ch.append(("gla_v", hd))
                ch.append(("gla_g", hd))
            for hd in range(4):
                ch.append(("gla_out", hd))
        for g in range(2):
            cs = list(range(g * 11, (g + 1) * 11))
            for c in cs:
                ch.append(("up", l, c))
            for half in range(2):
                for part in (cs[0:4], cs[4:8], cs[8:11]):
                    ch.append(("down", l, half, tuple(part)))
    return ch


def _cols(W, cols):
    sub = W[:, cols].reshape(8, 128, len(cols)).transpose(1, 0, 2)
    return sub.reshape(128, -1)


def pack_wstream(inp, layers):
    chunks = plan(layers)
    out = np.zeros((len(chunks), 128, 2048), np.float32)
    hg_in = inp["hg_w_in"][0]
    hg_out = inp["hg_w_out"][0]
    gl_in = inp["gla_w_in"][0]
    gl_out = inp["gla_w_out"][0]
    for i, c in enumerate(chunks):
        kind = c[0]
        if kind == "hg_in":
            base = {"q": 0, "f": 1024, "i": 2048, "g": 3072}[c[1]] + c[2] * 256
            out[i] = _cols(hg_in, np.arange(base, base + 256))
        elif kind == "hg_out":
            r0 = c[1] * 256
            out[i] = hg_out[r0:r0 + 256].reshape(2, 128, 1024).transpose(1, 0, 2).reshape(128, 2048)
        elif kind == "gla_lr":
            out[i, :, :1024] = _cols(gl_in, np.arange(2960, 3088))
        elif kind == "gla_qk":
            hd = c[1]
            cols = np.concatenate([np.arange(hd * 128, hd * 128 + 128), np.arange(512 + hd * 128, 512 + hd * 128 + 128)])
            out[i] = _cols(gl_in, cols)
        elif kind == "gla_v":
            out[i] = _cols(gl_in, np.arange(1024 + c[1] * 256, 1024 + c[1] * 256 + 256))
        elif kind == "gla_g":
            out[i] = _cols(gl_in, np.arange(2048 + c[1] * 256, 2048 + c[1] * 256 + 256))
        elif kind == "gla_out":
            r0 = c[1] * 256
            out[i] = gl_out[r0:r0 + 256].reshape(2, 128, 1024).transpose(1, 0, 2).reshape(128, 2048)
        elif kind == "up":
            l, ct = c[1], c[2]
            cols = np.concatenate([np.arange(ct * 128, ct * 128 + 128), np.arange(DFF + ct * 128, DFF + ct * 128 + 128)])
            out[i] = _cols(inp["ffn_w_up"][l], cols)
        elif kind == "down":
            l, half, part = c[1], c[2], c[3]
            wd = inp["ffn_w_down"][l]
            for j, ct in enumerate(part):
                out[i, :, j * 512:(j + 1) * 512] = wd[ct * 128:(ct + 1) * 128, half * 512:(half + 1) * 512]
    return out


_CO = {}
_off = 0
for _n, _w in (("nm", 16), ("nf", 16), ("nfin", 8), ("lbl", 24), ("hgnw", 1), ("bgk", 4), ("glanw", 2),
               ("cw", 264), ("cb", 88), ("eps", 1), ("one", 1), ("lnc", 1), ("ident", 128), ("cmask", 128),
               ("smask", 512), ("w2", 512)):
    _CO[_n] = _off
    _off += _w
NCONST = _off


def pack_consts(inp):
    c = np.zeros((128, NCONST), np.float32)

    def put(name, arr):
        arr = np.asarray(arr, np.float32)
        c[:, _CO[name]:_CO[name] + arr.shape[1]] = arr

    pd = lambda v: np.asarray(v, np.float32).reshape(-1, 128).T
    put("nm", np.concatenate([pd(inp["norm_mixer_w"][l]) for l in range(2)], axis=1))
    put("nf", np.concatenate([pd(inp["norm_ffn_w"][l]) for l in range(2)], axis=1))
    put("nfin", pd(inp["norm_final_w"]))
    put("lbl", np.concatenate([pd(inp["lb_logits"][k]) for k in range(3)], axis=1))
    put("hgnw", pd(inp["hg_norm_w"][0]))
    put("bgk", pd(inp["gla_b_gk_up"][0]))
    put("glanw", pd(inp["gla_norm_w"][0]))
    cw = np.asarray(inp["ffn_conv_w"], np.float32)
    put("cw", np.concatenate([pd(cw[l, j]) for l in range(2) for j in range(3)], axis=1))
    cb = np.asarray(inp["ffn_conv_b"], np.float32)
    put("cb", np.concatenate([pd(cb[l]) for l in range(2)], axis=1))
    put("eps", np.full((128, 1), 1e-6, np.float32))
    put("one", np.ones((128, 1), np.float32))
    put("lnc", np.full((128, 1), np.log(128.0 ** -0.5), np.float32))
    put("ident", np.eye(128, dtype=np.float32))
    s = np.arange(128)[:, None]
    t = np.arange(128)[None, :]
    put("cmask", ((s // 64 == t // 64) & (s <= t)).astype(np.float32))
    sm = np.ones((128, 512), np.float32)
    sm[:, ::64] = 0.0
    put("smask", sm)
    w2 = np.zeros((128, 512), np.float32)
    w2[112:128, :] = np.asarray(inp["gla_w_gk_up"][0], np.float32)
    put("w2", w2)
    return c


def build(NT, layers=(0, 1), final_norm=True, same_sync=("act", "dve", "pool")):
    nc = bass.Bass("TRN2", target_bir_lowering=False)
    nc.dge_precook = False
    chunks = plan(layers)
    NCH = len(chunks)
    xT_d = nc.dram_tensor("xT", [D, NT * TT], F32, kind="ExternalInput").ap()
    ws_d = nc.dram_tensor("wstream", [NCH, 128, 2048], F32R, kind="ExternalInput").ap()
    cs_d = nc.dram_tensor("consts", [128, NCONST], F32, kind="ExternalInput").ap()
    out_d = nc.dram_tensor("outT", [D, NT * TT], F32, kind="ExternalOutput").ap()

    R = Rec(same_engine_sync=same_sync)
    import os as _os
    if _os.environ.get("KLIMIT"):
        R.limit = int(_os.environ["KLIMIT"])
    with contextlib.ExitStack() as st:
        def sbt(name, shape, dt=F32, nbuf=1):
            t = st.enter_context(nc.sbuf_tensor(name, shape, dt))
            return t, [Buf(f"{name}{i}") for i in range(nbuf)]

        def tile(name, shape, dt=F32):
            t, b = sbt(name, shape, dt)
            return V(t[:], b)

        hT_t, hT_b = sbt("hT", [128, 8, TT], F32, 8)
        yT_t, yT_b = sbt("yT", [128, 8, TT], F32R, 8)
        oT_t, oT_b = sbt("oT", [128, 8, TT], F32R, 8)
        aT_t, aT_b = sbt("aT", [128, 11, TT], F32R, 11)
        hT = [V(hT_t[:, i, :], [hT_b[i]]) for i in range(8)]
        yT = [V(yT_t[:, i, :], [yT_b[i]]) for i in range(8)]
        oT = [V(oT_t[:, i, :], [oT_b[i]]) for i in range(8)]
        aT = [V(aT_t[:, i, :], [aT_b[i]]) for i in range(11)]
        hT_all = V(hT_t[:], hT_b)
        CS = tile("consts_sb", [128, NCONST])
        slots = [tile(f"slot{i}", [128, 2048], F32R) for i in range(NSLOT)]
        slot_sems = [R.new_dma_sem(f"ws{i}") for i in range(NSLOT)]

        def cst(name, j=0, w=1):
            o = _CO[name] + j
            return CS[:, o:o + w]

        ones1024 = tile("ones1024", [128, 128], F32R)
        ones128 = tile("ones128", [128, 128], F32R)
        ones256 = tile("ones256", [128, 128], F32R)
        ident_bf = tile("ident_bf", [128, 128], BF16)
        lb = tile("lb", [128, 8])
        clb = tile("clb", [128, 8])
        lbe = tile("lbe", [128, 24])
        nbgk = tile("nbgk", [128, 4])
        W2r = tile("W2r", [128, 512], F32R)
        G_sb = tile("G_sb", [128, TT], F32R)
        lnv = tile("lnv", [128, TT])
        rstd = tile("rstd", [128, TT])
        S32hg = [tile(f"S32hg{h}", [128, 128]) for h in range(8)]
        Sbfhg = [tile(f"Sbfhg{h}", [128, 128], BF16) for h in range(8)]
        S32gl = [tile(f"S32gl{h}", [128, 256]) for h in range(4)]
        Sbfgl = [tile(f"Sbfgl{h}", [128, 256], BF16) for h in range(4)]
        tails_t = st.enter_context(nc.sbuf_tensor("tails", [128, 2, 44, 2], F32))
        tails = [[V(tails_t[:, l, ct, :], [Buf(f"tail{l}_{ct}")]) for ct in range(44)] for l in range(2)]
        TM = []
        for s in range(2):
            d = {}
            for n in ("C", "D", "SG1", "E"):
                d[n] = tile(f"t{n}{s}", [128, TT])
            d["XA"] = tile(f"tXA{s}", [128, TT + 2])
            d["XG"] = tile(f"tXG{s}", [128, TT + 2])
            d["A"] = d["XA"][:, 0:TT]
            d["B"] = d["XG"][:, 0:TT]
            d["UA"] = d["C"]
            d["UG"] = d["D"]
            d["OSQ0"] = tile(f"tOSQ0{s}", [128, TT], F32R)
            d["OSQ1"] = tile(f"tOSQ1{s}", [128, TT], F32R)
            for n in ("kT", "kkT", "qT"):
                d[n] = tile(f"t{n}{s}", [128, TT], BF16)
            d["kktok"] = tile(f"tkktok{s}", [128, 4, 128], BF16)
            d["vtok"] = tile(f"tvtok{s}", [128, 4, 256], BF16)
            d["scT"] = [tile(f"tscT{s}{i}", [128, 128], BF16) for i in range(2)]
            d["ebl"] = tile(f"tebl{s}", [128, 8])
            TM.append(d)
        banks = []
        for i in range(8):
            t = st.enter_context(nc.psum_tensor(f"bank{i}", [128, TT], F32))
            banks.append((t, [Buf(f"bk{i}a"), Buf(f"bk{i}b")]))
        pstate = {"big": 0, "sc": 0, "tr": 0, "nbig": 6}

        def pbank():
            n = pstate["nbig"]
            i = pstate["big"] % n
            pstate["big"] += 1
            t, b = banks[i]
            return V(t[:], b)

        def psc_region():
            h = pstate["sc"] % 2
            pstate["sc"] += 1
            t, b = banks[6]
            return V(t[:, h * 128:(h + 1) * 128], [b[0]])

        def pU_region(nv):
            t, b = banks[6]
            return V(t[:, 256:256 + nv], [b[1]])

        def ptrans():
            h = pstate["tr"] % 2
            pstate["tr"] += 1
            t, b = banks[7]
            return V(t[:, h * 256:(h + 1) * 256].bitcast(BF16), [b[h]])

        def bufs_of(*vs):
            r = []
            for v in vs:
                if isinstance(v, V):
                    r += v.bufs
            return r

        def apof(v):
            return v.ap if isinstance(v, V) else v

        def mm(out, lhsT, rhs, start=True, stop=True, signal=None):
            sig = stop if signal is None else signal
            R.op("pe", lambda e: e.matmul(out.ap, lhsT.ap, rhs.ap, start=start, stop=stop),
                 reads=lhsT.bufs + rhs.bufs, writes=out.bufs, signal=sig)

        def tr(out, in_):
            R.op("pe", lambda e: e.transpose(out.ap, in_.ap, ident_bf.ap),
                 reads=in_.bufs + ident_bf.bufs, writes=out.bufs, signal=True)

        def ACT(out, in_, func, bias=None, scale=None):
            kw = {}
            if bias is not None:
                kw["bias"] = apof(bias)
            if scale is not None:
                kw["scale"] = apof(scale)
            R.op("act", lambda e: e.activation(out=out.ap, in_=in_.ap, func=func, **kw),
                 reads=bufs_of(in_, bias, scale), writes=out.bufs)

        def TTo(eng, out, in0, in1, op):
            R.op(eng, lambda e: e.tensor_tensor(out=out.ap, in0=in0.ap, in1=in1.ap, op=op),
                 reads=bufs_of(in0, in1), writes=out.bufs)

        def TS(eng, out, in0, s1, op0, s2=None, op1=None):
            kw = {}
            if op1 is None and eng == "pool":
                if op0 == ALU.add:
                    s2, op1 = 1.0, ALU.mult
                elif op0 == ALU.mult:
                    s2, op1 = 0.0, ALU.add
            if op1 is not None:
                kw["op1"] = op1
            R.op(eng, lambda e: e.tensor_scalar(out=out.ap, in0=in0.ap, scalar1=apof(s1), scalar2=apof(s2), op0=op0, **kw),
                 reads=bufs_of(in0, s1, s2), writes=out.bufs)

        def STT(eng, out, in0, scalar, in1, op0, op1):
            R.op(eng, lambda e: e.scalar_tensor_tensor(out=out.ap, in0=in0.ap, scalar=apof(scalar), in1=in1.ap, op0=op0, op1=op1),
                 reads=bufs_of(in0, scalar, in1), writes=out.bufs)

        def SCAN(out, d0, d1):
            R.op("dve", lambda e: e.tensor_tensor_scan(out=out.ap, data0=d0.ap, data1=d1.ap, initial=0.0, op0=ALU.mult, op1=ALU.add),
                 reads=bufs_of(d0, d1), writes=out.bufs)

        def RECIP(eng, out, in_, exact=True):
            R.op(eng, lambda e: e.reciprocal(out=out.ap, in_=in_.ap), reads=in_.bufs, writes=out.bufs)

        def CP(eng, out, in_):
            if eng == "act":
                R.op("act", lambda e: e.copy(out=out.ap, in_=in_.ap), reads=in_.bufs, writes=out.bufs)
            else:
                R.op(eng, lambda e: e.tensor_copy(out=out.ap, in_=in_.ap), reads=in_.bufs, writes=out.bufs)

        def MEMSET(eng, out, val):
            R.op(eng, lambda e: e.memset(out.ap, val), writes=out.bufs)

        wst = {"next_dma": 0, "cur": 0}
        total_chunks = NCH * NT

        def ws_get(kind, lag=1):
            j = wst["cur"]
            wst["cur"] += 1
            assert chunks[j % NCH][0] == kind, (chunks[j % NCH], kind)
            upto = min(total_chunks - 1, j + NSLOT - lag)
            while wst["next_dma"] <= upto:
                m = wst["next_dma"]
                wst["next_dma"] += 1
                k = m % NCH
                sl = slots[m % NSLOT]
                if chunks[k][0] == "gla_lr":
                    R.op("sp", lambda e, sl=sl, k=k: e.dma_start(out=sl.ap[:, 0:1024], in_=ws_d[k, :, 0:1024]),
                         writes=sl.bufs, dma_sem=slot_sems[m % NSLOT])
                else:
                    R.op("sp", lambda e, sl=sl, k=k: e.dma_start(out=sl.ap, in_=ws_d[k]),
                         writes=sl.bufs, dma_sem=slot_sems[m % NSLOT])
            return slots[j % NSLOT]

        s_c = R.new_dma_sem("ld_c")
        s_h = R.new_dma_sem("ld_h")
        s_o = R.new_dma_sem("st_o")
        R.op("sp", lambda e: e.dma_start(out=CS.ap, in_=cs_d), writes=CS.bufs, dma_sem=s_c)
        onesf = tile("onesf", [128, 128])
        for ot, val in ((ones1024, 1.0 / 1024), (ones128, 1.0 / 128), (ones256, 1.0 / 256)):
            MEMSET("pool", onesf, val)
            CP("act", ot, onesf)
        for h in range(8):
            MEMSET("pool", S32hg[h], 0.0)
            MEMSET("pool", Sbfhg[h], 0.0)
        for h in range(4):
            MEMSET("pool", S32gl[h], 0.0)
            MEMSET("pool", Sbfgl[h], 0.0)
        tails_all = V(tails_t[:], [tails[l][ct].bufs[0] for l in range(2) for ct in range(44)])
        MEMSET("pool", tails_all, 0.0)
        CP("act", ident_bf, cst("ident", 0, 128))
        CP("act", W2r, cst("w2", 0, 512))
        ACT(lbe, cst("lbl", 0, 24), AF.Exp)
        TTo("pool", lb, lbe[:, 0:8], lbe[:, 8:16], ALU.add)
        TTo("pool", lb, lb, lbe[:, 16:24], ALU.add)
        RECIP("dve", lb, lb, exact=True)
        TTo("dve", lb, lb, lbe[:, 0:8], ALU.mult)
        ACT(clb, lb, AF.Ln, scale=-1.0, bias=cst("one"))
        TS("pool", nbgk, cst("bgk", 0, 4), -1.0, ALU.mult)
        one = cst("one")
        eps = cst("eps")
        cmask = cst("cmask", 0, 128)
        smask = cst("smask", 0, 512)

        def rmsnorm(wname, woff, dst):
            for dt in range(8):
                ACT(yT[dt], hT[dt], AF.Square)
            pb = pbank()
            for dt in range(8):
                mm(pb, ones1024, yT[dt], start=(dt == 0), stop=(dt == 7))
            ACT(lnv, pb, AF.Ln, bias=eps)
            ACT(rstd, lnv, AF.Exp, scale=-0.5)
            for dt in range(8):
                if dt % 2 == 0:
                    STT("dve", dst[dt], hT[dt], cst(wname, woff + dt), rstd, ALU.mult, ALU.mult)
                else:
                    ACT(dst[dt], hT[dt], AF.Identity, scale=cst(wname, woff + dt))
                    TTo("pool", dst[dt], dst[dt], rstd, ALU.mult)

        def out_proj(kind):
            so = [ws_get(kind, lag=i + 1) for i in range(4)]
            for ft in range(8):
                pw = pbank()
                for h in range(8):
                    w = so[h // 2].rearrange("p (a f) -> p a f", a=2)[:, h % 2, ft * 128:(ft + 1) * 128]
                    mm(pw, w, oT[h], start=(h == 0), stop=(h == 7))
                TTo("dve", hT[ft], hT[ft], pw, ALU.add)

        def recurrence(tm, a_cols, n_vt, Sbf, S32, vcol0):
            kT, kkT, qT, kktok, vtok, ebl = tm["kT"], tm["kkT"], tm["qT"], tm["kktok"], tm["vtok"], tm["ebl"]
            po = [pbank() for _ in range(n_vt)]
            nv = 128 * n_vt
            for blk in range(4):
                cols = slice(blk * 128, (blk + 1) * 128)
                psc = psc_region()
                mm(psc, kT[:, cols], qT[:, cols])
                scT = tm["scT"][blk % 2]
                TTo("dve", scT, psc, cmask, ALU.mult)
                for vt in range(n_vt):
                    mm(po[vt][:, cols], vtok[:, blk, vcol0 + vt * 128:vcol0 + (vt + 1) * 128], scT,
                       start=True, stop=False, signal=False)
                for ci in range(2):
                    c = blk * 2 + ci
                    ccols = slice(c * 64, (c + 1) * 64)
                    rows = slice(ci * 64, (ci + 1) * 64)
                    for vt in range(n_vt):
                        mm(po[vt][:, ccols], Sbf[:, vt * 128:(vt + 1) * 128], qT[:, ccols],
                           start=False, stop=True, signal=(vt == n_vt - 1))
                    pU = pU_region(nv)
                    mm(pU, kktok[rows, blk, :], vtok[rows, blk, vcol0:vcol0 + nv])
                    STT("dve", S32, S32, ebl[:, c:c + 1], pU, ALU.mult, ALU.add)
                    CP("act", Sbf, S32)
            return po

        def kk_transposes(tm):
            ptr = ptrans()
            for blk in range(4):
                tr(ptr[:, blk * 128:(blk + 1) * 128], tm["kkT"][:, blk * 128:(blk + 1) * 128])
            CP("act", tm["kktok"].rearrange("p a b -> p (a b)"), ptr)

        def silu_gate(dst, pg):
            ACT(dst, pg, AF.Exp, scale=-1.0)
            ACT(dst, dst, AF.Ln, bias=one)
            ACT(dst, dst, AF.Exp, scale=-1.0)
            TTo("dve", dst, pg, dst, ALU.mult)

        def mixer_hg():
            rmsnorm("nm", 0, yT)
            for hp in range(4):
                sq = ws_get("hg_in", 1).rearrange("p (dt f) -> p dt f", dt=8)
                sf = ws_get("hg_in", 2).rearrange("p (dt f) -> p dt f", dt=8)
                si = ws_get("hg_in", 3).rearrange("p (dt f) -> p dt f", dt=8)
                sg = ws_get("hg_in", 4).rearrange("p (dt f) -> p dt f", dt=8)
                tmv = TM[hp % 2]
                for blk in range(4):
                    pv = pbank()[:, 0:256]
                    for dt in range(8):
                        mm(pv, yT[dt][:, blk * 128:(blk + 1) * 128], si[:, dt, :], start=(dt == 0), stop=(dt == 7))
                    CP("act", tmv["vtok"][:, blk, :], pv)
                for a in range(2):
                    h = 2 * hp + a
                    tm = dict(TM[h % 2])
                    tm["vtok"] = tmv["vtok"]
                    A_, B_, C_, D_ = tm["A"], tm["B"], tm["C"], tm["D"]
                    fc = slice(a * 128, (a + 1) * 128)
                    pf = pbank()
                    for dt in range(8):
                        mm(pf, sf[:, dt, fc], yT[dt], start=(dt == 0), stop=(dt == 7))
                    pq = pbank()
                    for dt in range(8):
                        mm(pq, sq[:, dt, fc], yT[dt], start=(dt == 0), stop=(dt == 7))
                    pg = pbank()
                    for dt in range(8):
                        mm(pg, sg[:, dt, fc], yT[dt], start=(dt == 0), stop=(dt == 7))
                    ACT(A_, pf, AF.Exp, scale=-1.0)
                    ACT(B_, A_, AF.Ln, bias=one)
                    ACT(C_, A_, AF.Ln, scale=lb[:, h:h + 1], bias=one)
                    TTo("pool", C_, C_, B_, ALU.subtract)
                    SCAN(D_, smask, C_)
                    TTo("dve", A_, pf, B_, ALU.add)
                    TTo("pool", A_, A_, D_, ALU.add)
                    ACT(tm["kT"], A_, AF.Exp, scale=-1.0, bias=clb[:, h:h + 1])
                    ACT(tm["ebl"], D_[:, 63::64], AF.Exp)
                    TTo("pool", tm["kkT"].rearrange("p (c k) -> p c k", k=64),
                        tm["kT"].rearrange("p (c k) -> p c k", k=64), tm["ebl"].bcast3(64), ALU.mult)
                    kk_transposes(tm)
                    ACT(C_, pq, AF.Exp, scale=-1.0)
                    ACT(C_, C_, AF.Ln, bias=one)
                    TTo("pool", C_, D_, C_, ALU.subtract)
                    ACT(C_, C_, AF.Exp, bias=cst("lnc"))
                    TTo("dve", tm["qT"], pq, C_, ALU.mult)
                    silu_gate(A_, pg)
                    po = recurrence(tm, None, 1, Sbfhg[h], S32hg[h], a * 128)[0]
                    ACT(tm["OSQ0"], po, AF.Square)
                    pss = pbank()
                    mm(pss, ones128, tm["OSQ0"])
                    ACT(C_, pss, AF.Ln, bias=eps)
                    ACT(C_, C_, AF.Exp, scale=-0.5)
                    STT("dve", D_, po, cst("hgnw"), C_, ALU.mult, ALU.mult)
                    TTo("pool", oT[h], D_, A_, ALU.mult)
            out_proj("hg_out")

        def mixer_gla():
            rmsnorm("nm", 8, yT)
            slr = ws_get("gla_lr", 1)[:, 0:1024].rearrange("p (dt f) -> p dt f", dt=8)
            pG = pbank()
            for dt in range(8):
                mm(pG, slr[:, dt, :], yT[dt], start=(dt == 0), stop=(dt == 7))
            CP("act", G_sb, pG)
            for hd in range(4):
                sqk = ws_get("gla_qk", 1).rearrange("p (dt two f) -> p dt two f", dt=8, two=2)
                sv = ws_get("gla_v", 2).rearrange("p (dt f) -> p dt f", dt=8)
                sg = ws_get("gla_g", 3).rearrange("p (dt f) -> p dt f", dt=8)
                tm = TM[hd % 2]
                A_, B_, C_, D_ = tm["A"], tm["B"], tm["C"], tm["D"]
                SG = [tm["E"], tm["SG1"]]
                for vt in range(2):
                    pg = pbank()
                    for dt in range(8):
                        mm(pg, sg[:, dt, vt * 128:(vt + 1) * 128], yT[dt], start=(dt == 0), stop=(dt == 7))
                    silu_gate(SG[vt], pg)
                for blk in range(4):
                    pv = pbank()[:, 0:256]
                    for dt in range(8):
                        mm(pv, yT[dt][:, blk * 128:(blk + 1) * 128], sv[:, dt, :], start=(dt == 0), stop=(dt == 7))
                    CP("act", tm["vtok"][:, blk, :], pv)
                pgk = pbank()
                mm(pgk, W2r[:, hd * 128:(hd + 1) * 128], G_sb)
                ACT(A_, pgk, AF.Exp, scale=-1.0, bias=nbgk[:, hd:hd + 1])
                ACT(A_, A_, AF.Ln, bias=one)
                SCAN(D_, smask, A_)
                ACT(B_, D_, AF.Exp, scale=-1.0 / 16)
                ACT(C_, D_, AF.Exp, scale=1.0 / 16)
                ACT(tm["ebl"], D_[:, 63::64], AF.Exp, scale=-1.0 / 16)
                pq = pbank()
                for dt in range(8):
                    mm(pq, sqk[:, dt, 0, :], yT[dt], start=(dt == 0), stop=(dt == 7))
                pk = pbank()
                for dt in range(8):
                    mm(pk, sqk[:, dt, 1, :], yT[dt], start=(dt == 0), stop=(dt == 7))
                STT("dve", tm["qT"], pq, 128 ** -0.5, B_, ALU.mult, ALU.mult)
                TTo("dve", tm["kT"], pk, C_, ALU.mult)
                TTo("pool", tm["kkT"].rearrange("p (c k) -> p c k", k=64),
                    tm["kT"].rearrange("p (c k) -> p c k", k=64), tm["ebl"].bcast3(64), ALU.mult)
                kk_transposes(tm)
                po = recurrence(tm, None, 2, Sbfgl[hd], S32gl[hd], 0)
                OSQ = [tm["OSQ0"], tm["OSQ1"]]
                for vt in range(2):
                    ACT(OSQ[vt], po[vt], AF.Square)
                pss = pbank()
                mm(pss, ones256, OSQ[0], start=True, stop=False)
                mm(pss, ones256, OSQ[1], start=False, stop=True)
                ACT(C_, pss, AF.Ln, bias=eps)
                ACT(C_, C_, AF.Exp, scale=-0.5)
                for vt in range(2):
                    STT("dve", D_, po[vt], cst("glanw", vt), C_, ALU.mult, ALU.mult)
                    TTo("pool", oT[hd * 2 + vt], D_, SG[vt], ALU.mult)
            out_proj("gla_out")

        def ffn(l):
            rmsnorm("nf", 8 * l, yT)
            pstate["nbig"] = 8
            for g in range(2):
                for ci in range(11):
                    c = g * 11 + ci
                    su = ws_get("up", 1).rearrange("p (dt two f) -> p dt two f", dt=8, two=2)
                    tm = TM[ci % 2]
                    pa = pbank()
                    for dt in range(8):
                        mm(pa, su[:, dt, 0, :], yT[dt], start=(dt == 0), stop=(dt == 7))
                    pg = pbank()
                    for dt in range(8):
                        mm(pg, su[:, dt, 1, :], yT[dt], start=(dt == 0), stop=(dt == 7))
                    for (ps, ct, xs, u, e1, e2) in ((pa, c, tm["XA"], tm["UA"], "dve", "dve"),
                                                     (pg, NCT + c, tm["XG"], tm["UG"], "dve", "dve")):
                        cw = lambda j, ct=ct: cst("cw", (l * 3 + j) * 44 + ct)
                        CP("act", xs[:, 2:TT + 2], ps)
                        CP("pool", xs[:, 0:2], tails[l][ct])
                        if ct < NCT:
                            ACT(u, xs[:, 0:TT], AF.Identity, scale=cw(0), bias=cst("cb", l * 44 + ct))
                        else:
                            TS("dve", u, xs[:, 0:TT], cw(0), ALU.mult, cst("cb", l * 44 + ct), ALU.add)
                        STT(e1, u, xs[:, 1:TT + 1], cw(1), u, ALU.mult, ALU.add)
                        STT(e2, u, xs[:, 2:TT + 2], cw(2), u, ALU.mult, ALU.add)
                        CP("pool", tails[l][ct], xs[:, TT:TT + 2])
                    E = tm["E"]
                    ACT(E, tm["UG"], AF.Silu)
                    TTo("pool", aT[ci], E, tm["UA"], ALU.mult)
                for half in range(2):
                    accs = [pbank() for _ in range(4)]
                    for part in (range(0, 4), range(4, 8), range(8, 11)):
                        sd = ws_get("down", 1).rearrange("p (i f) -> p i f", i=4)
                        for i, ci in enumerate(part):
                            for j in range(4):
                                mm(accs[j], sd[:, i, j * 128:(j + 1) * 128], aT[ci], start=(ci == 0), stop=(ci == 10))
                    for j in range(4):
                        ft = half * 4 + j
                        TTo("dve", hT[ft], hT[ft], accs[j], ALU.add)
            pstate["nbig"] = 6
            pstate["big"] = 0

        out_toks = []
        for tI in range(NT):
            t0 = tI * TT
            R.op("sp", lambda e, t0=t0: e.dma_start(out=hT_t[:], in_=xT_d[:, t0:t0 + TT].rearrange("(dt p) t -> p dt t", p=128)),
                 writes=hT_b, dma_sem=s_h)
            for l in layers:
                if l == 0:
                    mixer_hg()
                else:
                    mixer_gla()
                ffn(l)
            if final_norm:
                rmsnorm("nfin", 0, hT)
            tok = R.op("sp", lambda e, t0=t0: e.dma_start(out=out_d[:, t0:t0 + TT].rearrange("(dt p) t -> p dt t", p=128), in_=hT_t[:]),
                       reads=hT_b, dma_sem=s_o, force=True)
            out_toks.append(tok)
        R.final_wait("sp", [out_toks[-1]])
        assert wst["cur"] == total_chunks
        print("total ops", R.total, {k: v for k, v in R.cnt.items()})
        R.emit(nc)
    return nc


_PROGS = {}


def _prog(NT, layers, final_norm):
    key = (NT, tuple(layers), final_norm)
    if key not in _PROGS:
        _PROGS[key] = build(NT, layers, final_norm)
    return _PROGS[key]


FUSED = True


def kernel(**inputs):
    inp = {k: np.asarray(v) for k, v in inputs.items()}
    x = inp["x"].astype(np.float32, copy=False)
    B = x.shape[0]
    consts = pack_consts(inp)
    xT = [np.ascontiguousarray(x[b].T) for b in range(B)]
    if FUSED:
        stages = [((0, 1), True)]
    else:
        stages = [((0,), False), ((1,), True)]
    cur = xT
    for layers, fin in stages:
        ws = pack_wstream(inp, layers)
        nc = _prog(T // TT, layers, fin)
        in_maps = [{"xT": cur[b], "wstream": ws, "consts": consts} for b in range(B)]
        res = run_bass_kernel_spmd(nc, in_maps, core_ids=list(range(B)))
        cur = [np.asarray(res.results[b]["outT"]) for b in range(B)]
    out = np.stack([np.ascontiguousarray(cur[b].T) for b in range(B)], axis=0)
    return out.astype(np.float32, copy=False)
```

```python
import contextlib
import numpy as np
import concourse.bass as bass
import concourse.mybir as mybir
from concourse.bass_utils import run_bass_kernel_spmd

F32 = mybir.dt.float32
F32R = mybir.dt.float32r
BF16 = mybir.dt.bfloat16
AF = mybir.ActivationFunctionType
ALU = mybir.AluOpType

D = 1024
T = 4096
TT = 512
DFF = 2816
NCT = 22
NSLOT = 7
ENGS = ("pe", "act", "dve", "pool", "sp")


class Buf:
    __slots__ = ("name", "last_w", "readers")

    def __init__(self, name=""):
        self.name = name
        self.last_w = None
        self.readers = {}


class Rec:
    def __init__(self, same_engine_sync=("act", "dve", "pool")):
        self.ops = {e: [] for e in ENGS}
        self.cnt = {e: 0 for e in ENGS}
        self.waited = {}
        self.same_sync = set(same_engine_sync)
        self.dma_sems = []
        self.prog = []
        self.finals = []

    def new_dma_sem(self, name):
        self.cnt[name] = 0
        self.dma_sems.append(name)
        return name

    def op(self, eng, issue, reads=(), writes=(), signal=True, dma_sem=None, force=False, dur=500.0, avail=None):
        self.prog.append(dict(eng=eng, issue=issue, reads=list(reads), writes=list(writes), signal=signal,
                              dma_sem=dma_sem, dur=float(dur), avail=float(avail if avail is not None else dur)))
        return None

    def _op(self, eng, issue, reads=(), writes=(), signal=True, dma_sem=None):
        need = {}

        def add(tok):
            if tok is None:
                return
            k, v = tok
            if need.get(k, 0) < v:
                need[k] = v

        for b in reads:
            add(b.last_w)
        for b in writes:
            add(b.last_w)
            for k, v in b.readers.items():
                add((k, v))
        waits = []
        for k, v in need.items():
            if k == eng and eng not in self.same_sync:
                continue
            if k == eng and v > self.cnt[eng]:
                continue
            if self.waited.get((eng, k), 0) >= v:
                continue
            self.waited[(eng, k)] = v
            waits.append((k, v))
        if dma_sem is not None:
            self.cnt[dma_sem] += 16
            tok = (dma_sem, self.cnt[dma_sem])
            inc = (dma_sem, 16)
        elif signal:
            self.cnt[eng] += 1
            tok = (eng, self.cnt[eng])
            inc = (eng, 1)
        else:
            tok = (eng, self.cnt[eng] + 1)
            inc = None
        for b in reads:
            if b.readers.get(tok[0], 0) < tok[1]:
                b.readers[tok[0]] = tok[1]
        for b in writes:
            b.last_w = tok
            b.readers = {}
        self.ops[eng].append((waits, issue, inc))
        return tok

    def final_wait(self, eng, toks):
        self.finals.append((eng, [(k, v) for (k, v) in toks]))

    def schedule(self, window=96, lat=350.0):
        prog = self.prog
        units = []
        open_pe = None
        for o in prog:
            if o["eng"] == "pe":
                if open_pe is None:
                    open_pe = dict(eng="pe", ops=[], deps=set(), idx=len(units))
                    units.append(open_pe)
                open_pe["ops"].append(o)
                o["unit"] = open_pe["idx"]
                if o["signal"]:
                    open_pe = None
            else:
                u = dict(eng=o["eng"], ops=[o], deps=set(), idx=len(units))
                units.append(u)
                o["unit"] = u["idx"]
        assert open_pe is None
        lastw = {}
        rdrs = {}
        for o in prog:
            u = o["unit"]
            d = units[u]["deps"]
            for b in o["reads"]:
                w = lastw.get(id(b))
                if w is not None and w != u:
                    d.add(w)
            for b in o["writes"]:
                w = lastw.get(id(b))
                if w is not None and w != u:
                    d.add(w)
                for r in rdrs.get(id(b), ()):
                    if r != u:
                        d.add(r)
            for b in o["reads"]:
                rdrs.setdefault(id(b), set()).add(u)
            for b in o["writes"]:
                lastw[id(b)] = u
                rdrs[id(b)] = set()
        n = len(units)
        succ = [[] for _ in range(n)]
        ndep = [0] * n
        for u in units:
            ndep[u["idx"]] = len(u["deps"])
            for d in u["deps"]:
                succ[d].append(u["idx"])
        per_eng = {e: [u["idx"] for u in units if u["eng"] == e] for e in ENGS}
        pos = {e: 0 for e in ENGS}
        done = [False] * n
        fin = [0.0] * n
        free = {e: 0.0 for e in ENGS}
        order = []
        remaining = n
        while remaining:
            best = None
            for e in ENGS:
                lst = per_eng[e]
                p = pos[e]
                while p < len(lst) and done[lst[p]]:
                    p += 1
                pos[e] = p
                cnt = 0
                q = p
                win = 1 if e == "pe" else window
                while q < len(lst) and cnt < win:
                    ui = lst[q]
                    q += 1
                    if done[ui]:
                        continue
                    cnt += 1
                    if ndep[ui]:
                        continue
                    u = units[ui]
                    st = free[e]
                    for d in u["deps"]:
                        t = fin[d] + lat
                        if t > st:
                            st = t
                    if best is None or st < best[0] or (st == best[0] and ui < best[1]):
                        best = (st, ui)
                    if st <= free[e]:
                        break
            st, ui = best
            u = units[ui]
            e = u["eng"]
            t = st
            for o in u["ops"]:
                t += o["dur"]
            free[e] = t
            fin[ui] = st + sum(o["dur"] for o in u["ops"][:-1]) + u["ops"][-1]["avail"]
            done[ui] = True
            remaining -= 1
            for sidx in succ[ui]:
                ndep[sidx] -= 1
            order.append(ui)
        self.est_ns = max(fin)
        for ui in order:
            for o in units[ui]["ops"]:
                self._op(o["eng"], o["issue"], o["reads"], o["writes"], o["signal"], o["dma_sem"])

    def emit(self, nc):
        import os as _os
        self.schedule(window=int(_os.environ.get("KWINDOW", "96")))
        for eng, toks in self.finals:
            self.ops[eng].append((toks, None, None))
        names = [e for e in ENGS if e != "sp"] + self.dma_sems
        with contextlib.ExitStack() as st:
            sems = {n: st.enter_context(nc.semaphore("s_" + n)) for n in names}
            block = st.enter_context(nc.Block())

            def replay(name, e):
                for waits, issue, inc in self.ops[name]:
                    for k, v in waits:
                        e.wait_ge(sems[k], v)
                    if issue is None:
                        continue
                    ins = issue(e)
                    if inc is not None:
                        ins.then_inc(sems[inc[0]], inc[1])

            @block.tensor
            def _(e):
                replay("pe", e)

            @block.scalar
            def _(e):
                replay("act", e)

            @block.vector
            def _(e):
                replay("dve", e)

            @block.gpsimd
            def _(e):
                replay("pool", e)

            @block.sync
            def _(e):
                replay("sp", e)


class V:
    __slots__ = ("ap", "bufs")

    def __init__(self, ap, bufs):
        self.ap = ap
        self.bufs = list(bufs)

    def __getitem__(self, k):
        return V(self.ap[k], self.bufs)

    def bitcast(self, dt):
        return V(self.ap.bitcast(dt), self.bufs)

    def rearrange(self, s, **kw):
        return V(self.ap.rearrange(s, **kw), self.bufs)

    def bcast3(self, n):
        return V(self.ap.unsqueeze(2).broadcast_to(list(self.ap.shape) + [n]), self.bufs)


def plan(layers):
    ch = []
    for l in layers:
        if l == 0:
            for hp in range(4):
                for k in ("q", "f", "i", "g"):
                    ch.append(("hg_in", k, hp))
            for hp in range(4):
                ch.append(("hg_out", hp))
        else:
            ch.append(("gla_lr",))
            for hd in range(4):
                ch.append(("gla_qk", hd))
                ch.append(("gla_v", hd))
                ch.append(("gla_g", hd))
            for hd in range(4):
                ch.append(("gla_out", hd))
        for g in range(2):
            cs = list(range(g * 11, (g + 1) * 11))
            for c in cs:
                ch.append(("up", l, c))
            for half in range(2):
                for part in (cs[0:4], cs[4:8], cs[8:11]):
                    ch.append(("down", l, half, tuple(part)))
    return ch


def _cols(W, cols):
    sub = W[:, cols].reshape(8, 128, len(cols)).transpose(1, 0, 2)
    return sub.reshape(128, -1)


def pack_wstream(inp, layers):
    chunks = plan(layers)
    out = np.zeros((len(chunks), 128, 2048), np.float32)
    hg_in = inp["hg_w_in"][0]
    hg_out = inp["hg_w_out"][0]
    gl_in = inp["gla_w_in"][0]
    gl_out = inp["gla_w_out"][0]
    for i, c in enumerate(chunks):
        kind = c[0]
        if kind == "hg_in":
            base = {"q": 0, "f": 1024, "i": 2048, "g": 3072}[c[1]] + c[2] * 256
            out[i] = _cols(hg_in, np.arange(base, base + 256))
        elif kind == "hg_out":
            r0 = c[1] * 256
            out[i] = hg_out[r0:r0 + 256].reshape(2, 128, 1024).transpose(1, 0, 2).reshape(128, 2048)
        elif kind == "gla_lr":
            out[i, :, :1024] = _cols(gl_in, np.arange(2960, 3088))
        elif kind == "gla_qk":
            hd = c[1]
            cols = np.concatenate([np.arange(hd * 128, hd * 128 + 128), np.arange(512 + hd * 128, 512 + hd * 128 + 128)])
            out[i] = _cols(gl_in, cols)
        elif kind == "gla_v":
            out[i] = _cols(gl_in, np.arange(1024 + c[1] * 256, 1024 + c[1] * 256 + 256))
        elif kind == "gla_g":
            out[i] = _cols(gl_in, np.arange(2048 + c[1] * 256, 2048 + c[1] * 256 + 256))
        elif kind == "gla_out":
            r0 = c[1] * 256
            out[i] = gl_out[r0:r0 + 256].reshape(2, 128, 1024).transpose(1, 0, 2).reshape(128, 2048)
        elif kind == "up":
            l, ct = c[1], c[2]
            cols = np.concatenate([np.arange(ct * 128, ct * 128 + 128), np.arange(DFF + ct * 128, DFF + ct * 128 + 128)])
            out[i] = _cols(inp["ffn_w_up"][l], cols)
        elif kind == "down":
            l, half, part = c[1], c[2], c[3]
            wd = inp["ffn_w_down"][l]
            for j, ct in enumerate(part):
                out[i, :, j * 512:(j + 1) * 512] = wd[ct * 128:(ct + 1) * 128, half * 512:(half + 1) * 512]
    return out


_CO = {}
_off = 0
for _n, _w in (("nm", 16), ("nf", 16), ("nfin", 8), ("lbl", 24), ("hgnw", 1), ("bgk", 4), ("glanw", 2),
               ("cw", 264), ("cb", 88), ("eps", 1), ("one", 1), ("lnc", 1), ("ident", 128), ("cmask", 128),
               ("smask", 512), ("w2", 512)):
    _CO[_n] = _off
    _off += _w
NCONST = _off


def pack_consts(inp):
    c = np.zeros((128, NCONST), np.float32)

    def put(name, arr):
        arr = np.asarray(arr, np.float32)
        c[:, _CO[name]:_CO[name] + arr.shape[1]] = arr

    pd = lambda v: np.asarray(v, np.float32).reshape(-1, 128).T
    put("nm", np.concatenate([pd(inp["norm_mixer_w"][l]) for l in range(2)], axis=1))
    put("nf", np.concatenate([pd(inp["norm_ffn_w"][l]) for l in range(2)], axis=1))
    put("nfin", pd(inp["norm_final_w"]))
    put("lbl", np.concatenate([pd(inp["lb_logits"][k]) for k in range(3)], axis=1))
    put("hgnw", pd(inp["hg_norm_w"][0]))
    put("bgk", pd(inp["gla_b_gk_up"][0]))
    put("glanw", pd(inp["gla_norm_w"][0]))
    cw = np.asarray(inp["ffn_conv_w"], np.float32)
    put("cw", np.concatenate([pd(cw[l, j]) for l in range(2) for j in range(3)], axis=1))
    cb = np.asarray(inp["ffn_conv_b"], np.float32)
    put("cb", np.concatenate([pd(cb[l]) for l in range(2)], axis=1))
    put("eps", np.full((128, 1), 1e-6, np.float32))
    put("one", np.ones((128, 1), np.float32))
    put("lnc", np.full((128, 1), np.log(128.0 ** -0.5), np.float32))
    put("ident", np.eye(128, dtype=np.float32))
    s = np.arange(128)[:, None]
    t = np.arange(128)[None, :]
    put("cmask", ((s // 64 == t // 64) & (s <= t)).astype(np.float32))
    sm = np.ones((128, 512), np.float32)
    sm[:, ::64] = 0.0
    put("smask", sm)
    w2 = np.zeros((128, 512), np.float32)
    w2[112:128, :] = np.asarray(inp["gla_w_gk_up"][0], np.float32)
    put("w2", w2)
    return c


def build(NT, layers=(0, 1), final_norm=True, same_sync=("act", "dve", "pool")):
    nc = bass.Bass("TRN2", target_bir_lowering=False)
    nc.dge_precook = False
    chunks = plan(layers)
    NCH = len(chunks)
    xT_d = nc.dram_tensor("xT", [NT, 128, 8 * TT], F32, kind="ExternalInput").ap()
    ws_d = nc.dram_tensor("wstream", [NCH, 128, 2048], F32R, kind="ExternalInput").ap()
    cs_d = nc.dram_tensor("consts", [128, NCONST], F32, kind="ExternalInput").ap()
    out_d = nc.dram_tensor("outT", [NT, 128, 8 * TT], F32, kind="ExternalOutput").ap()

    R = Rec(same_engine_sync=same_sync)
    with contextlib.ExitStack() as st:
        def sbt(name, shape, dt=F32, nbuf=1):
            t = st.enter_context(nc.sbuf_tensor(name, shape, dt))
            return t, [Buf(f"{name}{i}") for i in range(nbuf)]

        def tile(name, shape, dt=F32):
            t, b = sbt(name, shape, dt)
            return V(t[:], b)

        hT_t, hT_b = sbt("hT", [128, 8, TT], F32, 8)
        yT_t, yT_b = sbt("yT", [128, 8, TT], F32R, 8)
        oT_t, oT_b = sbt("oT", [128, 8, TT], F32R, 8)
        aT_t, aT_b = sbt("aT", [128, 11, TT], F32R, 11)
        hT = [V(hT_t[:, i, :], [hT_b[i]]) for i in range(8)]
        yT = [V(yT_t[:, i, :], [yT_b[i]]) for i in range(8)]
        oT = [V(oT_t[:, i, :], [oT_b[i]]) for i in range(8)]
        aT = [V(aT_t[:, i, :], [aT_b[i]]) for i in range(11)]
        hT_all = V(hT_t[:], hT_b)
        CS = tile("consts_sb", [128, NCONST])
        slots = [tile(f"slot{i}", [128, 2048], F32R) for i in range(NSLOT)]
        slot_sems = [R.new_dma_sem(f"ws{i}") for i in range(NSLOT)]

        def cst(name, j=0, w=1):
            o = _CO[name] + j
            return CS[:, o:o + w]

        ones1024 = tile("ones1024", [128, 128], F32R)
        ones128 = tile("ones128", [128, 128], F32R)
        ones256 = tile("ones256", [128, 128], F32R)
        ident_bf = tile("ident_bf", [128, 128], BF16)
        lb = tile("lb", [128, 8])
        clb = tile("clb", [128, 8])
        lbe = tile("lbe", [128, 24])
        nbgk = tile("nbgk", [128, 4])
        W2r = tile("W2r", [128, 512], F32R)
        G_sb = tile("G_sb", [128, TT], F32R)
        lnv = tile("lnv", [128, TT])
        rstd = tile("rstd", [128, TT])
        S32hg = [tile(f"S32hg{h}", [128, 128]) for h in range(8)]
        Sbfhg = [tile(f"Sbfhg{h}", [128, 128], BF16) for h in range(8)]
        S32gl = [tile(f"S32gl{h}", [128, 256]) for h in range(4)]
        Sbfgl = [tile(f"Sbfgl{h}", [128, 256], BF16) for h in range(4)]
        tails_t = st.enter_context(nc.sbuf_tensor("tails", [128, 2, 44, 2], F32))
        tails = [[V(tails_t[:, l, ct, :], [Buf(f"tail{l}_{ct}")]) for ct in range(44)] for l in range(2)]
        TM = []
        for s in range(2):
            d = {}
            for n in ("C", "D", "SG1", "E"):
                d[n] = tile(f"t{n}{s}", [128, TT])
            d["XA"] = tile(f"tXA{s}", [128, TT + 2])
            d["XG"] = tile(f"tXG{s}", [128, TT + 2])
            d["A"] = d["XA"][:, 0:TT]
            d["B"] = d["XG"][:, 0:TT]
            d["UA"] = d["C"]
            d["UG"] = d["D"]
            d["OSQ0"] = tile(f"tOSQ0{s}", [128, TT], F32R)
            d["OSQ1"] = tile(f"tOSQ1{s}", [128, TT], F32R)
            for n in ("kT", "kkT", "qT"):
                d[n] = tile(f"t{n}{s}", [128, TT], BF16)
            d["kktok"] = tile(f"tkktok{s}", [128, 4, 128], BF16)
            d["vtok"] = tile(f"tvtok{s}", [128, 4, 256], BF16)
            d["scT"] = [tile(f"tscT{s}{i}", [128, 128], BF16) for i in range(2)]
            d["ebl"] = tile(f"tebl{s}", [128, 8])
            TM.append(d)
        banks = []
        for i in range(8):
            t = st.enter_context(nc.psum_tensor(f"bank{i}", [128, TT], F32))
            banks.append((t, [Buf(f"bk{i}a"), Buf(f"bk{i}b")]))
        pstate = {"big": 0, "sc": 0, "tr": 0, "nbig": 6}

        def pbank():
            n = pstate["nbig"]
            i = pstate["big"] % n
            pstate["big"] += 1
            t, b = banks[i]
            return V(t[:], b)

        def psc_region():
            h = pstate["sc"] % 2
            pstate["sc"] += 1
            t, b = banks[6]
            return V(t[:, h * 128:(h + 1) * 128], [b[0]])

        def pU_region(nv):
            t, b = banks[6]
            return V(t[:, 256:256 + nv], [b[1]])

        def ptrans():
            h = pstate["tr"] % 2
            pstate["tr"] += 1
            t, b = banks[7]
            return V(t[:, h * 256:(h + 1) * 256].bitcast(BF16), [b[h]])

        def bufs_of(*vs):
            r = []
            for v in vs:
                if isinstance(v, V):
                    r += v.bufs
            return r

        def apof(v):
            return v.ap if isinstance(v, V) else v

        def mm(out, lhsT, rhs, start=True, stop=True, signal=None):
            sig = stop if signal is None else signal
            R.op("pe", lambda e: e.matmul(out.ap, lhsT.ap, rhs.ap, start=start, stop=stop),
                 reads=lhsT.bufs + rhs.bufs, writes=out.bufs, signal=sig,
                 dur=max(64, rhs.ap.shape[-1]) / 2.2 + 20)

        def tr(out, in_):
            R.op("pe", lambda e: e.transpose(out.ap, in_.ap, ident_bf.ap),
                 reads=in_.bufs + ident_bf.bufs, writes=out.bufs, signal=True, dur=90)

        def nfree(v):
            n = 1
            for d in v.ap.shape[1:]:
                n *= d
            return n

        def edur(eng, v, mult=1.0):
            n = nfree(v) * mult
            if eng == "act":
                return 220 + 0.85 * n
            if eng == "dve":
                return 120 + 1.0 * n
            return 150 + 2.3 * n

        def ACT(out, in_, func, bias=None, scale=None):
            kw = {}
            if bias is not None:
                kw["bias"] = apof(bias)
            if scale is not None:
                kw["scale"] = apof(scale)
            R.op("act", lambda e: e.activation(out=out.ap, in_=in_.ap, func=func, **kw),
                 reads=bufs_of(in_, bias, scale), writes=out.bufs, dur=edur("act", out))

        def TTo(eng, out, in0, in1, op):
            R.op(eng, lambda e: e.tensor_tensor(out=out.ap, in0=in0.ap, in1=in1.ap, op=op),
                 reads=bufs_of(in0, in1), writes=out.bufs, dur=edur(eng, out))

        def TS(eng, out, in0, s1, op0, s2=None, op1=None):
            kw = {}
            if op1 is None and eng == "pool":
                if op0 == ALU.add:
                    s2, op1 = 1.0, ALU.mult
                elif op0 == ALU.mult:
                    s2, op1 = 0.0, ALU.add
            if op1 is not None:
                kw["op1"] = op1
            R.op(eng, lambda e: e.tensor_scalar(out=out.ap, in0=in0.ap, scalar1=apof(s1), scalar2=apof(s2), op0=op0, **kw),
                 reads=bufs_of(in0, s1, s2), writes=out.bufs, dur=edur(eng, out))

        def STT(eng, out, in0, scalar, in1, op0, op1):
            R.op(eng, lambda e: e.scalar_tensor_tensor(out=out.ap, in0=in0.ap, scalar=apof(scalar), in1=in1.ap, op0=op0, op1=op1),
                 reads=bufs_of(in0, scalar, in1), writes=out.bufs, dur=edur(eng, out))

        def SCAN(out, d0, d1):
            R.op("dve", lambda e: e.tensor_tensor_scan(out=out.ap, data0=d0.ap, data1=d1.ap, initial=0.0, op0=ALU.mult, op1=ALU.add),
                 reads=bufs_of(d0, d1), writes=out.bufs, dur=edur("dve", out, 2.0))

        def RECIP(eng, out, in_, exact=True):
            R.op(eng, lambda e: e.reciprocal(out=out.ap, in_=in_.ap), reads=in_.bufs, writes=out.bufs, dur=edur(eng, out, 6.0))

        def CP(eng, out, in_):
            if eng == "act":
                R.op("act", lambda e: e.copy(out=out.ap, in_=in_.ap), reads=in_.bufs, writes=out.bufs, dur=edur("act", out))
            else:
                R.op(eng, lambda e: e.tensor_copy(out=out.ap, in_=in_.ap), reads=in_.bufs, writes=out.bufs, dur=edur(eng, out))

        def MEMSET(eng, out, val):
            R.op(eng, lambda e: e.memset(out.ap, val), writes=out.bufs, dur=edur(eng, out, 0.5))

        wst = {"next_dma": 0, "cur": 0}
        total_chunks = NCH * NT

        def ws_get(kind, lag=1):
            j = wst["cur"]
            wst["cur"] += 1
            assert chunks[j % NCH][0] == kind, (chunks[j % NCH], kind)
            upto = min(total_chunks - 1, j + NSLOT - lag)
            while wst["next_dma"] <= upto:
                m = wst["next_dma"]
                wst["next_dma"] += 1
                k = m % NCH
                sl = slots[m % NSLOT]
                if chunks[k][0] == "gla_lr":
                    R.op("sp", lambda e, sl=sl, k=k: e.dma_start(out=sl.ap[:, 0:1024], in_=ws_d[k, :, 0:1024]),
                         writes=sl.bufs, dma_sem=slot_sems[m % NSLOT], dur=600, avail=4000)
                else:
                    R.op("sp", lambda e, sl=sl, k=k: e.dma_start(out=sl.ap, in_=ws_d[k]),
                         writes=sl.bufs, dma_sem=slot_sems[m % NSLOT], dur=600, avail=5500)
            return slots[j % NSLOT]

        s_c = R.new_dma_sem("ld_c")
        s_h = R.new_dma_sem("ld_h")
        s_o = R.new_dma_sem("st_o")
        R.op("sp", lambda e: e.dma_start(out=CS.ap, in_=cs_d), writes=CS.bufs, dma_sem=s_c, dur=600, avail=4000)
        onesf = tile("onesf", [128, 128])
        for ot, val in ((ones1024, 1.0 / 1024), (ones128, 1.0 / 128), (ones256, 1.0 / 256)):
            MEMSET("pool", onesf, val)
            CP("act", ot, onesf)
        for h in range(8):
            MEMSET("pool", S32hg[h], 0.0)
            MEMSET("pool", Sbfhg[h], 0.0)
        for h in range(4):
            MEMSET("pool", S32gl[h], 0.0)
            MEMSET("pool", Sbfgl[h], 0.0)
        tails_all = V(tails_t[:], [tails[l][ct].bufs[0] for l in range(2) for ct in range(44)])
        MEMSET("pool", tails_all, 0.0)
        CP("act", ident_bf, cst("ident", 0, 128))
        CP("act", W2r, cst("w2", 0, 512))
        ACT(lbe, cst("lbl", 0, 24), AF.Exp)
        TTo("pool", lb, lbe[:, 0:8], lbe[:, 8:16], ALU.add)
        TTo("pool", lb, lb, lbe[:, 16:24], ALU.add)
        RECIP("dve", lb, lb, exact=True)
        TTo("dve", lb, lb, lbe[:, 0:8], ALU.mult)
        ACT(clb, lb, AF.Ln, scale=-1.0, bias=cst("one"))
        TS("pool", nbgk, cst("bgk", 0, 4), -1.0, ALU.mult)
        one = cst("one")
        eps = cst("eps")
        cmask = cst("cmask", 0, 128)
        smask = cst("smask", 0, 512)

        def rmsnorm(wname, woff, dst):
            for dt in range(8):
                ACT(yT[dt], hT[dt], AF.Square)
            pb = pbank()
            for dt in range(8):
                mm(pb, ones1024, yT[dt], start=(dt == 0), stop=(dt == 7))
            ACT(lnv, pb, AF.Ln, bias=eps)
            ACT(rstd, lnv, AF.Exp, scale=-0.5)
            for dt in range(8):
                if dt % 2 == 0:
                    STT("dve", dst[dt], hT[dt], cst(wname, woff + dt), rstd, ALU.mult, ALU.mult)
                else:
                    ACT(dst[dt], hT[dt], AF.Identity, scale=cst(wname, woff + dt))
                    TTo("pool", dst[dt], dst[dt], rstd, ALU.mult)

        def out_proj(kind):
            so = [ws_get(kind, lag=i + 1) for i in range(4)]
            for ft in range(8):
                pw = pbank()
                for h in range(8):
                    w = so[h // 2].rearrange("p (a f) -> p a f", a=2)[:, h % 2, ft * 128:(ft + 1) * 128]
                    mm(pw, w, oT[h], start=(h == 0), stop=(h == 7))
                TTo("dve", hT[ft], hT[ft], pw, ALU.add)

        def recurrence(tm, a_cols, n_vt, Sbf, S32, vcol0):
            kT, kkT, qT, kktok, vtok, ebl = tm["kT"], tm["kkT"], tm["qT"], tm["kktok"], tm["vtok"], tm["ebl"]
            po = [pbank() for _ in range(n_vt)]
            nv = 128 * n_vt
            for blk in range(4):
                cols = slice(blk * 128, (blk + 1) * 128)
                psc = psc_region()
                mm(psc, kT[:, cols], qT[:, cols])
                scT = tm["scT"][blk % 2]
                TTo("dve", scT, psc, cmask, ALU.mult)
                for vt in range(n_vt):
                    mm(po[vt][:, cols], vtok[:, blk, vcol0 + vt * 128:vcol0 + (vt + 1) * 128], scT,
                       start=True, stop=False, signal=False)
                for ci in range(2):
                    c = blk * 2 + ci
                    ccols = slice(c * 64, (c + 1) * 64)
                    rows = slice(ci * 64, (ci + 1) * 64)
                    for vt in range(n_vt):
                        mm(po[vt][:, ccols], Sbf[:, vt * 128:(vt + 1) * 128], qT[:, ccols],
                           start=False, stop=True, signal=(vt == n_vt - 1))
                    pU = pU_region(nv)
                    mm(pU, kktok[rows, blk, :], vtok[rows, blk, vcol0:vcol0 + nv])
                    STT("dve", S32, S32, ebl[:, c:c + 1], pU, ALU.mult, ALU.add)
                    CP("act", Sbf, S32)
            return po

        def kk_transposes(tm):
            ptr = ptrans()
            for blk in range(4):
                tr(ptr[:, blk * 128:(blk + 1) * 128], tm["kkT"][:, blk * 128:(blk + 1) * 128])
            CP("act", tm["kktok"].rearrange("p a b -> p (a b)"), ptr)

        def silu_gate(dst, pg):
            ACT(dst, pg, AF.Exp, scale=-1.0)
            ACT(dst, dst, AF.Ln, bias=one)
            ACT(dst, dst, AF.Exp, scale=-1.0)
            TTo("dve", dst, pg, dst, ALU.mult)

        def mixer_hg():
            rmsnorm("nm", 0, yT)
            for hp in range(4):
                sq = ws_get("hg_in", 1).rearrange("p (dt f) -> p dt f", dt=8)
                sf = ws_get("hg_in", 2).rearrange("p (dt f) -> p dt f", dt=8)
                si = ws_get("hg_in", 3).rearrange("p (dt f) -> p dt f", dt=8)
                sg = ws_get("hg_in", 4).rearrange("p (dt f) -> p dt f", dt=8)
                tmv = TM[hp % 2]
                for blk in range(4):
                    pv = pbank()[:, 0:256]
                    for dt in range(8):
                        mm(pv, yT[dt][:, blk * 128:(blk + 1) * 128], si[:, dt, :], start=(dt == 0), stop=(dt == 7))
                    CP("act", tmv["vtok"][:, blk, :], pv)
                for a in range(2):
                    h = 2 * hp + a
                    tm = dict(TM[h % 2])
                    tm["vtok"] = tmv["vtok"]
                    A_, B_, C_, D_ = tm["A"], tm["B"], tm["C"], tm["D"]
                    fc = slice(a * 128, (a + 1) * 128)
                    pf = pbank()
                    for dt in range(8):
                        mm(pf, sf[:, dt, fc], yT[dt], start=(dt == 0), stop=(dt == 7))
                    pq = pbank()
                    for dt in range(8):
                        mm(pq, sq[:, dt, fc], yT[dt], start=(dt == 0), stop=(dt == 7))
                    pg = pbank()
                    for dt in range(8):
                        mm(pg, sg[:, dt, fc], yT[dt], start=(dt == 0), stop=(dt == 7))
                    ACT(A_, pf, AF.Exp, scale=-1.0)
                    ACT(B_, A_, AF.Ln, bias=one)
                    ACT(C_, A_, AF.Ln, scale=lb[:, h:h + 1], bias=one)
                    TTo("pool", C_, C_, B_, ALU.subtract)
                    SCAN(D_, smask, C_)
                    TTo("dve", A_, pf, B_, ALU.add)
                    TTo("pool", A_, A_, D_, ALU.add)
                    ACT(tm["kT"], A_, AF.Exp, scale=-1.0, bias=clb[:, h:h + 1])
                    ACT(tm["ebl"], D_[:, 63::64], AF.Exp)
                    TTo("pool", tm["kkT"].rearrange("p (c k) -> p c k", k=64),
                        tm["kT"].rearrange("p (c k) -> p c k", k=64), tm["ebl"].bcast3(64), ALU.mult)
                    kk_transposes(tm)
                    ACT(C_, pq, AF.Exp, scale=-1.0)
                    ACT(C_, C_, AF.Ln, bias=one)
                    TTo("pool", C_, D_, C_, ALU.subtract)
                    ACT(C_, C_, AF.Exp, bias=cst("lnc"))
                    TTo("dve", tm["qT"], pq, C_, ALU.mult)
                    silu_gate(A_, pg)
                    po = recurrence(tm, None, 1, Sbfhg[h], S32hg[h], a * 128)[0]
                    ACT(tm["OSQ0"], po, AF.Square)
                    pss = pbank()
                    mm(pss, ones128, tm["OSQ0"])
                    ACT(C_, pss, AF.Ln, bias=eps)
                    ACT(C_, C_, AF.Exp, scale=-0.5)
                    STT("dve", D_, po, cst("hgnw"), C_, ALU.mult, ALU.mult)
                    TTo("pool", oT[h], D_, A_, ALU.mult)
            out_proj("hg_out")

        def mixer_gla():
            rmsnorm("nm", 8, yT)
            slr = ws_get("gla_lr", 1)[:, 0:1024].rearrange("p (dt f) -> p dt f", dt=8)
            pG = pbank()
            for dt in range(8):
                mm(pG, slr[:, dt, :], yT[dt], start=(dt == 0), stop=(dt == 7))
            CP("act", G_sb, pG)
            for hd in range(4):
                sqk = ws_get("gla_qk", 1).rearrange("p (dt two f) -> p dt two f", dt=8, two=2)
                sv = ws_get("gla_v", 2).rearrange("p (dt f) -> p dt f", dt=8)
                sg = ws_get("gla_g", 3).rearrange("p (dt f) -> p dt f", dt=8)
                tm = TM[hd % 2]
                A_, B_, C_, D_ = tm["A"], tm["B"], tm["C"], tm["D"]
                SG = [tm["E"], tm["SG1"]]
                for vt in range(2):
                    pg = pbank()
                    for dt in range(8):
                        mm(pg, sg[:, dt, vt * 128:(vt + 1) * 128], yT[dt], start=(dt == 0), stop=(dt == 7))
                    silu_gate(SG[vt], pg)
                for blk in range(4):
                    pv = pbank()[:, 0:256]
                    for dt in range(8):
                        mm(pv, yT[dt][:, blk * 128:(blk + 1) * 128], sv[:, dt, :], start=(dt == 0), stop=(dt == 7))
                    CP("act", tm["vtok"][:, blk, :], pv)
                pgk = pbank()
                mm(pgk, W2r[:, hd * 128:(hd + 1) * 128], G_sb)
                ACT(A_, pgk, AF.Exp, scale=-1.0, bias=nbgk[:, hd:hd + 1])
                ACT(A_, A_, AF.Ln, bias=one)
                SCAN(D_, smask, A_)
                ACT(B_, D_, AF.Exp, scale=-1.0 / 16)
                ACT(C_, D_, AF.Exp, scale=1.0 / 16)
                ACT(tm["ebl"], D_[:, 63::64], AF.Exp, scale=-1.0 / 16)
                pq = pbank()
                for dt in range(8):
                    mm(pq, sqk[:, dt, 0, :], yT[dt], start=(dt == 0), stop=(dt == 7))
                pk = pbank()
                for dt in range(8):
                    mm(pk, sqk[:, dt, 1, :], yT[dt], start=(dt == 0), stop=(dt == 7))
                STT("dve", tm["qT"], pq, 128 ** -0.5, B_, ALU.mult, ALU.mult)
                TTo("dve", tm["kT"], pk, C_, ALU.mult)
                TTo("pool", tm["kkT"].rearrange("p (c k) -> p c k", k=64),
                    tm["kT"].rearrange("p (c k) -> p c k", k=64), tm["ebl"].bcast3(64), ALU.mult)
                kk_transposes(tm)
                po = recurrence(tm, None, 2, Sbfgl[hd], S32gl[hd], 0)
                OSQ = [tm["OSQ0"], tm["OSQ1"]]
                for vt in range(2):
                    ACT(OSQ[vt], po[vt], AF.Square)
                pss = pbank()
                mm(pss, ones256, OSQ[0], start=True, stop=False)
                mm(pss, ones256, OSQ[1], start=False, stop=True)
                ACT(C_, pss, AF.Ln, bias=eps)
                ACT(C_, C_, AF.Exp, scale=-0.5)
                for vt in range(2):
                    STT("dve", D_, po[vt], cst("glanw", vt), C_, ALU.mult, ALU.mult)
                    TTo("pool", oT[hd * 2 + vt], D_, SG[vt], ALU.mult)
            out_proj("gla_out")

        def ffn(l):
            rmsnorm("nf", 8 * l, yT)
            pstate["nbig"] = 8
            for g in range(2):
                for ci in range(11):
                    c = g * 11 + ci
                    su = ws_get("up", 1).rearrange("p (dt two f) -> p dt two f", dt=8, two=2)
                    tm = TM[ci % 2]
                    pa = pbank()
                    for dt in range(8):
                        mm(pa, su[:, dt, 0, :], yT[dt], start=(dt == 0), stop=(dt == 7))
                    pg = pbank()
                    for dt in range(8):
                        mm(pg, su[:, dt, 1, :], yT[dt], start=(dt == 0), stop=(dt == 7))
                    for (ps, ct, xs, u, e1, e2) in ((pa, c, tm["XA"], tm["UA"], "dve", "dve"),
                                                     (pg, NCT + c, tm["XG"], tm["UG"], "dve", "dve")):
                        cw = lambda j, ct=ct: cst("cw", (l * 3 + j) * 44 + ct)
                        CP("act", xs[:, 2:TT + 2], ps)
                        CP("pool", xs[:, 0:2], tails[l][ct])
                        if ct < NCT:
                            ACT(u, xs[:, 0:TT], AF.Identity, scale=cw(0), bias=cst("cb", l * 44 + ct))
                        else:
                            TS("dve", u, xs[:, 0:TT], cw(0), ALU.mult, cst("cb", l * 44 + ct), ALU.add)
                        STT(e1, u, xs[:, 1:TT + 1], cw(1), u, ALU.mult, ALU.add)
                        STT(e2, u, xs[:, 2:TT + 2], cw(2), u, ALU.mult, ALU.add)
                        CP("pool", tails[l][ct], xs[:, TT:TT + 2])
                    E = tm["E"]
                    ACT(E, tm["UG"], AF.Silu)
                    TTo("pool", aT[ci], E, tm["UA"], ALU.mult)
                for half in range(2):
                    accs = [pbank() for _ in range(4)]
                    for part in (range(0, 4), range(4, 8), range(8, 11)):
                        sd = ws_get("down", 1).rearrange("p (i f) -> p i f", i=4)
                        for i, ci in enumerate(part):
                            for j in range(4):
                                mm(accs[j], sd[:, i, j * 128:(j + 1) * 128], aT[ci], start=(ci == 0), stop=(ci == 10))
                    for j in range(4):
                        ft = half * 4 + j
                        TTo("dve", hT[ft], hT[ft], accs[j], ALU.add)
            pstate["nbig"] = 6
            pstate["big"] = 0

        out_toks = []
        for tI in range(NT):
            t0 = tI * TT
            R.op("sp", lambda e, tI=tI: e.dma_start(out=hT_t[:], in_=xT_d[tI].rearrange("p (dt t) -> p dt t", dt=8)),
                 writes=hT_b, dma_sem=s_h, dur=600, avail=9000)
            for l in layers:
                if l == 0:
                    mixer_hg()
                else:
                    mixer_gla()
                ffn(l)
            if final_norm:
                rmsnorm("nfin", 0, hT)
            tok = R.op("sp", lambda e, tI=tI: e.dma_start(out=out_d[tI].rearrange("p (dt t) -> p dt t", dt=8), in_=hT_t[:]),
                       reads=hT_b, dma_sem=s_o, dur=600, avail=6000)
        R.final_wait("sp", [(s_o, 16 * NT)])
        assert wst["cur"] == total_chunks
        R.emit(nc)
    return nc


_PROGS = {}


def _prog(NT, layers, final_norm):
    key = (NT, tuple(layers), final_norm)
    if key not in _PROGS:
        _PROGS[key] = build(NT, layers, final_norm)
    return _PROGS[key]


FUSED = True


def to_tiles(xb):
    nt = xb.shape[0] // TT
    return np.ascontiguousarray(xb.reshape(nt, TT, 8, 128).transpose(0, 3, 2, 1)).reshape(nt, 128, 8 * TT)


def from_tiles(o):
    nt = o.shape[0]
    return np.ascontiguousarray(o.reshape(nt, 128, 8, TT).transpose(0, 3, 2, 1)).reshape(nt * TT, D)


def kernel(**inputs):
    inp = {k: np.asarray(v) for k, v in inputs.items()}
    x = inp["x"].astype(np.float32, copy=False)
    B = x.shape[0]
    consts = pack_consts(inp)
    xT = [to_tiles(x[b]) for b in range(B)]
    if FUSED:
        stages = [((0, 1), True)]
    else:
        stages = [((0,), False), ((1,), True)]
    cur = xT
    for layers, fin in stages:
        ws = pack_wstream(inp, layers)
        nc = _prog(T // TT, layers, fin)
        in_maps = [{"xT": cur[b], "wstream": ws, "consts": consts} for b in range(B)]
        res = run_bass_kernel_spmd(nc, in_maps, core_ids=list(range(B)))
        cur = [np.asarray(res.results[b]["outT"]) for b in range(B)]
    out = np.stack([from_tiles(cur[b]) for b in range(B)], axis=0)
    return out.astype(np.float32, copy=False)
```

```python
import contextlib
import numpy as np
import concourse.bass as bass
import concourse.mybir as mybir
from concourse.bass_utils import run_bass_kernel_spmd

F32 = mybir.dt.float32
F32R = mybir.dt.float32r
BF16 = mybir.dt.bfloat16
AF = mybir.ActivationFunctionType
ALU = mybir.AluOpType

D = 1024
T = 4096
TT = 512
DFF = 2816
NCT = 22
NSLOT = 7
ENGS = ("pe", "act", "dve", "pool", "sp")


class Buf:
    __slots__ = ("name", "last_w", "readers")

    def __init__(self, name=""):
        self.name = name
        self.last_w = None
        self.readers = {}


class Rec:
    def __init__(self, same_engine_sync=("act", "dve", "pool")):
        self.ops = {e: [] for e in ENGS}
        self.cnt = {e: 0 for e in ENGS}
        self.waited = {}
        self.same_sync = set(same_engine_sync)
        self.dma_sems = []
        self.prog = []
        self.finals = []

    def new_dma_sem(self, name):
        self.cnt[name] = 0
        self.dma_sems.append(name)
        return name

    def op(self, eng, issue, reads=(), writes=(), signal=True, dma_sem=None, force=False, dur=500.0, avail=None):
        self.prog.append(dict(eng=eng, issue=issue, reads=list(reads), writes=list(writes), signal=signal,
                              dma_sem=dma_sem, dur=float(dur), avail=float(avail if avail is not None else dur)))
        return None

    def _op(self, eng, issue, reads=(), writes=(), signal=True, dma_sem=None):
        need = {}

        def add(tok):
            if tok is None:
                return
            k, v = tok
            if need.get(k, 0) < v:
                need[k] = v

        for b in reads:
            add(b.last_w)
        for b in writes:
            add(b.last_w)
            for k, v in b.readers.items():
                add((k, v))
        waits = []
        for k, v in need.items():
            if k == eng and eng not in self.same_sync:
                continue
            if k == eng and v > self.cnt[eng]:
                continue
            if self.waited.get((eng, k), 0) >= v:
                continue
            self.waited[(eng, k)] = v
            waits.append((k, v))
        if dma_sem is not None:
            self.cnt[dma_sem] += 16
            tok = (dma_sem, self.cnt[dma_sem])
            inc = (dma_sem, 16)
        elif signal:
            self.cnt[eng] += 1
            tok = (eng, self.cnt[eng])
            inc = (eng, 1)
        else:
            tok = (eng, self.cnt[eng] + 1)
            inc = None
        for b in reads:
            if b.readers.get(tok[0], 0) < tok[1]:
                b.readers[tok[0]] = tok[1]
        for b in writes:
            b.last_w = tok
            b.readers = {}
        self.ops[eng].append((waits, issue, inc))
        return tok

    def final_wait(self, eng, toks):
        self.finals.append((eng, [(k, v) for (k, v) in toks]))

    def schedule(self, window=96, lat=350.0):
        prog = self.prog
        units = []
        open_pe = None
        for o in prog:
            if o["eng"] == "pe":
                if open_pe is None:
                    open_pe = dict(eng="pe", ops=[], deps=set(), idx=len(units))
                    units.append(open_pe)
                open_pe["ops"].append(o)
                o["unit"] = open_pe["idx"]
                if o["signal"]:
                    open_pe = None
            else:
                u = dict(eng=o["eng"], ops=[o], deps=set(), idx=len(units))
                units.append(u)
                o["unit"] = u["idx"]
        assert open_pe is None
        lastw = {}
        rdrs = {}
        for o in prog:
            u = o["unit"]
            d = units[u]["deps"]
            for b in o["reads"]:
                w = lastw.get(id(b))
                if w is not None and w != u:
                    d.add(w)
            for b in o["writes"]:
                w = lastw.get(id(b))
                if w is not None and w != u:
                    d.add(w)
                for r in rdrs.get(id(b), ()):
                    if r != u:
                        d.add(r)
            for b in o["reads"]:
                rdrs.setdefault(id(b), set()).add(u)
            for b in o["writes"]:
                lastw[id(b)] = u
                rdrs[id(b)] = set()
        n = len(units)
        succ = [[] for _ in range(n)]
        ndep = [0] * n
        for u in units:
            ndep[u["idx"]] = len(u["deps"])
            for d in u["deps"]:
                succ[d].append(u["idx"])
        per_eng = {e: [u["idx"] for u in units if u["eng"] == e] for e in ENGS}
        pos = {e: 0 for e in ENGS}
        done = [False] * n
        fin = [0.0] * n
        free = {e: 0.0 for e in ENGS}
        order = []
        remaining = n
        while remaining:
            best = None
            for e in ENGS:
                lst = per_eng[e]
                p = pos[e]
                while p < len(lst) and done[lst[p]]:
                    p += 1
                pos[e] = p
                cnt = 0
                q = p
                win = 1 if e == "pe" else window
                while q < len(lst) and cnt < win:
                    ui = lst[q]
                    q += 1
                    if done[ui]:
                        continue
                    cnt += 1
                    if ndep[ui]:
                        continue
                    u = units[ui]
                    st = free[e]
                    for d in u["deps"]:
                        t = fin[d] + lat
                        if t > st:
                            st = t
                    if best is None or st < best[0] or (st == best[0] and ui < best[1]):
                        best = (st, ui)
                    if st <= free[e]:
                        break
            st, ui = best
            u = units[ui]
            e = u["eng"]
            t = st
            for o in u["ops"]:
                t += o["dur"]
            free[e] = t
            fin[ui] = st + sum(o["dur"] for o in u["ops"][:-1]) + u["ops"][-1]["avail"]
            done[ui] = True
            remaining -= 1
            for sidx in succ[ui]:
                ndep[sidx] -= 1
            order.append(ui)
        self.est_ns = max(fin)
        for ui in order:
            for o in units[ui]["ops"]:
                self._op(o["eng"], o["issue"], o["reads"], o["writes"], o["signal"], o["dma_sem"])

    def emit(self, nc):
        import os as _os
        self.schedule(window=int(_os.environ.get("KWINDOW", "96")))
        for eng, toks in self.finals:
            self.ops[eng].append((toks, None, None))
        names = [e for e in ENGS if e != "sp"] + self.dma_sems
        with contextlib.ExitStack() as st:
            sems = {n: st.enter_context(nc.semaphore("s_" + n)) for n in names}
            block = st.enter_context(nc.Block())

            def replay(name, e):
                for waits, issue, inc in self.ops[name]:
                    for k, v in waits:
                        e.wait_ge(sems[k], v)
                    if issue is None:
                        continue
                    ins = issue(e)
                    if inc is not None:
                        ins.then_inc(sems[inc[0]], inc[1])

            @block.tensor
            def _(e):
                replay("pe", e)

            @block.scalar
            def _(e):
                replay("act", e)

            @block.vector
            def _(e):
                replay("dve", e)

            @block.gpsimd
            def _(e):
                replay("pool", e)

            @block.sync
            def _(e):
                replay("sp", e)


class V:
    __slots__ = ("ap", "bufs")

    def __init__(self, ap, bufs):
        self.ap = ap
        self.bufs = list(bufs)

    def __getitem__(self, k):
        return V(self.ap[k], self.bufs)

    def bitcast(self, dt):
        return V(self.ap.bitcast(dt), self.bufs)

    def rearrange(self, s, **kw):
        return V(self.ap.rearrange(s, **kw), self.bufs)

    def bcast3(self, n):
        return V(self.ap.unsqueeze(2).broadcast_to(list(self.ap.shape) + [n]), self.bufs)


def plan(layers):
    ch = []
    for l in layers:
        if l == 0:
            for hp in range(4):
                for k in ("q", "f", "i", "g"):
                    ch.append(("hg_in", k, hp))
            for hp in range(4):
                ch.append(("hg_out", hp))
        else:
            ch.append(("gla_lr",))
            for hd in range(4):
                ch.append(("gla_qk", hd))
                ch.append(("gla_v", hd))
                ch.append(("gla_g", hd))
            for hd in range(4):
                ch.append(("gla_out", hd))
        for g in range(2):
            cs = list(range(g * 11, (g + 1) * 11))
            for c in cs:
                ch.append(("up", l, c))
            for half in range(2):
                for part in (cs[0:4], cs[4:8], cs[8:11]):
                    ch.append(("down", l, half, tuple(part)))
    return ch


def _cols(W, cols):
    sub = W[:, cols].reshape(8, 128, len(cols)).transpose(1, 0, 2)
    return sub.reshape(128, -1)


def pack_wstream(inp, layers):
    chunks = plan(layers)
    out = np.zeros((len(chunks), 128, 2048), np.float32)
    hg_in = inp["hg_w_in"][0]
    hg_out = inp["hg_w_out"][0]
    gl_in = inp["gla_w_in"][0]
    gl_out = inp["gla_w_out"][0]
    for i, c in enumerate(chunks):
        kind = c[0]
        if kind == "hg_in":
            base = {"q": 0, "f": 1024, "i": 2048, "g": 3072}[c[1]] + c[2] * 256
            out[i] = _cols(hg_in, np.arange(base, base + 256))
        elif kind == "hg_out":
            r0 = c[1] * 256
            out[i] = hg_out[r0:r0 + 256].reshape(2, 128, 1024).transpose(1, 0, 2).reshape(128, 2048)
        elif kind == "gla_lr":
            out[i, :, :1024] = _cols(gl_in, np.arange(2960, 3088))
        elif kind == "gla_qk":
            hd = c[1]
            cols = np.concatenate([np.arange(hd * 128, hd * 128 + 128), np.arange(512 + hd * 128, 512 + hd * 128 + 128)])
            out[i] = _cols(gl_in, cols)
        elif kind == "gla_v":
            out[i] = _cols(gl_in, np.arange(1024 + c[1] * 256, 1024 + c[1] * 256 + 256))
        elif kind == "gla_g":
            out[i] = _cols(gl_in, np.arange(2048 + c[1] * 256, 2048 + c[1] * 256 + 256))
        elif kind == "gla_out":
            r0 = c[1] * 256
            out[i] = gl_out[r0:r0 + 256].reshape(2, 128, 1024).transpose(1, 0, 2).reshape(128, 2048)
        elif kind == "up":
            l, ct = c[1], c[2]
            cols = np.concatenate([np.arange(ct * 128, ct * 128 + 128), np.arange(DFF + ct * 128, DFF + ct * 128 + 128)])
            out[i] = _cols(inp["ffn_w_up"][l], cols)
        elif kind == "down":
            l, half, part = c[1], c[2], c[3]
            wd = inp["ffn_w_down"][l]
            for j, ct in enumerate(part):
                out[i, :, j * 512:(j + 1) * 512] = wd[ct * 128:(ct + 1) * 128, half * 512:(half + 1) * 512]
    return out


_CO = {}
_off = 0
for _n, _w in (("nm", 16), ("nf", 16), ("nfin", 8), ("lbl", 24), ("hgnw", 1), ("bgk", 4), ("glanw", 2),
               ("cw", 264), ("cb", 88), ("eps", 1), ("one", 1), ("lnc", 1), ("ident", 128), ("cmask", 128),
               ("smask", 512), ("w2", 512)):
    _CO[_n] = _off
    _off += _w
NCONST = _off


def pack_consts(inp):
    c = np.zeros((128, NCONST), np.float32)

    def put(name, arr):
        arr = np.asarray(arr, np.float32)
        c[:, _CO[name]:_CO[name] + arr.shape[1]] = arr

    pd = lambda v: np.asarray(v, np.float32).reshape(-1, 128).T
    put("nm", np.concatenate([pd(inp["norm_mixer_w"][l]) for l in range(2)], axis=1))
    put("nf", np.concatenate([pd(inp["norm_ffn_w"][l]) for l in range(2)], axis=1))
    put("nfin", pd(inp["norm_final_w"]))
    put("lbl", np.concatenate([pd(inp["lb_logits"][k]) for k in range(3)], axis=1))
    put("hgnw", pd(inp["hg_norm_w"][0]))
    put("bgk", pd(inp["gla_b_gk_up"][0]))
    put("glanw", pd(inp["gla_norm_w"][0]))
    cw = np.asarray(inp["ffn_conv_w"], np.float32)
    put("cw", np.concatenate([pd(cw[l, j]) for l in range(2) for j in range(3)], axis=1))
    cb = np.asarray(inp["ffn_conv_b"], np.float32)
    put("cb", np.concatenate([pd(cb[l]) for l in range(2)], axis=1))
    put("eps", np.full((128, 1), 1e-6, np.float32))
    put("one", np.ones((128, 1), np.float32))
    put("lnc", np.full((128, 1), np.log(128.0 ** -0.5), np.float32))
    put("ident", np.eye(128, dtype=np.float32))
    s = np.arange(128)[:, None]
    t = np.arange(128)[None, :]
    put("cmask", ((s // 64 == t // 64) & (s <= t)).astype(np.float32))
    sm = np.ones((128, 512), np.float32)
    sm[:, ::64] = 0.0
    put("smask", sm)
    w2 = np.zeros((128, 512), np.float32)
    w2[112:128, :] = np.asarray(inp["gla_w_gk_up"][0], np.float32)
    put("w2", w2)
    return c


def build(NT, layers=(0, 1), final_norm=True, same_sync=("act", "dve", "pool")):
    nc = bass.Bass("TRN2", target_bir_lowering=False)
    nc.dge_precook = False
    chunks = plan(layers)
    NCH = len(chunks)
    xT_d = nc.dram_tensor("xT", [NT, 128, 8 * TT], F32, kind="ExternalInput").ap()
    ws_d = nc.dram_tensor("wstream", [NCH, 128, 2048], F32R, kind="ExternalInput").ap()
    cs_d = nc.dram_tensor("consts", [128, NCONST], F32, kind="ExternalInput").ap()
    out_d = nc.dram_tensor("outT", [NT, 128, 8 * TT], F32, kind="ExternalOutput").ap()

    R = Rec(same_engine_sync=same_sync)
    with contextlib.ExitStack() as st:
        def sbt(name, shape, dt=F32, nbuf=1):
            t = st.enter_context(nc.sbuf_tensor(name, shape, dt))
            return t, [Buf(f"{name}{i}") for i in range(nbuf)]

        def tile(name, shape, dt=F32):
            t, b = sbt(name, shape, dt)
            return V(t[:], b)

        hT_t, hT_b = sbt("hT", [128, 8, TT], F32, 8)
        yT_t, yT_b = sbt("yT", [128, 8, TT], F32R, 8)
        oT_t, oT_b = sbt("oT", [128, 8, TT], F32R, 8)
        aT_t, aT_b = sbt("aT", [128, 11, TT], F32R, 11)
        hT = [V(hT_t[:, i, :], [hT_b[i]]) for i in range(8)]
        yT = [V(yT_t[:, i, :], [yT_b[i]]) for i in range(8)]
        oT = [V(oT_t[:, i, :], [oT_b[i]]) for i in range(8)]
        aT = [V(aT_t[:, i, :], [aT_b[i]]) for i in range(11)]
        hT_all = V(hT_t[:], hT_b)
        CS = tile("consts_sb", [128, NCONST])
        slots = [tile(f"slot{i}", [128, 2048], F32R) for i in range(NSLOT)]
        slot_sems = [R.new_dma_sem(f"ws{i}") for i in range(NSLOT)]

        def cst(name, j=0, w=1):
            o = _CO[name] + j
            return CS[:, o:o + w]

        ones1024 = tile("ones1024", [128, 128], F32R)
        ones128 = tile("ones128", [128, 128], F32R)
        ones256 = tile("ones256", [128, 128], F32R)
        ident_bf = tile("ident_bf", [128, 128], BF16)
        lb = tile("lb", [128, 8])
        clb = tile("clb", [128, 8])
        lbe = tile("lbe", [128, 24])
        nbgk = tile("nbgk", [128, 4])
        W2r = tile("W2r", [128, 512], F32R)
        G_sb = tile("G_sb", [128, TT], F32R)
        lnv = tile("lnv", [128, TT])
        rstd = tile("rstd", [128, TT])
        S32hg = [tile(f"S32hg{h}", [128, 128]) for h in range(8)]
        Sbfhg = [tile(f"Sbfhg{h}", [128, 128], BF16) for h in range(8)]
        S32gl = [tile(f"S32gl{h}", [128, 256]) for h in range(4)]
        Sbfgl = [tile(f"Sbfgl{h}", [128, 256], BF16) for h in range(4)]
        tails_t = st.enter_context(nc.sbuf_tensor("tails", [128, 2, 44, 2], F32))
        tails = [[V(tails_t[:, l, ct, :], [Buf(f"tail{l}_{ct}")]) for ct in range(44)] for l in range(2)]
        TM = []
        for s in range(2):
            d = {}
            for n in ("C", "D", "SG1", "E"):
                d[n] = tile(f"t{n}{s}", [128, TT])
            d["XA"] = tile(f"tXA{s}", [128, TT + 2])
            d["XG"] = tile(f"tXG{s}", [128, TT + 2])
            d["A"] = d["XA"][:, 0:TT]
            d["B"] = d["XG"][:, 0:TT]
            d["UA"] = d["C"]
            d["UG"] = d["D"]
            d["OSQ0"] = tile(f"tOSQ0{s}", [128, TT], F32R)
            d["OSQ1"] = tile(f"tOSQ1{s}", [128, TT], F32R)
            for n in ("kT", "kkT", "qT"):
                d[n] = tile(f"t{n}{s}", [128, TT], BF16)
            d["kktok"] = tile(f"tkktok{s}", [128, 4, 128], BF16)
            d["vtok"] = tile(f"tvtok{s}", [128, 4, 256], BF16)
            d["scT"] = [tile(f"tscT{s}{i}", [128, 128], BF16) for i in range(2)]
            d["ebl"] = tile(f"tebl{s}", [128, 8])
            TM.append(d)
        banks = []
        for i in range(8):
            t = st.enter_context(nc.psum_tensor(f"bank{i}", [128, TT], F32))
            banks.append((t, [Buf(f"bk{i}a"), Buf(f"bk{i}b")]))
        pstate = {"big": 0, "sc": 0, "tr": 0, "nbig": 6}

        def pbank():
            n = pstate["nbig"]
            i = pstate["big"] % n
            pstate["big"] += 1
            t, b = banks[i]
            return V(t[:], b)

        def psc_region():
            h = pstate["sc"] % 2
            pstate["sc"] += 1
            t, b = banks[6]
            return V(t[:, h * 128:(h + 1) * 128], [b[0]])

        def pU_region(nv):
            t, b = banks[6]
            return V(t[:, 256:256 + nv], [b[1]])

        def ptrans():
            h = pstate["tr"] % 2
            pstate["tr"] += 1
            t, b = banks[7]
            return V(t[:, h * 256:(h + 1) * 256].bitcast(BF16), [b[h]])

        def bufs_of(*vs):
            r = []
            for v in vs:
                if isinstance(v, V):
                    r += v.bufs
            return r

        def apof(v):
            return v.ap if isinstance(v, V) else v

        def mm(out, lhsT, rhs, start=True, stop=True, signal=None):
            sig = stop if signal is None else signal
            R.op("pe", lambda e: e.matmul(out.ap, lhsT.ap, rhs.ap, start=start, stop=stop),
                 reads=lhsT.bufs + rhs.bufs, writes=out.bufs, signal=sig,
                 dur=max(64, rhs.ap.shape[-1]) / 2.2 + 20)

        def tr(out, in_):
            R.op("pe", lambda e: e.transpose(out.ap, in_.ap, ident_bf.ap),
                 reads=in_.bufs + ident_bf.bufs, writes=out.bufs, signal=True, dur=90)

        def nfree(v):
            n = 1
            for d in v.ap.shape[1:]:
                n *= d
            return n

        def edur(eng, v, mult=1.0):
            n = nfree(v) * mult
            if eng == "act":
                return 220 + 0.85 * n
            if eng == "dve":
                return 120 + 1.0 * n
            return 150 + 2.3 * n

        def ACT(out, in_, func, bias=None, scale=None):
            kw = {}
            if bias is not None:
                kw["bias"] = apof(bias)
            if scale is not None:
                kw["scale"] = apof(scale)
            R.op("act", lambda e: e.activation(out=out.ap, in_=in_.ap, func=func, **kw),
                 reads=bufs_of(in_, bias, scale), writes=out.bufs, dur=edur("act", out))

        def TTo(eng, out, in0, in1, op):
            R.op(eng, lambda e: e.tensor_tensor(out=out.ap, in0=in0.ap, in1=in1.ap, op=op),
                 reads=bufs_of(in0, in1), writes=out.bufs, dur=edur(eng, out))

        def TS(eng, out, in0, s1, op0, s2=None, op1=None):
            kw = {}
            if op1 is None and eng == "pool":
                if op0 == ALU.add:
                    s2, op1 = 1.0, ALU.mult
                elif op0 == ALU.mult:
                    s2, op1 = 0.0, ALU.add
            if op1 is not None:
                kw["op1"] = op1
            R.op(eng, lambda e: e.tensor_scalar(out=out.ap, in0=in0.ap, scalar1=apof(s1), scalar2=apof(s2), op0=op0, **kw),
                 reads=bufs_of(in0, s1, s2), writes=out.bufs, dur=edur(eng, out))

        def STT(eng, out, in0, scalar, in1, op0, op1):
            R.op(eng, lambda e: e.scalar_tensor_tensor(out=out.ap, in0=in0.ap, scalar=apof(scalar), in1=in1.ap, op0=op0, op1=op1),
                 reads=bufs_of(in0, scalar, in1), writes=out.bufs, dur=edur(eng, out))

        def SCAN(out, d0, d1):
            R.op("dve", lambda e: e.tensor_tensor_scan(out=out.ap, data0=d0.ap, data1=d1.ap, initial=0.0, op0=ALU.mult, op1=ALU.add),
                 reads=bufs_of(d0, d1), writes=out.bufs, dur=edur("dve", out, 2.0))

        def RECIP(eng, out, in_, exact=True):
            R.op(eng, lambda e: e.reciprocal(out=out.ap, in_=in_.ap), reads=in_.bufs, writes=out.bufs, dur=edur(eng, out, 6.0))

        def CP(eng, out, in_):
            if eng == "act":
                R.op("act", lambda e: e.copy(out=out.ap, in_=in_.ap), reads=in_.bufs, writes=out.bufs, dur=edur("act", out))
            else:
                R.op(eng, lambda e: e.tensor_copy(out=out.ap, in_=in_.ap), reads=in_.bufs, writes=out.bufs, dur=edur(eng, out))

        def MEMSET(eng, out, val):
            R.op(eng, lambda e: e.memset(out.ap, val), writes=out.bufs, dur=edur(eng, out, 0.5))

        wst = {"next_dma": 0, "cur": 0}
        total_chunks = NCH * NT

        def ws_get(kind, lag=1):
            j = wst["cur"]
            wst["cur"] += 1
            assert chunks[j % NCH][0] == kind, (chunks[j % NCH], kind)
            upto = min(total_chunks - 1, j + NSLOT - lag)
            while wst["next_dma"] <= upto:
                m = wst["next_dma"]
                wst["next_dma"] += 1
                k = m % NCH
                sl = slots[m % NSLOT]
                if chunks[k][0] == "gla_lr":
                    R.op("sp", lambda e, sl=sl, k=k: e.dma_start(out=sl.ap[:, 0:1024], in_=ws_d[k, :, 0:1024]),
                         writes=sl.bufs, dma_sem=slot_sems[m % NSLOT], dur=600, avail=4000)
                else:
                    R.op("sp", lambda e, sl=sl, k=k: e.dma_start(out=sl.ap, in_=ws_d[k]),
                         writes=sl.bufs, dma_sem=slot_sems[m % NSLOT], dur=600, avail=5500)
            return slots[j % NSLOT]

        s_c = R.new_dma_sem("ld_c")
        s_h = R.new_dma_sem("ld_h")
        s_o = R.new_dma_sem("st_o")
        R.op("sp", lambda e: e.dma_start(out=CS.ap, in_=cs_d), writes=CS.bufs, dma_sem=s_c, dur=600, avail=4000)
        onesf = tile("onesf", [128, 128])
        for ot, val in ((ones1024, 1.0 / 1024), (ones128, 1.0 / 128), (ones256, 1.0 / 256)):
            MEMSET("pool", onesf, val)
            CP("act", ot, onesf)
        for h in range(8):
            MEMSET("pool", S32hg[h], 0.0)
            MEMSET("pool", Sbfhg[h], 0.0)
        for h in range(4):
            MEMSET("pool", S32gl[h], 0.0)
            MEMSET("pool", Sbfgl[h], 0.0)
        tails_all = V(tails_t[:], [tails[l][ct].bufs[0] for l in range(2) for ct in range(44)])
        MEMSET("pool", tails_all, 0.0)
        CP("act", ident_bf, cst("ident", 0, 128))
        CP("act", W2r, cst("w2", 0, 512))
        ACT(lbe, cst("lbl", 0, 24), AF.Exp)
        TTo("pool", lb, lbe[:, 0:8], lbe[:, 8:16], ALU.add)
        TTo("pool", lb, lb, lbe[:, 16:24], ALU.add)
        RECIP("dve", lb, lb, exact=True)
        TTo("dve", lb, lb, lbe[:, 0:8], ALU.mult)
        ACT(clb, lb, AF.Ln, scale=-1.0, bias=cst("one"))
        TS("pool", nbgk, cst("bgk", 0, 4), -1.0, ALU.mult)
        one = cst("one")
        eps = cst("eps")
        cmask = cst("cmask", 0, 128)
        smask = cst("smask", 0, 512)

        def rmsnorm(wname, woff, dst):
            for dt in range(8):
                ACT(yT[dt], hT[dt], AF.Square)
            pb = pbank()
            for dt in range(8):
                mm(pb, ones1024, yT[dt], start=(dt == 0), stop=(dt == 7))
            ACT(lnv, pb, AF.Ln, bias=eps)
            ACT(rstd, lnv, AF.Exp, scale=-0.5)
            for dt in range(8):
                if dt % 2 == 0:
                    STT("dve", dst[dt], hT[dt], cst(wname, woff + dt), rstd, ALU.mult, ALU.mult)
                else:
                    ACT(dst[dt], hT[dt], AF.Identity, scale=cst(wname, woff + dt))
                    TTo("pool", dst[dt], dst[dt], rstd, ALU.mult)

        def out_proj(kind):
            so = [ws_get(kind, lag=i + 1) for i in range(4)]
            for ft in range(8):
                pw = pbank()
                for h in range(8):
                    w = so[h // 2].rearrange("p (a f) -> p a f", a=2)[:, h % 2, ft * 128:(ft + 1) * 128]
                    mm(pw, w, oT[h], start=(h == 0), stop=(h == 7))
                TTo("dve", hT[ft], hT[ft], pw, ALU.add)

        def recurrence(tm, po, n_vt, Sbf, S32, vcol0):
            kT, kkT, qT, kktok, vtok, ebl = tm["kT"], tm["kkT"], tm["qT"], tm["kktok"], tm["vtok"], tm["ebl"]
            nv = 128 * n_vt
            for blk in range(4):
                cols = slice(blk * 128, (blk + 1) * 128)
                psc = psc_region()
                mm(psc, kT[:, cols], qT[:, cols])
                scT = tm["scT"][blk % 2]
                TTo("dve", scT, psc, cmask, ALU.mult)
                for vt in range(n_vt):
                    mm(po[vt][:, cols], vtok[:, blk, vcol0 + vt * 128:vcol0 + (vt + 1) * 128], scT,
                       start=True, stop=False, signal=False)
                for ci in range(2):
                    c = blk * 2 + ci
                    ccols = slice(c * 64, (c + 1) * 64)
                    rows = slice(ci * 64, (ci + 1) * 64)
                    for vt in range(n_vt):
                        mm(po[vt][:, ccols], Sbf[:, vt * 128:(vt + 1) * 128], qT[:, ccols],
                           start=False, stop=True, signal=(vt == n_vt - 1))
                    pU = pU_region(nv)
                    mm(pU, kktok[rows, blk, :], vtok[rows, blk, vcol0:vcol0 + nv])
                    STT("dve", S32, S32, ebl[:, c:c + 1], pU, ALU.mult, ALU.add)
                    CP("act", Sbf, S32)
                yield

        def kk_transposes(tm):
            ptr = ptrans()
            for blk in range(4):
                tr(ptr[:, blk * 128:(blk + 1) * 128], tm["kkT"][:, blk * 128:(blk + 1) * 128])
            CP("act", tm["kktok"].rearrange("p a b -> p (a b)"), ptr)

        def silu_gate(dst, pg):
            ACT(dst, pg, AF.Exp, scale=-1.0)
            ACT(dst, dst, AF.Ln, bias=one)
            ACT(dst, dst, AF.Exp, scale=-1.0)
            TTo("dve", dst, pg, dst, ALU.mult)

        def mixer_hg():
            rmsnorm("nm", 0, yT)
            for hp in range(4):
                sq = ws_get("hg_in", 1).rearrange("p (dt f) -> p dt f", dt=8)
                sf = ws_get("hg_in", 2).rearrange("p (dt f) -> p dt f", dt=8)
                si = ws_get("hg_in", 3).rearrange("p (dt f) -> p dt f", dt=8)
                sg = ws_get("hg_in", 4).rearrange("p (dt f) -> p dt f", dt=8)
                tmv = TM[hp % 2]
                for blk in range(4):
                    pv = pbank()[:, 0:256]
                    for dt in range(8):
                        mm(pv, yT[dt][:, blk * 128:(blk + 1) * 128], si[:, dt, :], start=(dt == 0), stop=(dt == 7))
                    CP("act", tmv["vtok"][:, blk, :], pv)
                tms = []
                for a in range(2):
                    h = 2 * hp + a
                    tm = dict(TM[h % 2])
                    tm["vtok"] = tmv["vtok"]
                    tms.append(tm)
                    A_, B_, C_, D_ = tm["A"], tm["B"], tm["C"], tm["D"]
                    fc = slice(a * 128, (a + 1) * 128)
                    pf = pbank()
                    for dt in range(8):
                        mm(pf, sf[:, dt, fc], yT[dt], start=(dt == 0), stop=(dt == 7))
                    pq = pbank()
                    for dt in range(8):
                        mm(pq, sq[:, dt, fc], yT[dt], start=(dt == 0), stop=(dt == 7))
                    pg = pbank()
                    for dt in range(8):
                        mm(pg, sg[:, dt, fc], yT[dt], start=(dt == 0), stop=(dt == 7))
                    ACT(A_, pf, AF.Exp, scale=-1.0)
                    ACT(B_, A_, AF.Ln, bias=one)
                    ACT(C_, A_, AF.Ln, scale=lb[:, h:h + 1], bias=one)
                    TTo("pool", C_, C_, B_, ALU.subtract)
                    SCAN(D_, smask, C_)
                    TTo("dve", A_, pf, B_, ALU.add)
                    TTo("pool", A_, A_, D_, ALU.add)
                    ACT(tm["kT"], A_, AF.Exp, scale=-1.0, bias=clb[:, h:h + 1])
                    ACT(tm["ebl"], D_[:, 63::64], AF.Exp)
                    TTo("pool", tm["kkT"].rearrange("p (c k) -> p c k", k=64),
                        tm["kT"].rearrange("p (c k) -> p c k", k=64), tm["ebl"].bcast3(64), ALU.mult)
                    kk_transposes(tm)
                    ACT(C_, pq, AF.Exp, scale=-1.0)
                    ACT(C_, C_, AF.Ln, bias=one)
                    TTo("pool", C_, D_, C_, ALU.subtract)
                    ACT(C_, C_, AF.Exp, bias=cst("lnc"))
                    TTo("dve", tm["qT"], pq, C_, ALU.mult)
                    silu_gate(A_, pg)
                pos = [pbank(), pbank()]
                gens = [recurrence(tms[a], [pos[a]], 1, Sbfhg[2 * hp + a], S32hg[2 * hp + a], a * 128) for a in range(2)]
                for blk in range(4):
                    for g in gens:
                        next(g)
                for a in range(2):
                    h = 2 * hp + a
                    tm = tms[a]
                    A_, C_, D_ = tm["A"], tm["C"], tm["D"]
                    po = pos[a]
                    ACT(tm["OSQ0"], po, AF.Square)
                    pss = pbank()
                    mm(pss, ones128, tm["OSQ0"])
                    ACT(C_, pss, AF.Ln, bias=eps)
                    ACT(C_, C_, AF.Exp, scale=-0.5)
                    STT("dve", D_, po, cst("hgnw"), C_, ALU.mult, ALU.mult)
                    TTo("pool", oT[h], D_, A_, ALU.mult)
            out_proj("hg_out")

        def mixer_gla():
            rmsnorm("nm", 8, yT)
            slr = ws_get("gla_lr", 1)[:, 0:1024].rearrange("p (dt f) -> p dt f", dt=8)
            pG = pbank()
            for dt in range(8):
                mm(pG, slr[:, dt, :], yT[dt], start=(dt == 0), stop=(dt == 7))
            CP("act", G_sb, pG)
            for hd0 in (0, 2):
                st = []
                for hd in (hd0, hd0 + 1):
                    sqk = ws_get("gla_qk", 1).rearrange("p (dt two f) -> p dt two f", dt=8, two=2)
                    sv = ws_get("gla_v", 2).rearrange("p (dt f) -> p dt f", dt=8)
                    sg = ws_get("gla_g", 3).rearrange("p (dt f) -> p dt f", dt=8)
                    tm = TM[hd % 2]
                    A_, B_, C_, D_ = tm["A"], tm["B"], tm["C"], tm["D"]
                    SG = [tm["E"], tm["SG1"]]
                    for vt in range(2):
                        pg = pbank()
                        for dt in range(8):
                            mm(pg, sg[:, dt, vt * 128:(vt + 1) * 128], yT[dt], start=(dt == 0), stop=(dt == 7))
                        silu_gate(SG[vt], pg)
                    for blk in range(4):
                        pv = pbank()[:, 0:256]
                        for dt in range(8):
                            mm(pv, yT[dt][:, blk * 128:(blk + 1) * 128], sv[:, dt, :], start=(dt == 0), stop=(dt == 7))
                        CP("act", tm["vtok"][:, blk, :], pv)
                    pgk = pbank()
                    mm(pgk, W2r[:, hd * 128:(hd + 1) * 128], G_sb)
                    ACT(A_, pgk, AF.Exp, scale=-1.0, bias=nbgk[:, hd:hd + 1])
                    ACT(A_, A_, AF.Ln, bias=one)
                    SCAN(D_, smask, A_)
                    ACT(B_, D_, AF.Exp, scale=-1.0 / 16)
                    ACT(C_, D_, AF.Exp, scale=1.0 / 16)
                    ACT(tm["ebl"], D_[:, 63::64], AF.Exp, scale=-1.0 / 16)
                    pq = pbank()
                    for dt in range(8):
                        mm(pq, sqk[:, dt, 0, :], yT[dt], start=(dt == 0), stop=(dt == 7))
                    pk = pbank()
                    for dt in range(8):
                        mm(pk, sqk[:, dt, 1, :], yT[dt], start=(dt == 0), stop=(dt == 7))
                    STT("dve", tm["qT"], pq, 128 ** -0.5, B_, ALU.mult, ALU.mult)
                    TTo("dve", tm["kT"], pk, C_, ALU.mult)
                    TTo("pool", tm["kkT"].rearrange("p (c k) -> p c k", k=64),
                        tm["kT"].rearrange("p (c k) -> p c k", k=64), tm["ebl"].bcast3(64), ALU.mult)
                    kk_transposes(tm)
                    st.append((hd, tm, SG))
                pos = [[pbank(), pbank()] for _ in st]
                gens = [recurrence(tm, pos[i], 2, Sbfgl[hd], S32gl[hd], 0) for i, (hd, tm, SG) in enumerate(st)]
                for blk in range(4):
                    for g in gens:
                        next(g)
                for i, (hd, tm, SG) in enumerate(st):
                    C_, D_ = tm["C"], tm["D"]
                    po = pos[i]
                    OSQ = [tm["OSQ0"], tm["OSQ1"]]
                    for vt in range(2):
                        ACT(OSQ[vt], po[vt], AF.Square)
                    pss = pbank()
                    mm(pss, ones256, OSQ[0], start=True, stop=False)
                    mm(pss, ones256, OSQ[1], start=False, stop=True)
                    ACT(C_, pss, AF.Ln, bias=eps)
                    ACT(C_, C_, AF.Exp, scale=-0.5)
                    for vt in range(2):
                        STT("dve", D_, po[vt], cst("glanw", vt), C_, ALU.mult, ALU.mult)
                        TTo("pool", oT[hd * 2 + vt], D_, SG[vt], ALU.mult)
            out_proj("gla_out")

        def ffn(l):
            rmsnorm("nf", 8 * l, yT)
            pstate["nbig"] = 8
            for g in range(2):
                for ci in range(11):
                    c = g * 11 + ci
                    su = ws_get("up", 1).rearrange("p (dt two f) -> p dt two f", dt=8, two=2)
                    tm = TM[ci % 2]
                    pa = pbank()
                    for dt in range(8):
                        mm(pa, su[:, dt, 0, :], yT[dt], start=(dt == 0), stop=(dt == 7))
                    pg = pbank()
                    for dt in range(8):
                        mm(pg, su[:, dt, 1, :], yT[dt], start=(dt == 0), stop=(dt == 7))
                    for (ps, ct, xs, u, e1, e2) in ((pa, c, tm["XA"], tm["UA"], "dve", "dve"),
                                                     (pg, NCT + c, tm["XG"], tm["UG"], "dve", "dve")):
                        cw = lambda j, ct=ct: cst("cw", (l * 3 + j) * 44 + ct)
                        CP("act", xs[:, 2:TT + 2], ps)
                        CP("pool", xs[:, 0:2], tails[l][ct])
                        if ct < NCT:
                            ACT(u, xs[:, 0:TT], AF.Identity, scale=cw(0), bias=cst("cb", l * 44 + ct))
                        else:
                            TS("dve", u, xs[:, 0:TT], cw(0), ALU.mult, cst("cb", l * 44 + ct), ALU.add)
                        STT(e1, u, xs[:, 1:TT + 1], cw(1), u, ALU.mult, ALU.add)
                        STT(e2, u, xs[:, 2:TT + 2], cw(2), u, ALU.mult, ALU.add)
                        CP("pool", tails[l][ct], xs[:, TT:TT + 2])
                    E = tm["E"]
                    ACT(E, tm["UG"], AF.Silu)
                    TTo("pool", aT[ci], E, tm["UA"], ALU.mult)
                for half in range(2):
                    accs = [pbank() for _ in range(4)]
                    for part in (range(0, 4), range(4, 8), range(8, 11)):
                        sd = ws_get("down", 1).rearrange("p (i f) -> p i f", i=4)
                        for i, ci in enumerate(part):
                            for j in range(4):
                                mm(accs[j], sd[:, i, j * 128:(j + 1) * 128], aT[ci], start=(ci == 0), stop=(ci == 10))
                    for j in range(4):
                        ft = half * 4 + j
                        TTo("dve", hT[ft], hT[ft], accs[j], ALU.add)
            pstate["nbig"] = 6
            pstate["big"] = 0

        out_toks = []
        for tI in range(NT):
            t0 = tI * TT
            R.op("sp", lambda e, tI=tI: e.dma_start(out=hT_t[:], in_=xT_d[tI].rearrange("p (dt t) -> p dt t", dt=8)),
                 writes=hT_b, dma_sem=s_h, dur=600, avail=9000)
            for l in layers:
                if l == 0:
                    mixer_hg()
                else:
                    mixer_gla()
                ffn(l)
            if final_norm:
                rmsnorm("nfin", 0, hT)
            tok = R.op("sp", lambda e, tI=tI: e.dma_start(out=out_d[tI].rearrange("p (dt t) -> p dt t", dt=8), in_=hT_t[:]),
                       reads=hT_b, dma_sem=s_o, dur=600, avail=6000)
        R.final_wait("sp", [(s_o, 16 * NT)])
        assert wst["cur"] == total_chunks
        R.emit(nc)
    return nc


_PROGS = {}


def _prog(NT, layers, final_norm):
    key = (NT, tuple(layers), final_norm)
    if key not in _PROGS:
        _PROGS[key] = build(NT, layers, final_norm)
    return _PROGS[key]


FUSED = True


def to_tiles(xb):
    nt = xb.shape[0] // TT
    return np.ascontiguousarray(xb.reshape(nt, TT, 8, 128).transpose(0, 3, 2, 1)).reshape(nt, 128, 8 * TT)


def from_tiles(o):
    nt = o.shape[0]
    return np.ascontiguousarray(o.reshape(nt, 128, 8, TT).transpose(0, 3, 2, 1)).reshape(nt * TT, D)


def kernel(**inputs):
    inp = {k: np.asarray(v) for k, v in inputs.items()}
    x = inp["x"].astype(np.float32, copy=False)
    B = x.shape[0]
    consts = pack_consts(inp)
    xT = [to_tiles(x[b]) for b in range(B)]
    if FUSED:
        stages = [((0, 1), True)]
    else:
        stages = [((0,), False), ((1,), True)]
    cur = xT
    for layers, fin in stages:
        ws = pack_wstream(inp, layers)
        nc = _prog(T // TT, layers, fin)
        in_maps = [{"xT": cur[b], "wstream": ws, "consts": consts} for b in range(B)]
        res = run_bass_kernel_spmd(nc, in_maps, core_ids=list(range(B)))
        cur = [np.asarray(res.results[b]["outT"]) for b in range(B)]
    out = np.stack([from_tiles(cur[b]) for b in range(B)], axis=0)
    return out.astype(np.float32, copy=False)
```

```python
import contextlib
import numpy as np
import concourse.bass as bass
import concourse.mybir as mybir
from concourse.bass_utils import run_bass_kernel_spmd

F32 = mybir.dt.float32
F32R = mybir.dt.float32r
BF16 = mybir.dt.bfloat16
AF = mybir.ActivationFunctionType
ALU = mybir.AluOpType

D = 1024
T = 4096
TT = 512
DFF = 2816
NCT = 22
NSLOT = 7
ENGS = ("pe", "act", "dve", "pool", "sp")


class Buf:
    __slots__ = ("name", "last_w", "readers")

    def __init__(self, name=""):
        self.name = name
        self.last_w = None
        self.readers = {}


class Rec:
    def __init__(self, same_engine_sync=("act", "dve", "pool")):
        self.ops = {e: [] for e in ENGS}
        self.cnt = {e: 0 for e in ENGS}
        self.waited = {}
        self.same_sync = set(same_engine_sync)
        self.dma_sems = []
        self.prog = []
        self.finals = []

    def new_dma_sem(self, name):
        self.cnt[name] = 0
        self.dma_sems.append(name)
        return name

    def op(self, eng, issue, reads=(), writes=(), signal=True, dma_sem=None, force=False, dur=500.0, avail=None):
        self.prog.append(dict(eng=eng, issue=issue, reads=list(reads), writes=list(writes), signal=signal,
                              dma_sem=dma_sem, dur=float(dur), avail=float(avail if avail is not None else dur)))
        return None

    def _op(self, eng, issue, reads=(), writes=(), signal=True, dma_sem=None):
        need = {}

        def add(tok):
            if tok is None:
                return
            k, v = tok
            if need.get(k, 0) < v:
                need[k] = v

        for b in reads:
            add(b.last_w)
        for b in writes:
            add(b.last_w)
            for k, v in b.readers.items():
                add((k, v))
        waits = []
        for k, v in need.items():
            if k == eng and eng not in self.same_sync:
                continue
            if k == eng and v > self.cnt[eng]:
                continue
            if self.waited.get((eng, k), 0) >= v:
                continue
            self.waited[(eng, k)] = v
            waits.append((k, v))
        if dma_sem is not None:
            self.cnt[dma_sem] += 16
            tok = (dma_sem, self.cnt[dma_sem])
            inc = (dma_sem, 16)
        elif signal:
            self.cnt[eng] += 1
            tok = (eng, self.cnt[eng])
            inc = (eng, 1)
        else:
            tok = (eng, self.cnt[eng] + 1)
            inc = None
        for b in reads:
            if b.readers.get(tok[0], 0) < tok[1]:
                b.readers[tok[0]] = tok[1]
        for b in writes:
            b.last_w = tok
            b.readers = {}
        self.ops[eng].append((waits, issue, inc))
        return tok

    def final_wait(self, eng, toks):
        self.finals.append((eng, [(k, v) for (k, v) in toks]))

    def schedule(self, window=96, lat=350.0):
        prog = self.prog
        units = []
        open_pe = None
        for o in prog:
            if o["eng"] == "pe":
                if open_pe is None:
                    open_pe = dict(eng="pe", ops=[], deps=set(), idx=len(units))
                    units.append(open_pe)
                open_pe["ops"].append(o)
                o["unit"] = open_pe["idx"]
                if o["signal"]:
                    open_pe = None
            else:
                u = dict(eng=o["eng"], ops=[o], deps=set(), idx=len(units))
                units.append(u)
                o["unit"] = u["idx"]
        assert open_pe is None
        lastw = {}
        rdrs = {}
        for o in prog:
            u = o["unit"]
            d = units[u]["deps"]
            for b in o["reads"]:
                w = lastw.get(id(b))
                if w is not None and w != u:
                    d.add(w)
            for b in o["writes"]:
                w = lastw.get(id(b))
                if w is not None and w != u:
                    d.add(w)
                for r in rdrs.get(id(b), ()):
                    if r != u:
                        d.add(r)
            for b in o["reads"]:
                rdrs.setdefault(id(b), set()).add(u)
            for b in o["writes"]:
                lastw[id(b)] = u
                rdrs[id(b)] = set()
        n = len(units)
        succ = [[] for _ in range(n)]
        ndep = [0] * n
        for u in units:
            ndep[u["idx"]] = len(u["deps"])
            for d in u["deps"]:
                succ[d].append(u["idx"])
        per_eng = {e: [u["idx"] for u in units if u["eng"] == e] for e in ENGS}
        pos = {e: 0 for e in ENGS}
        done = [False] * n
        fin = [0.0] * n
        free = {e: 0.0 for e in ENGS}
        order = []
        remaining = n
        while remaining:
            best = None
            for e in ENGS:
                lst = per_eng[e]
                p = pos[e]
                while p < len(lst) and done[lst[p]]:
                    p += 1
                pos[e] = p
                cnt = 0
                q = p
                win = 1 if e == "pe" else window
                while q < len(lst) and cnt < win:
                    ui = lst[q]
                    q += 1
                    if done[ui]:
                        continue
                    cnt += 1
                    if ndep[ui]:
                        continue
                    u = units[ui]
                    st = free[e]
                    for d in u["deps"]:
                        t = fin[d] + lat
                        if t > st:
                            st = t
                    if best is None or st < best[0] or (st == best[0] and ui < best[1]):
                        best = (st, ui)
                    if st <= free[e]:
                        break
            st, ui = best
            u = units[ui]
            e = u["eng"]
            t = st
            for o in u["ops"]:
                t += o["dur"]
            free[e] = t
            fin[ui] = st + sum(o["dur"] for o in u["ops"][:-1]) + u["ops"][-1]["avail"]
            done[ui] = True
            remaining -= 1
            for sidx in succ[ui]:
                ndep[sidx] -= 1
            order.append(ui)
        self.est_ns = max(fin)
        for ui in order:
            for o in units[ui]["ops"]:
                self._op(o["eng"], o["issue"], o["reads"], o["writes"], o["signal"], o["dma_sem"])

    def emit(self, nc):
        import os as _os
        self.schedule(window=int(_os.environ.get("KWINDOW", "96")))
        for eng, toks in self.finals:
            self.ops[eng].append((toks, None, None))
        names = [e for e in ENGS if e != "sp"] + self.dma_sems
        with contextlib.ExitStack() as st:
            sems = {n: st.enter_context(nc.semaphore("s_" + n)) for n in names}
            block = st.enter_context(nc.Block())

            def replay(name, e):
                for waits, issue, inc in self.ops[name]:
                    for k, v in waits:
                        e.wait_ge(sems[k], v)
                    if issue is None:
                        continue
                    ins = issue(e)
                    if inc is not None:
                        ins.then_inc(sems[inc[0]], inc[1])

            @block.tensor
            def _(e):
                replay("pe", e)

            @block.scalar
            def _(e):
                replay("act", e)

            @block.vector
            def _(e):
                replay("dve", e)

            @block.gpsimd
            def _(e):
                replay("pool", e)

            @block.sync
            def _(e):
                replay("sp", e)
                for n_ in names:
                    if self.cnt[n_] > 0:
                        e.wait_ge(sems[n_], self.cnt[n_])
                for n_ in names:
                    e.sem_clear(sems[n_])


class V:
    __slots__ = ("ap", "bufs")

    def __init__(self, ap, bufs):
        self.ap = ap
        self.bufs = list(bufs)

    def __getitem__(self, k):
        return V(self.ap[k], self.bufs)

    def bitcast(self, dt):
        return V(self.ap.bitcast(dt), self.bufs)

    def rearrange(self, s, **kw):
        return V(self.ap.rearrange(s, **kw), self.bufs)

    def bcast3(self, n):
        return V(self.ap.unsqueeze(2).broadcast_to(list(self.ap.shape) + [n]), self.bufs)


def plan(layers):
    ch = []
    for l in layers:
        if l == 0:
            for hp in range(4):
                for k in ("q", "f", "i", "g"):
                    ch.append(("hg_in", k, hp))
            for hp in range(4):
                ch.append(("hg_out", hp))
        else:
            ch.append(("gla_lr",))
            for hd in range(4):
                ch.append(("gla_qk", hd))
                ch.append(("gla_v", hd))
                ch.append(("gla_g", hd))
            for hd in range(4):
                ch.append(("gla_out", hd))
        for g in range(2):
            cs = list(range(g * 11, (g + 1) * 11))
            for c in cs:
                ch.append(("up", l, c))
            for half in range(2):
                for part in (cs[0:4], cs[4:8], cs[8:11]):
                    ch.append(("down", l, half, tuple(part)))
    return ch


def _cols(W, cols):
    sub = W[:, cols].reshape(8, 128, len(cols)).transpose(1, 0, 2)
    return sub.reshape(128, -1)


def pack_wstream(inp, layers):
    chunks = plan(layers)
    out = np.zeros((len(chunks), 128, 2048), np.float32)
    hg_in = inp["hg_w_in"][0]
    hg_out = inp["hg_w_out"][0]
    gl_in = inp["gla_w_in"][0]
    gl_out = inp["gla_w_out"][0]
    for i, c in enumerate(chunks):
        kind = c[0]
        if kind == "hg_in":
            base = {"q": 0, "f": 1024, "i": 2048, "g": 3072}[c[1]] + c[2] * 256
            out[i] = _cols(hg_in, np.arange(base, base + 256))
        elif kind == "hg_out":
            r0 = c[1] * 256
            out[i] = hg_out[r0:r0 + 256].reshape(2, 128, 1024).transpose(1, 0, 2).reshape(128, 2048)
        elif kind == "gla_lr":
            out[i, :, :1024] = _cols(gl_in, np.arange(2960, 3088))
        elif kind == "gla_qk":
            hd = c[1]
            cols = np.concatenate([np.arange(hd * 128, hd * 128 + 128), np.arange(512 + hd * 128, 512 + hd * 128 + 128)])
            out[i] = _cols(gl_in, cols)
        elif kind == "gla_v":
            out[i] = _cols(gl_in, np.arange(1024 + c[1] * 256, 1024 + c[1] * 256 + 256))
        elif kind == "gla_g":
            out[i] = _cols(gl_in, np.arange(2048 + c[1] * 256, 2048 + c[1] * 256 + 256))
        elif kind == "gla_out":
            r0 = c[1] * 256
            out[i] = gl_out[r0:r0 + 256].reshape(2, 128, 1024).transpose(1, 0, 2).reshape(128, 2048)
        elif kind == "up":
            l, ct = c[1], c[2]
            cols = np.concatenate([np.arange(ct * 128, ct * 128 + 128), np.arange(DFF + ct * 128, DFF + ct * 128 + 128)])
            out[i] = _cols(inp["ffn_w_up"][l], cols)
        elif kind == "down":
            l, half, part = c[1], c[2], c[3]
            wd = inp["ffn_w_down"][l]
            for j, ct in enumerate(part):
                out[i, :, j * 512:(j + 1) * 512] = wd[ct * 128:(ct + 1) * 128, half * 512:(half + 1) * 512]
    return out


_CO = {}
_off = 0
for _n, _w in (("nm", 16), ("nf", 16), ("nfin", 8), ("lbl", 24), ("hgnw", 1), ("bgk", 4), ("glanw", 2),
               ("cw", 264), ("cb", 88), ("eps", 1), ("one", 1), ("lnc", 1), ("ident", 128), ("cmask", 128),
               ("smask", 512), ("w2", 512)):
    _CO[_n] = _off
    _off += _w
NCONST = _off


def pack_consts(inp):
    c = np.zeros((128, NCONST), np.float32)

    def put(name, arr):
        arr = np.asarray(arr, np.float32)
        c[:, _CO[name]:_CO[name] + arr.shape[1]] = arr

    pd = lambda v: np.asarray(v, np.float32).reshape(-1, 128).T
    put("nm", np.concatenate([pd(inp["norm_mixer_w"][l]) for l in range(2)], axis=1))
    put("nf", np.concatenate([pd(inp["norm_ffn_w"][l]) for l in range(2)], axis=1))
    put("nfin", pd(inp["norm_final_w"]))
    put("lbl", np.concatenate([pd(inp["lb_logits"][k]) for k in range(3)], axis=1))
    put("hgnw", pd(inp["hg_norm_w"][0]))
    put("bgk", pd(inp["gla_b_gk_up"][0]))
    put("glanw", pd(inp["gla_norm_w"][0]))
    cw = np.asarray(inp["ffn_conv_w"], np.float32)
    put("cw", np.concatenate([pd(cw[l, j]) for l in range(2) for j in range(3)], axis=1))
    cb = np.asarray(inp["ffn_conv_b"], np.float32)
    put("cb", np.concatenate([pd(cb[l]) for l in range(2)], axis=1))
    put("eps", np.full((128, 1), 1e-6, np.float32))
    put("one", np.ones((128, 1), np.float32))
    put("lnc", np.full((128, 1), np.log(128.0 ** -0.5), np.float32))
    put("ident", np.eye(128, dtype=np.float32))
    s = np.arange(128)[:, None]
    t = np.arange(128)[None, :]
    put("cmask", ((s // 64 == t // 64) & (s <= t)).astype(np.float32))
    sm = np.ones((128, 512), np.float32)
    sm[:, ::64] = 0.0
    put("smask", sm)
    w2 = np.zeros((128, 512), np.float32)
    w2[112:128, :] = np.asarray(inp["gla_w_gk_up"][0], np.float32)
    put("w2", w2)
    return c


def build(NT, layers=(0, 1), final_norm=True, same_sync=("act", "dve", "pool")):
    nc = bass.Bass("TRN2", target_bir_lowering=False)
    nc.dge_precook = False
    chunks = plan(layers)
    NCH = len(chunks)
    xT_d = nc.dram_tensor("xT", [NT, 128, 8 * TT], F32, kind="ExternalInput").ap()
    ws_d = nc.dram_tensor("wstream", [NCH, 128, 2048], F32R, kind="ExternalInput").ap()
    cs_d = nc.dram_tensor("consts", [128, NCONST], F32, kind="ExternalInput").ap()
    out_d = nc.dram_tensor("outT", [NT, 128, 8 * TT], F32, kind="ExternalOutput").ap()

    R = Rec(same_engine_sync=same_sync)
    with contextlib.ExitStack() as st:
        def sbt(name, shape, dt=F32, nbuf=1):
            t = st.enter_context(nc.sbuf_tensor(name, shape, dt))
            return t, [Buf(f"{name}{i}") for i in range(nbuf)]

        def tile(name, shape, dt=F32):
            t, b = sbt(name, shape, dt)
            return V(t[:], b)

        hT_t, hT_b = sbt("hT", [128, 8, TT], F32, 8)
        yT_t, yT_b = sbt("yT", [128, 8, TT], F32R, 8)
        oT_t, oT_b = sbt("oT", [128, 8, TT], F32R, 8)
        aT_t, aT_b = sbt("aT", [128, 11, TT], F32R, 11)
        hT = [V(hT_t[:, i, :], [hT_b[i]]) for i in range(8)]
        yT = [V(yT_t[:, i, :], [yT_b[i]]) for i in range(8)]
        oT = [V(oT_t[:, i, :], [oT_b[i]]) for i in range(8)]
        aT = [V(aT_t[:, i, :], [aT_b[i]]) for i in range(11)]
        hT_all = V(hT_t[:], hT_b)
        CS = tile("consts_sb", [128, NCONST])
        slots = [tile(f"slot{i}", [128, 2048], F32R) for i in range(NSLOT)]
        slot_sems = [R.new_dma_sem(f"ws{i}") for i in range(NSLOT)]

        def cst(name, j=0, w=1):
            o = _CO[name] + j
            return CS[:, o:o + w]

        ones1024 = tile("ones1024", [128, 128], F32R)
        ones128 = tile("ones128", [128, 128], F32R)
        ones256 = tile("ones256", [128, 128], F32R)
        ident_bf = tile("ident_bf", [128, 128], BF16)
        lb = tile("lb", [128, 8])
        clb = tile("clb", [128, 8])
        lbe = tile("lbe", [128, 24])
        nbgk = tile("nbgk", [128, 4])
        W2r = tile("W2r", [128, 512], F32R)
        G_sb = tile("G_sb", [128, TT], F32R)
        lnv = tile("lnv", [128, TT])
        rstd = tile("rstd", [128, TT])
        S32hg = [tile(f"S32hg{h}", [128, 128]) for h in range(8)]
        Sbfhg = [tile(f"Sbfhg{h}", [128, 128], BF16) for h in range(8)]
        S32gl = [tile(f"S32gl{h}", [128, 256]) for h in range(4)]
        Sbfgl = [tile(f"Sbfgl{h}", [128, 256], BF16) for h in range(4)]
        tails_t = st.enter_context(nc.sbuf_tensor("tails", [128, 2, 44, 2], F32))
        tails = [[V(tails_t[:, l, ct, :], [Buf(f"tail{l}_{ct}")]) for ct in range(44)] for l in range(2)]
        TM = []
        for s in range(2):
            d = {}
            for n in ("C", "D", "SG1", "E"):
                d[n] = tile(f"t{n}{s}", [128, TT])
            d["XA"] = tile(f"tXA{s}", [128, TT + 2])
            d["XG"] = tile(f"tXG{s}", [128, TT + 2])
            d["A"] = d["XA"][:, 0:TT]
            d["B"] = d["XG"][:, 0:TT]
            d["UA"] = d["C"]
            d["UG"] = d["D"]
            d["OSQ0"] = tile(f"tOSQ0{s}", [128, TT], F32R)
            d["OSQ1"] = tile(f"tOSQ1{s}", [128, TT], F32R)
            for n in ("kT", "kkT", "qT"):
                d[n] = tile(f"t{n}{s}", [128, TT], BF16)
            d["kktok"] = tile(f"tkktok{s}", [128, 4, 128], BF16)
            d["vtok"] = tile(f"tvtok{s}", [128, 4, 256], BF16)
            d["scT"] = [tile(f"tscT{s}{i}", [128, 128], BF16) for i in range(2)]
            d["ebl"] = tile(f"tebl{s}", [128, 8])
            TM.append(d)
        banks = []
        for i in range(8):
            t = st.enter_context(nc.psum_tensor(f"bank{i}", [128, TT], F32))
            banks.append((t, [Buf(f"bk{i}a"), Buf(f"bk{i}b")]))
        pstate = {"big": 0, "sc": 0, "tr": 0, "nbig": 6}

        def pbank():
            n = pstate["nbig"]
            i = pstate["big"] % n
            pstate["big"] += 1
            t, b = banks[i]
            return V(t[:], b)

        def psc_region():
            h = pstate["sc"] % 2
            pstate["sc"] += 1
            t, b = banks[6]
            return V(t[:, h * 128:(h + 1) * 128], [b[0]])

        def pU_region(nv):
            t, b = banks[6]
            return V(t[:, 256:256 + nv], [b[1]])

        def ptrans():
            h = pstate["tr"] % 2
            pstate["tr"] += 1
            t, b = banks[7]
            return V(t[:, h * 256:(h + 1) * 256].bitcast(BF16), [b[h]])

        def bufs_of(*vs):
            r = []
            for v in vs:
                if isinstance(v, V):
                    r += v.bufs
            return r

        def apof(v):
            return v.ap if isinstance(v, V) else v

        def mm(out, lhsT, rhs, start=True, stop=True, signal=None):
            sig = stop if signal is None else signal
            R.op("pe", lambda e: e.matmul(out.ap, lhsT.ap, rhs.ap, start=start, stop=stop),
                 reads=lhsT.bufs + rhs.bufs, writes=out.bufs, signal=sig,
                 dur=max(64, rhs.ap.shape[-1]) / 2.2 + 20)

        def tr(out, in_):
            R.op("pe", lambda e: e.transpose(out.ap, in_.ap, ident_bf.ap),
                 reads=in_.bufs + ident_bf.bufs, writes=out.bufs, signal=True, dur=90)

        def nfree(v):
            n = 1
            for d in v.ap.shape[1:]:
                n *= d
            return n

        def edur(eng, v, mult=1.0):
            n = nfree(v) * mult
            if eng == "act":
                return 220 + 0.85 * n
            if eng == "dve":
                return 120 + 1.0 * n
            return 150 + 2.3 * n

        def ACT(out, in_, func, bias=None, scale=None):
            kw = {}
            if bias is not None:
                kw["bias"] = apof(bias)
            if scale is not None:
                kw["scale"] = apof(scale)
            R.op("act", lambda e: e.activation(out=out.ap, in_=in_.ap, func=func, **kw),
                 reads=bufs_of(in_, bias, scale), writes=out.bufs, dur=edur("act", out))

        def TTo(eng, out, in0, in1, op):
            R.op(eng, lambda e: e.tensor_tensor(out=out.ap, in0=in0.ap, in1=in1.ap, op=op),
                 reads=bufs_of(in0, in1), writes=out.bufs, dur=edur(eng, out))

        def TS(eng, out, in0, s1, op0, s2=None, op1=None):
            kw = {}
            if op1 is None and eng == "pool":
                if op0 == ALU.add:
                    s2, op1 = 1.0, ALU.mult
                elif op0 == ALU.mult:
                    s2, op1 = 0.0, ALU.add
            if op1 is not None:
                kw["op1"] = op1
            R.op(eng, lambda e: e.tensor_scalar(out=out.ap, in0=in0.ap, scalar1=apof(s1), scalar2=apof(s2), op0=op0, **kw),
                 reads=bufs_of(in0, s1, s2), writes=out.bufs, dur=edur(eng, out))

        def STT(eng, out, in0, scalar, in1, op0, op1):
            R.op(eng, lambda e: e.scalar_tensor_tensor(out=out.ap, in0=in0.ap, scalar=apof(scalar), in1=in1.ap, op0=op0, op1=op1),
                 reads=bufs_of(in0, scalar, in1), writes=out.bufs, dur=edur(eng, out))

        def SCAN(out, d0, d1):
            R.op("dve", lambda e: e.tensor_tensor_scan(out=out.ap, data0=d0.ap, data1=d1.ap, initial=0.0, op0=ALU.mult, op1=ALU.add),
                 reads=bufs_of(d0, d1), writes=out.bufs, dur=edur("dve", out, 2.0))

        def RECIP(eng, out, in_, exact=True):
            R.op(eng, lambda e: e.reciprocal(out=out.ap, in_=in_.ap), reads=in_.bufs, writes=out.bufs, dur=edur(eng, out, 6.0))

        def CP(eng, out, in_):
            if eng == "act":
                R.op("act", lambda e: e.copy(out=out.ap, in_=in_.ap), reads=in_.bufs, writes=out.bufs, dur=edur("act", out))
            else:
                R.op(eng, lambda e: e.tensor_copy(out=out.ap, in_=in_.ap), reads=in_.bufs, writes=out.bufs, dur=edur(eng, out))

        def MEMSET(eng, out, val):
            R.op(eng, lambda e: e.memset(out.ap, val), writes=out.bufs, dur=edur(eng, out, 0.5))

        wst = {"next_dma": 0, "cur": 0}
        total_chunks = NCH * NT

        def ws_get(kind, lag=1):
            j = wst["cur"]
            wst["cur"] += 1
            assert chunks[j % NCH][0] == kind, (chunks[j % NCH], kind)
            upto = min(total_chunks - 1, j + NSLOT - lag)
            while wst["next_dma"] <= upto:
                m = wst["next_dma"]
                wst["next_dma"] += 1
                k = m % NCH
                sl = slots[m % NSLOT]
                if chunks[k][0] == "gla_lr":
                    R.op("sp", lambda e, sl=sl, k=k: e.dma_start(out=sl.ap[:, 0:1024], in_=ws_d[k, :, 0:1024]),
                         writes=sl.bufs, dma_sem=slot_sems[m % NSLOT], dur=600, avail=4000)
                else:
                    R.op("sp", lambda e, sl=sl, k=k: e.dma_start(out=sl.ap, in_=ws_d[k]),
                         writes=sl.bufs, dma_sem=slot_sems[m % NSLOT], dur=600, avail=5500)
            return slots[j % NSLOT]

        s_c = R.new_dma_sem("ld_c")
        s_h = R.new_dma_sem("ld_h")
        s_o = R.new_dma_sem("st_o")
        R.op("sp", lambda e: e.dma_start(out=CS.ap, in_=cs_d), writes=CS.bufs, dma_sem=s_c, dur=600, avail=4000)
        onesf = tile("onesf", [128, 128])
        for ot, val in ((ones1024, 1.0 / 1024), (ones128, 1.0 / 128), (ones256, 1.0 / 256)):
            MEMSET("pool", onesf, val)
            CP("act", ot, onesf)
        for h in range(8):
            MEMSET("pool", S32hg[h], 0.0)
            MEMSET("pool", Sbfhg[h], 0.0)
        for h in range(4):
            MEMSET("pool", S32gl[h], 0.0)
            MEMSET("pool", Sbfgl[h], 0.0)
        tails_all = V(tails_t[:], [tails[l][ct].bufs[0] for l in range(2) for ct in range(44)])
        MEMSET("pool", tails_all, 0.0)
        CP("act", ident_bf, cst("ident", 0, 128))
        CP("act", W2r, cst("w2", 0, 512))
        ACT(lbe, cst("lbl", 0, 24), AF.Exp)
        TTo("pool", lb, lbe[:, 0:8], lbe[:, 8:16], ALU.add)
        TTo("pool", lb, lb, lbe[:, 16:24], ALU.add)
        RECIP("dve", lb, lb, exact=True)
        TTo("dve", lb, lb, lbe[:, 0:8], ALU.mult)
        ACT(clb, lb, AF.Ln, scale=-1.0, bias=cst("one"))
        TS("pool", nbgk, cst("bgk", 0, 4), -1.0, ALU.mult)
        one = cst("one")
        eps = cst("eps")
        cmask = cst("cmask", 0, 128)
        smask = cst("smask", 0, 512)

        def rmsnorm(wname, woff, dst):
            for dt in range(8):
                ACT(yT[dt], hT[dt], AF.Square)
            pb = pbank()
            for dt in range(8):
                mm(pb, ones1024, yT[dt], start=(dt == 0), stop=(dt == 7))
            ACT(lnv, pb, AF.Ln, bias=eps)
            ACT(rstd, lnv, AF.Exp, scale=-0.5)
            for dt in range(8):
                if dt % 2 == 0:
                    STT("dve", dst[dt], hT[dt], cst(wname, woff + dt), rstd, ALU.mult, ALU.mult)
                else:
                    ACT(dst[dt], hT[dt], AF.Identity, scale=cst(wname, woff + dt))
                    TTo("pool", dst[dt], dst[dt], rstd, ALU.mult)

        def out_proj(kind):
            so = [ws_get(kind, lag=i + 1) for i in range(4)]
            for ft in range(8):
                pw = pbank()
                for h in range(8):
                    w = so[h // 2].rearrange("p (a f) -> p a f", a=2)[:, h % 2, ft * 128:(ft + 1) * 128]
                    mm(pw, w, oT[h], start=(h == 0), stop=(h == 7))
                TTo("dve", hT[ft], hT[ft], pw, ALU.add)

        def recurrence(tm, po, n_vt, Sbf, S32, vcol0):
            kT, kkT, qT, kktok, vtok, ebl = tm["kT"], tm["kkT"], tm["qT"], tm["kktok"], tm["vtok"], tm["ebl"]
            nv = 128 * n_vt
            for blk in range(4):
                cols = slice(blk * 128, (blk + 1) * 128)
                psc = psc_region()
                mm(psc, kT[:, cols], qT[:, cols])
                scT = tm["scT"][blk % 2]
                TTo("dve", scT, psc, cmask, ALU.mult)
                for vt in range(n_vt):
                    mm(po[vt][:, cols], vtok[:, blk, vcol0 + vt * 128:vcol0 + (vt + 1) * 128], scT,
                       start=True, stop=False, signal=False)
                for ci in range(2):
                    c = blk * 2 + ci
                    ccols = slice(c * 64, (c + 1) * 64)
                    rows = slice(ci * 64, (ci + 1) * 64)
                    for vt in range(n_vt):
                        mm(po[vt][:, ccols], Sbf[:, vt * 128:(vt + 1) * 128], qT[:, ccols],
                           start=False, stop=True, signal=(vt == n_vt - 1))
                    pU = pU_region(nv)
                    mm(pU, kktok[rows, blk, :], vtok[rows, blk, vcol0:vcol0 + nv])
                    STT("dve", S32, S32, ebl[:, c:c + 1], pU, ALU.mult, ALU.add)
                    CP("act", Sbf, S32)
                yield

        def kk_transposes(tm):
            ptr = ptrans()
            for blk in range(4):
                tr(ptr[:, blk * 128:(blk + 1) * 128], tm["kkT"][:, blk * 128:(blk + 1) * 128])
            CP("act", tm["kktok"].rearrange("p a b -> p (a b)"), ptr)

        def silu_gate(dst, pg):
            ACT(dst, pg, AF.Exp, scale=-1.0)
            ACT(dst, dst, AF.Ln, bias=one)
            ACT(dst, dst, AF.Exp, scale=-1.0)
            TTo("dve", dst, pg, dst, ALU.mult)

        def mixer_hg():
            rmsnorm("nm", 0, yT)
            for hp in range(4):
                sq = ws_get("hg_in", 1).rearrange("p (dt f) -> p dt f", dt=8)
                sf = ws_get("hg_in", 2).rearrange("p (dt f) -> p dt f", dt=8)
                si = ws_get("hg_in", 3).rearrange("p (dt f) -> p dt f", dt=8)
                sg = ws_get("hg_in", 4).rearrange("p (dt f) -> p dt f", dt=8)
                tmv = TM[hp % 2]
                for blk in range(4):
                    pv = pbank()[:, 0:256]
                    for dt in range(8):
                        mm(pv, yT[dt][:, blk * 128:(blk + 1) * 128], si[:, dt, :], start=(dt == 0), stop=(dt == 7))
                    CP("act", tmv["vtok"][:, blk, :], pv)
                tms = []
                for a in range(2):
                    h = 2 * hp + a
                    tm = dict(TM[h % 2])
                    tm["vtok"] = tmv["vtok"]
                    tms.append(tm)
                    A_, B_, C_, D_ = tm["A"], tm["B"], tm["C"], tm["D"]
                    fc = slice(a * 128, (a + 1) * 128)
                    pf = pbank()
                    for dt in range(8):
                        mm(pf, sf[:, dt, fc], yT[dt], start=(dt == 0), stop=(dt == 7))
                    pq = pbank()
                    for dt in range(8):
                        mm(pq, sq[:, dt, fc], yT[dt], start=(dt == 0), stop=(dt == 7))
                    pg = pbank()
                    for dt in range(8):
                        mm(pg, sg[:, dt, fc], yT[dt], start=(dt == 0), stop=(dt == 7))
                    ACT(A_, pf, AF.Exp, scale=-1.0)
                    ACT(B_, A_, AF.Ln, bias=one)
                    ACT(C_, A_, AF.Ln, scale=lb[:, h:h + 1], bias=one)
                    TTo("pool", C_, C_, B_, ALU.subtract)
                    SCAN(D_, smask, C_)
                    TTo("dve", A_, pf, B_, ALU.add)
                    TTo("pool", A_, A_, D_, ALU.add)
                    ACT(tm["kT"], A_, AF.Exp, scale=-1.0, bias=clb[:, h:h + 1])
                    ACT(tm["ebl"], D_[:, 63::64], AF.Exp)
                    TTo("pool", tm["kkT"].rearrange("p (c k) -> p c k", k=64),
                        tm["kT"].rearrange("p (c k) -> p c k", k=64), tm["ebl"].bcast3(64), ALU.mult)
                    kk_transposes(tm)
                    ACT(C_, pq, AF.Exp, scale=-1.0)
                    ACT(C_, C_, AF.Ln, bias=one)
                    TTo("pool", C_, D_, C_, ALU.subtract)
                    ACT(C_, C_, AF.Exp, bias=cst("lnc"))
                    TTo("dve", tm["qT"], pq, C_, ALU.mult)
                    silu_gate(A_, pg)
                pos = [pbank(), pbank()]
                gens = [recurrence(tms[a], [pos[a]], 1, Sbfhg[2 * hp + a], S32hg[2 * hp + a], a * 128) for a in range(2)]
                for blk in range(4):
                    for g in gens:
                        next(g)
                for a in range(2):
                    h = 2 * hp + a
                    tm = tms[a]
                    A_, C_, D_ = tm["A"], tm["C"], tm["D"]
                    po = pos[a]
                    ACT(tm["OSQ0"], po, AF.Square)
                    pss = pbank()
                    mm(pss, ones128, tm["OSQ0"])
                    ACT(C_, pss, AF.Ln, bias=eps)
                    ACT(C_, C_, AF.Exp, scale=-0.5)
                    STT("dve", D_, po, cst("hgnw"), C_, ALU.mult, ALU.mult)
                    TTo("pool", oT[h], D_, A_, ALU.mult)
            out_proj("hg_out")

        def mixer_gla():
            rmsnorm("nm", 8, yT)
            slr = ws_get("gla_lr", 1)[:, 0:1024].rearrange("p (dt f) -> p dt f", dt=8)
            pG = pbank()
            for dt in range(8):
                mm(pG, slr[:, dt, :], yT[dt], start=(dt == 0), stop=(dt == 7))
            CP("act", G_sb, pG)
            for hd0 in (0, 2):
                st = []
                for hd in (hd0, hd0 + 1):
                    sqk = ws_get("gla_qk", 1).rearrange("p (dt two f) -> p dt two f", dt=8, two=2)
                    sv = ws_get("gla_v", 2).rearrange("p (dt f) -> p dt f", dt=8)
                    sg = ws_get("gla_g", 3).rearrange("p (dt f) -> p dt f", dt=8)
                    tm = TM[hd % 2]
                    A_, B_, C_, D_ = tm["A"], tm["B"], tm["C"], tm["D"]
                    SG = [tm["E"], tm["SG1"]]
                    for vt in range(2):
                        pg = pbank()
                        for dt in range(8):
                            mm(pg, sg[:, dt, vt * 128:(vt + 1) * 128], yT[dt], start=(dt == 0), stop=(dt == 7))
                        silu_gate(SG[vt], pg)
                    for blk in range(4):
                        pv = pbank()[:, 0:256]
                        for dt in range(8):
                            mm(pv, yT[dt][:, blk * 128:(blk + 1) * 128], sv[:, dt, :], start=(dt == 0), stop=(dt == 7))
                        CP("act", tm["vtok"][:, blk, :], pv)
                    pgk = pbank()
                    mm(pgk, W2r[:, hd * 128:(hd + 1) * 128], G_sb)
                    ACT(A_, pgk, AF.Exp, scale=-1.0, bias=nbgk[:, hd:hd + 1])
                    ACT(A_, A_, AF.Ln, bias=one)
                    SCAN(D_, smask, A_)
                    ACT(B_, D_, AF.Exp, scale=-1.0 / 16)
                    ACT(C_, D_, AF.Exp, scale=1.0 / 16)
                    ACT(tm["ebl"], D_[:, 63::64], AF.Exp, scale=-1.0 / 16)
                    pq = pbank()
                    for dt in range(8):
                        mm(pq, sqk[:, dt, 0, :], yT[dt], start=(dt == 0), stop=(dt == 7))
                    pk = pbank()
                    for dt in range(8):
                        mm(pk, sqk[:, dt, 1, :], yT[dt], start=(dt == 0), stop=(dt == 7))
                    STT("dve", tm["qT"], pq, 128 ** -0.5, B_, ALU.mult, ALU.mult)
                    TTo("dve", tm["kT"], pk, C_, ALU.mult)
                    TTo("pool", tm["kkT"].rearrange("p (c k) -> p c k", k=64),
                        tm["kT"].rearrange("p (c k) -> p c k", k=64), tm["ebl"].bcast3(64), ALU.mult)
                    kk_transposes(tm)
                    st.append((hd, tm, SG))
                pos = [[pbank(), pbank()] for _ in st]
                gens = [recurrence(tm, pos[i], 2, Sbfgl[hd], S32gl[hd], 0) for i, (hd, tm, SG) in enumerate(st)]
                for blk in range(4):
                    for g in gens:
                        next(g)
                for i, (hd, tm, SG) in enumerate(st):
                    C_, D_ = tm["C"], tm["D"]
                    po = pos[i]
                    OSQ = [tm["OSQ0"], tm["OSQ1"]]
                    for vt in range(2):
                        ACT(OSQ[vt], po[vt], AF.Square)
                    pss = pbank()
                    mm(pss, ones256, OSQ[0], start=True, stop=False)
                    mm(pss, ones256, OSQ[1], start=False, stop=True)
                    ACT(C_, pss, AF.Ln, bias=eps)
                    ACT(C_, C_, AF.Exp, scale=-0.5)
                    for vt in range(2):
                        STT("dve", D_, po[vt], cst("glanw", vt), C_, ALU.mult, ALU.mult)
                        TTo("pool", oT[hd * 2 + vt], D_, SG[vt], ALU.mult)
            out_proj("gla_out")

        def ffn(l):
            rmsnorm("nf", 8 * l, yT)
            pstate["nbig"] = 8
            for g in range(2):
                for ci in range(11):
                    c = g * 11 + ci
                    su = ws_get("up", 1).rearrange("p (dt two f) -> p dt two f", dt=8, two=2)
                    tm = TM[ci % 2]
                    pa = pbank()
                    for dt in range(8):
                        mm(pa, su[:, dt, 0, :], yT[dt], start=(dt == 0), stop=(dt == 7))
                    pg = pbank()
                    for dt in range(8):
                        mm(pg, su[:, dt, 1, :], yT[dt], start=(dt == 0), stop=(dt == 7))
                    for (ps, ct, xs, u, e1, e2) in ((pa, c, tm["XA"], tm["UA"], "dve", "dve"),
                                                     (pg, NCT + c, tm["XG"], tm["UG"], "dve", "dve")):
                        cw = lambda j, ct=ct: cst("cw", (l * 3 + j) * 44 + ct)
                        CP("act", xs[:, 2:TT + 2], ps)
                        CP("pool", xs[:, 0:2], tails[l][ct])
                        if ct < NCT:
                            ACT(u, xs[:, 0:TT], AF.Identity, scale=cw(0), bias=cst("cb", l * 44 + ct))
                        else:
                            TS("dve", u, xs[:, 0:TT], cw(0), ALU.mult, cst("cb", l * 44 + ct), ALU.add)
                        STT(e1, u, xs[:, 1:TT + 1], cw(1), u, ALU.mult, ALU.add)
                        STT(e2, u, xs[:, 2:TT + 2], cw(2), u, ALU.mult, ALU.add)
                        CP("pool", tails[l][ct], xs[:, TT:TT + 2])
                    E = tm["E"]
                    ACT(E, tm["UG"], AF.Silu)
                    TTo("pool", aT[ci], E, tm["UA"], ALU.mult)
                for half in range(2):
                    accs = [pbank() for _ in range(4)]
                    for part in (range(0, 4), range(4, 8), range(8, 11)):
                        sd = ws_get("down", 1).rearrange("p (i f) -> p i f", i=4)
                        for i, ci in enumerate(part):
                            for j in range(4):
                                mm(accs[j], sd[:, i, j * 128:(j + 1) * 128], aT[ci], start=(ci == 0), stop=(ci == 10))
                    for j in range(4):
                        ft = half * 4 + j
                        TTo("dve", hT[ft], hT[ft], accs[j], ALU.add)
            pstate["nbig"] = 6
            pstate["big"] = 0

        out_toks = []
        for tI in range(NT):
            t0 = tI * TT
            R.op("sp", lambda e, tI=tI: e.dma_start(out=hT_t[:], in_=xT_d[tI].rearrange("p (dt t) -> p dt t", dt=8)),
                 writes=hT_b, dma_sem=s_h, dur=600, avail=9000)
            for l in layers:
                if l == 0:
                    mixer_hg()
                else:
                    mixer_gla()
                ffn(l)
            if final_norm:
                rmsnorm("nfin", 0, hT)
            tok = R.op("sp", lambda e, tI=tI: e.dma_start(out=out_d[tI].rearrange("p (dt t) -> p dt t", dt=8), in_=hT_t[:]),
                       reads=hT_b, dma_sem=s_o, dur=600, avail=6000)
        R.final_wait("sp", [(s_o, 16 * NT)])
        assert wst["cur"] == total_chunks
        R.emit(nc)
    return nc


_PROGS = {}


def _prog(NT, layers, final_norm):
    key = (NT, tuple(layers), final_norm)
    if key not in _PROGS:
        _PROGS[key] = build(NT, layers, final_norm)
    return _PROGS[key]


FUSED = True


def to_tiles(xb):
    nt = xb.shape[0] // TT
    return np.ascontiguousarray(xb.reshape(nt, TT, 8, 128).transpose(0, 3, 2, 1)).reshape(nt, 128, 8 * TT)


def from_tiles(o):
    nt = o.shape[0]
    return np.ascontiguousarray(o.reshape(nt, 128, 8, TT).transpose(0, 3, 2, 1)).reshape(nt * TT, D)


def kernel(**inputs):
    inp = {k: np.asarray(v) for k, v in inputs.items()}
    x = inp["x"].astype(np.float32, copy=False)
    B = x.shape[0]
    consts = pack_consts(inp)
    xT = [to_tiles(x[b]) for b in range(B)]
    if FUSED:
        stages = [((0, 1), True)]
    else:
        stages = [((0,), False), ((1,), True)]
    cur = xT
    for layers, fin in stages:
        ws = pack_wstream(inp, layers)
        nc = _prog(T // TT, layers, fin)
        in_maps = [{"xT": cur[b], "wstream": ws, "consts": consts} for b in range(B)]
        res = run_bass_kernel_spmd(nc, in_maps, core_ids=list(range(B)))
        cur = [np.asarray(res.results[b]["outT"]) for b in range(B)]
    out = np.stack([from_tiles(cur[b]) for b in range(B)], axis=0)
    return out.astype(np.float32, copy=False)
```

```python
import contextlib
import numpy as np
import concourse.bass as bass
import concourse.mybir as mybir
from concourse.bass_utils import run_bass_kernel_spmd

F32 = mybir.dt.float32
F32R = mybir.dt.float32r
BF16 = mybir.dt.bfloat16
AF = mybir.ActivationFunctionType
ALU = mybir.AluOpType

D = 1024
T = 4096
TT = 512
DFF = 2816
NCT = 22
NSLOT = 7
ENGS = ("pe", "act", "dve", "pool", "sp")


class Buf:
    __slots__ = ("name", "last_w", "readers")

    def __init__(self, name=""):
        self.name = name
        self.last_w = None
        self.readers = {}


class Rec:
    def __init__(self, same_engine_sync=("act", "dve", "pool")):
        self.ops = {e: [] for e in ENGS}
        self.cnt = {e: 0 for e in ENGS}
        self.waited = {}
        self.same_sync = set(same_engine_sync)
        self.dma_sems = []
        self.prog = []
        self.finals = []

    def new_dma_sem(self, name):
        self.cnt[name] = 0
        self.dma_sems.append(name)
        return name

    def op(self, eng, issue, reads=(), writes=(), signal=True, dma_sem=None, force=False, dur=500.0, avail=None):
        self.prog.append(dict(eng=eng, issue=issue, reads=list(reads), writes=list(writes), signal=signal,
                              dma_sem=dma_sem, dur=float(dur), avail=float(avail if avail is not None else dur)))
        return None

    def _op(self, eng, issue, reads=(), writes=(), signal=True, dma_sem=None):
        need = {}

        def add(tok):
            if tok is None:
                return
            k, v = tok
            if need.get(k, 0) < v:
                need[k] = v

        for b in reads:
            add(b.last_w)
        for b in writes:
            add(b.last_w)
            for k, v in b.readers.items():
                add((k, v))
        waits = []
        for k, v in need.items():
            if k == eng and eng not in self.same_sync:
                continue
            if k == eng and v > self.cnt[eng]:
                continue
            if self.waited.get((eng, k), 0) >= v:
                continue
            self.waited[(eng, k)] = v
            waits.append((k, v))
        if dma_sem is not None:
            self.cnt[dma_sem] += 16
            tok = (dma_sem, self.cnt[dma_sem])
            inc = (dma_sem, 16)
        elif signal:
            self.cnt[eng] += 1
            tok = (eng, self.cnt[eng])
            inc = (eng, 1)
        else:
            tok = (eng, self.cnt[eng] + 1)
            inc = None
        for b in reads:
            if b.readers.get(tok[0], 0) < tok[1]:
                b.readers[tok[0]] = tok[1]
        for b in writes:
            b.last_w = tok
            b.readers = {}
        self.ops[eng].append((waits, issue, inc))
        return tok

    def final_wait(self, eng, toks):
        self.finals.append((eng, [(k, v) for (k, v) in toks]))

    def schedule(self, window=96, lat=350.0):
        prog = self.prog
        units = []
        open_pe = None
        for o in prog:
            if o["eng"] == "pe":
                if open_pe is None:
                    open_pe = dict(eng="pe", ops=[], deps=set(), idx=len(units))
                    units.append(open_pe)
                open_pe["ops"].append(o)
                o["unit"] = open_pe["idx"]
                if o["signal"]:
                    open_pe = None
            else:
                u = dict(eng=o["eng"], ops=[o], deps=set(), idx=len(units))
                units.append(u)
                o["unit"] = u["idx"]
        assert open_pe is None
        lastw = {}
        rdrs = {}
        for o in prog:
            u = o["unit"]
            d = units[u]["deps"]
            for b in o["reads"]:
                w = lastw.get(id(b))
                if w is not None and w != u:
                    d.add(w)
            for b in o["writes"]:
                w = lastw.get(id(b))
                if w is not None and w != u:
                    d.add(w)
                for r in rdrs.get(id(b), ()):
                    if r != u:
                        d.add(r)
            for b in o["reads"]:
                rdrs.setdefault(id(b), set()).add(u)
            for b in o["writes"]:
                lastw[id(b)] = u
                rdrs[id(b)] = set()
        n = len(units)
        succ = [[] for _ in range(n)]
        ndep = [0] * n
        for u in units:
            ndep[u["idx"]] = len(u["deps"])
            for d in u["deps"]:
                succ[d].append(u["idx"])
        per_eng = {e: [u["idx"] for u in units if u["eng"] == e] for e in ENGS}
        pos = {e: 0 for e in ENGS}
        done = [False] * n
        fin = [0.0] * n
        free = {e: 0.0 for e in ENGS}
        order = []
        remaining = n
        while remaining:
            best = None
            for e in ENGS:
                lst = per_eng[e]
                p = pos[e]
                while p < len(lst) and done[lst[p]]:
                    p += 1
                pos[e] = p
                cnt = 0
                q = p
                win = 1 if e == "pe" else window
                while q < len(lst) and cnt < win:
                    ui = lst[q]
                    q += 1
                    if done[ui]:
                        continue
                    cnt += 1
                    if ndep[ui]:
                        continue
                    u = units[ui]
                    st = free[e]
                    for d in u["deps"]:
                        t = fin[d] + lat
                        if t > st:
                            st = t
                    if best is None or st < best[0] or (st == best[0] and ui < best[1]):
                        best = (st, ui)
                    if st <= free[e]:
                        break
            st, ui = best
            u = units[ui]
            e = u["eng"]
            t = st
            for o in u["ops"]:
                t += o["dur"]
            free[e] = t
            fin[ui] = st + sum(o["dur"] for o in u["ops"][:-1]) + u["ops"][-1]["avail"]
            done[ui] = True
            remaining -= 1
            for sidx in succ[ui]:
                ndep[sidx] -= 1
            order.append(ui)
        self.est_ns = max(fin)
        for ui in order:
            for o in units[ui]["ops"]:
                self._op(o["eng"], o["issue"], o["reads"], o["writes"], o["signal"], o["dma_sem"])

    def emit(self, nc):
        import os as _os
        self.schedule(window=int(_os.environ.get("KWINDOW", "96")))
        for eng, toks in self.finals:
            self.ops[eng].append((toks, None, None))
        names = [e for e in ENGS if e != "sp"] + self.dma_sems
        with contextlib.ExitStack() as st:
            sems = {n: st.enter_context(nc.semaphore("s_" + n)) for n in names}
            block = st.enter_context(nc.Block())

            def replay(name, e):
                for waits, issue, inc in self.ops[name]:
                    for k, v in waits:
                        e.wait_ge(sems[k], v)
                    if issue is None:
                        continue
                    ins = issue(e)
                    if inc is not None:
                        ins.then_inc(sems[inc[0]], inc[1])

            @block.tensor
            def _(e):
                replay("pe", e)

            @block.scalar
            def _(e):
                replay("act", e)

            @block.vector
            def _(e):
                replay("dve", e)

            @block.gpsimd
            def _(e):
                replay("pool", e)

            @block.sync
            def _(e):
                replay("sp", e)
                for n_ in names:
                    if self.cnt[n_] > 0:
                        e.wait_ge(sems[n_], self.cnt[n_])
                for n_ in names:
                    e.sem_clear(sems[n_])


class V:
    __slots__ = ("ap", "bufs")

    def __init__(self, ap, bufs):
        self.ap = ap
        self.bufs = list(bufs)

    def __getitem__(self, k):
        return V(self.ap[k], self.bufs)

    def bitcast(self, dt):
        return V(self.ap.bitcast(dt), self.bufs)

    def rearrange(self, s, **kw):
        return V(self.ap.rearrange(s, **kw), self.bufs)

    def bcast3(self, n):
        return V(self.ap.unsqueeze(2).broadcast_to(list(self.ap.shape) + [n]), self.bufs)


def plan(layers):
    ch = []
    for l in layers:
        if l == 0:
            for hp in range(4):
                for k in ("q", "f", "i", "g"):
                    ch.append(("hg_in", k, hp))
            for hp in range(4):
                ch.append(("hg_out", hp))
        else:
            ch.append(("gla_lr",))
            for hd in range(4):
                ch.append(("gla_qk", hd))
                ch.append(("gla_v", hd))
                ch.append(("gla_g", hd))
            for hd in range(4):
                ch.append(("gla_out", hd))
        for g in range(2):
            cs = list(range(g * 11, (g + 1) * 11))
            for c in cs:
                ch.append(("up", l, c))
            for half in range(2):
                for part in (cs[0:4], cs[4:8], cs[8:11]):
                    ch.append(("down", l, half, tuple(part)))
    return ch


def _cols(W, cols):
    sub = W[:, cols].reshape(8, 128, len(cols)).transpose(1, 0, 2)
    return sub.reshape(128, -1)


def pack_wstream(inp, layers):
    chunks = plan(layers)
    out = np.zeros((len(chunks), 128, 2048), np.float32)
    hg_in = inp["hg_w_in"][0]
    hg_out = inp["hg_w_out"][0]
    gl_in = inp["gla_w_in"][0]
    gl_out = inp["gla_w_out"][0]
    for i, c in enumerate(chunks):
        kind = c[0]
        if kind == "hg_in":
            base = {"q": 0, "f": 1024, "i": 2048, "g": 3072}[c[1]] + c[2] * 256
            out[i] = _cols(hg_in, np.arange(base, base + 256))
        elif kind == "hg_out":
            r0 = c[1] * 256
            out[i] = hg_out[r0:r0 + 256].reshape(2, 128, 1024).transpose(1, 0, 2).reshape(128, 2048)
        elif kind == "gla_lr":
            out[i, :, :1024] = _cols(gl_in, np.arange(2960, 3088))
        elif kind == "gla_qk":
            hd = c[1]
            cols = np.concatenate([np.arange(hd * 128, hd * 128 + 128), np.arange(512 + hd * 128, 512 + hd * 128 + 128)])
            out[i] = _cols(gl_in, cols)
        elif kind == "gla_v":
            out[i] = _cols(gl_in, np.arange(1024 + c[1] * 256, 1024 + c[1] * 256 + 256))
        elif kind == "gla_g":
            out[i] = _cols(gl_in, np.arange(2048 + c[1] * 256, 2048 + c[1] * 256 + 256))
        elif kind == "gla_out":
            r0 = c[1] * 256
            out[i] = gl_out[r0:r0 + 256].reshape(2, 128, 1024).transpose(1, 0, 2).reshape(128, 2048)
        elif kind == "up":
            l, ct = c[1], c[2]
            cols = np.concatenate([np.arange(ct * 128, ct * 128 + 128), np.arange(DFF + ct * 128, DFF + ct * 128 + 128)])
            out[i] = _cols(inp["ffn_w_up"][l], cols)
        elif kind == "down":
            l, half, part = c[1], c[2], c[3]
            wd = inp["ffn_w_down"][l]
            for j, ct in enumerate(part):
                out[i, :, j * 512:(j + 1) * 512] = wd[ct * 128:(ct + 1) * 128, half * 512:(half + 1) * 512]
    return out


_CO = {}
_off = 0
for _n, _w in (("nm", 16), ("nf", 16), ("nfin", 8), ("lbl", 24), ("hgnw", 1), ("bgk", 4), ("glanw", 2),
               ("cw", 264), ("cb", 88), ("eps", 1), ("one", 1), ("lnc", 1), ("ident", 128), ("cmask", 128),
               ("smask", 512), ("w2", 512)):
    _CO[_n] = _off
    _off += _w
NCONST = _off


def pack_consts(inp):
    c = np.zeros((128, NCONST), np.float32)

    def put(name, arr):
        arr = np.asarray(arr, np.float32)
        c[:, _CO[name]:_CO[name] + arr.shape[1]] = arr

    pd = lambda v: np.asarray(v, np.float32).reshape(-1, 128).T
    put("nm", np.concatenate([pd(inp["norm_mixer_w"][l]) for l in range(2)], axis=1))
    put("nf", np.concatenate([pd(inp["norm_ffn_w"][l]) for l in range(2)], axis=1))
    put("nfin", pd(inp["norm_final_w"]))
    put("lbl", np.concatenate([pd(inp["lb_logits"][k]) for k in range(3)], axis=1))
    put("hgnw", pd(inp["hg_norm_w"][0]))
    put("bgk", pd(inp["gla_b_gk_up"][0]))
    put("glanw", pd(inp["gla_norm_w"][0]))
    cw = np.asarray(inp["ffn_conv_w"], np.float32)
    put("cw", np.concatenate([pd(cw[l, j]) for l in range(2) for j in range(3)], axis=1))
    cb = np.asarray(inp["ffn_conv_b"], np.float32)
    put("cb", np.concatenate([pd(cb[l]) for l in range(2)], axis=1))
    put("eps", np.full((128, 1), 1e-6, np.float32))
    put("one", np.ones((128, 1), np.float32))
    put("lnc", np.full((128, 1), np.log(128.0 ** -0.5), np.float32))
    put("ident", np.eye(128, dtype=np.float32))
    s = np.arange(128)[:, None]
    t = np.arange(128)[None, :]
    put("cmask", ((s // 64 == t // 64) & (s <= t)).astype(np.float32))
    sm = np.ones((128, 512), np.float32)
    sm[:, ::64] = 0.0
    put("smask", sm)
    w2 = np.zeros((128, 512), np.float32)
    w2[112:128, :] = np.asarray(inp["gla_w_gk_up"][0], np.float32)
    put("w2", w2)
    return c


def build(NT, layers=(0, 1), final_norm=True, same_sync=("act", "dve", "pool")):
    nc = bass.Bass("TRN2", target_bir_lowering=False)
    nc.dge_precook = False
    chunks = plan(layers)
    NCH = len(chunks)
    xT_d = nc.dram_tensor("xT", [NT, 128, 8 * TT], F32, kind="ExternalInput").ap()
    ws_d = nc.dram_tensor("wstream", [NCH, 128, 2048], F32R, kind="ExternalInput").ap()
    cs_d = nc.dram_tensor("consts", [128, NCONST], F32, kind="ExternalInput").ap()
    out_d = nc.dram_tensor("outT", [NT, 128, 8 * TT], F32, kind="ExternalOutput").ap()

    R = Rec(same_engine_sync=same_sync)
    with contextlib.ExitStack() as st:
        def sbt(name, shape, dt=F32, nbuf=1):
            t = st.enter_context(nc.sbuf_tensor(name, shape, dt))
            return t, [Buf(f"{name}{i}") for i in range(nbuf)]

        def tile(name, shape, dt=F32):
            t, b = sbt(name, shape, dt)
            return V(t[:], b)

        hT_t, hT_b = sbt("hT", [128, 8, TT], F32, 8)
        yT_t, yT_b = sbt("yT", [128, 8, TT], F32R, 8)
        oT_t, oT_b = sbt("oT", [128, 8, TT], F32R, 8)
        aT_t, aT_b = sbt("aT", [128, 11, TT], F32R, 11)
        hT = [V(hT_t[:, i, :], [hT_b[i]]) for i in range(8)]
        yT = [V(yT_t[:, i, :], [yT_b[i]]) for i in range(8)]
        oT = [V(oT_t[:, i, :], [oT_b[i]]) for i in range(8)]
        aT = [V(aT_t[:, i, :], [aT_b[i]]) for i in range(11)]
        hT_all = V(hT_t[:], hT_b)
        CS = tile("consts_sb", [128, NCONST])
        slots = [tile(f"slot{i}", [128, 2048], F32R) for i in range(NSLOT)]
        slot_sems = [R.new_dma_sem(f"ws{i}") for i in range(NSLOT)]

        def cst(name, j=0, w=1):
            o = _CO[name] + j
            return CS[:, o:o + w]

        ones1024 = tile("ones1024", [128, 128], F32R)
        ones128 = tile("ones128", [128, 128], F32R)
        ones256 = tile("ones256", [128, 128], F32R)
        ident_bf = tile("ident_bf", [128, 128], BF16)
        lb = tile("lb", [128, 8])
        clb = tile("clb", [128, 8])
        lbe = tile("lbe", [128, 24])
        nbgk = tile("nbgk", [128, 4])
        W2r = tile("W2r", [128, 512], F32R)
        G_sb = tile("G_sb", [128, TT], F32R)
        lnv = tile("lnv", [128, TT])
        rstd = tile("rstd", [128, TT])
        S32hg = [tile(f"S32hg{h}", [128, 128]) for h in range(8)]
        Sbfhg = [tile(f"Sbfhg{h}", [128, 128], BF16) for h in range(8)]
        S32gl = [tile(f"S32gl{h}", [128, 256]) for h in range(4)]
        Sbfgl = [tile(f"Sbfgl{h}", [128, 256], BF16) for h in range(4)]
        tails_t = st.enter_context(nc.sbuf_tensor("tails", [128, 2, 44, 2], F32))
        tails = [[V(tails_t[:, l, ct, :], [Buf(f"tail{l}_{ct}")]) for ct in range(44)] for l in range(2)]
        TM = []
        for s in range(2):
            d = {}
            for n in ("C", "D", "SG1", "E"):
                d[n] = tile(f"t{n}{s}", [128, TT])
            d["XA"] = tile(f"tXA{s}", [128, TT + 2])
            d["XG"] = tile(f"tXG{s}", [128, TT + 2])
            d["A"] = d["XA"][:, 0:TT]
            d["B"] = d["XG"][:, 0:TT]
            d["UA"] = d["C"]
            d["UG"] = d["D"]
            d["OSQ0"] = tile(f"tOSQ0{s}", [128, TT], F32R)
            d["OSQ1"] = tile(f"tOSQ1{s}", [128, TT], F32R)
            for n in ("kT", "kkT", "qT"):
                d[n] = tile(f"t{n}{s}", [128, TT], BF16)
            d["kktok"] = tile(f"tkktok{s}", [128, 4, 128], BF16)
            d["vtok"] = tile(f"tvtok{s}", [128, 4, 256], BF16)
            d["scT"] = [tile(f"tscT{s}{i}", [128, 128], BF16) for i in range(2)]
            d["ebl"] = tile(f"tebl{s}", [128, 8])
            TM.append(d)
        banks = []
        for i in range(8):
            t = st.enter_context(nc.psum_tensor(f"bank{i}", [128, TT], F32))
            banks.append((t, [Buf(f"bk{i}a"), Buf(f"bk{i}b")]))
        pstate = {"big": 0, "sc": 0, "tr": 0, "nbig": 6}

        def pbank():
            n = pstate["nbig"]
            i = pstate["big"] % n
            pstate["big"] += 1
            t, b = banks[i]
            return V(t[:], b)

        def psc_region():
            h = pstate["sc"] % 2
            pstate["sc"] += 1
            t, b = banks[6]
            return V(t[:, h * 128:(h + 1) * 128], [b[0]])

        def pU_region(nv):
            t, b = banks[6]
            return V(t[:, 256:256 + nv], [b[1]])

        def ptrans():
            h = pstate["tr"] % 2
            pstate["tr"] += 1
            t, b = banks[7]
            return V(t[:, h * 256:(h + 1) * 256].bitcast(BF16), [b[h]])

        def bufs_of(*vs):
            r = []
            for v in vs:
                if isinstance(v, V):
                    r += v.bufs
            return r

        def apof(v):
            return v.ap if isinstance(v, V) else v

        def mm(out, lhsT, rhs, start=True, stop=True, signal=None):
            sig = stop if signal is None else signal
            R.op("pe", lambda e: e.matmul(out.ap, lhsT.ap, rhs.ap, start=start, stop=stop),
                 reads=lhsT.bufs + rhs.bufs, writes=out.bufs, signal=sig,
                 dur=max(64, rhs.ap.shape[-1]) / 2.2 + 20)

        def tr(out, in_):
            R.op("pe", lambda e: e.transpose(out.ap, in_.ap, ident_bf.ap),
                 reads=in_.bufs + ident_bf.bufs, writes=out.bufs, signal=True, dur=90)

        def nfree(v):
            n = 1
            for d in v.ap.shape[1:]:
                n *= d
            return n

        def edur(eng, v, mult=1.0):
            n = nfree(v) * mult
            if eng == "act":
                return 220 + 0.85 * n
            if eng == "dve":
                return 120 + 1.0 * n
            return 150 + 2.3 * n

        def ACT(out, in_, func, bias=None, scale=None):
            kw = {}
            if bias is not None:
                kw["bias"] = apof(bias)
            if scale is not None:
                kw["scale"] = apof(scale)
            R.op("act", lambda e: e.activation(out=out.ap, in_=in_.ap, func=func, **kw),
                 reads=bufs_of(in_, bias, scale), writes=out.bufs, dur=edur("act", out))

        def TTo(eng, out, in0, in1, op):
            R.op(eng, lambda e: e.tensor_tensor(out=out.ap, in0=in0.ap, in1=in1.ap, op=op),
                 reads=bufs_of(in0, in1), writes=out.bufs, dur=edur(eng, out))

        def TS(eng, out, in0, s1, op0, s2=None, op1=None):
            kw = {}
            if op1 is None and eng == "pool":
                if op0 == ALU.add:
                    s2, op1 = 1.0, ALU.mult
                elif op0 == ALU.mult:
                    s2, op1 = 0.0, ALU.add
            if op1 is not None:
                kw["op1"] = op1
            R.op(eng, lambda e: e.tensor_scalar(out=out.ap, in0=in0.ap, scalar1=apof(s1), scalar2=apof(s2), op0=op0, **kw),
                 reads=bufs_of(in0, s1, s2), writes=out.bufs, dur=edur(eng, out))

        def STT(eng, out, in0, scalar, in1, op0, op1):
            R.op(eng, lambda e: e.scalar_tensor_tensor(out=out.ap, in0=in0.ap, scalar=apof(scalar), in1=in1.ap, op0=op0, op1=op1),
                 reads=bufs_of(in0, scalar, in1), writes=out.bufs, dur=edur(eng, out))

        def SCAN(out, d0, d1):
            R.op("dve", lambda e: e.tensor_tensor_scan(out=out.ap, data0=d0.ap, data1=d1.ap, initial=0.0, op0=ALU.mult, op1=ALU.add),
                 reads=bufs_of(d0, d1), writes=out.bufs, dur=edur("dve", out, 2.0))

        def RECIP(eng, out, in_, exact=True):
            R.op(eng, lambda e: e.reciprocal(out=out.ap, in_=in_.ap), reads=in_.bufs, writes=out.bufs, dur=edur(eng, out, 6.0))

        def CP(eng, out, in_):
            if eng == "act":
                R.op("act", lambda e: e.copy(out=out.ap, in_=in_.ap), reads=in_.bufs, writes=out.bufs, dur=edur("act", out))
            else:
                R.op(eng, lambda e: e.tensor_copy(out=out.ap, in_=in_.ap), reads=in_.bufs, writes=out.bufs, dur=edur(eng, out))

        def MEMSET(eng, out, val):
            R.op(eng, lambda e: e.memset(out.ap, val), writes=out.bufs, dur=edur(eng, out, 0.5))

        wst = {"next_dma": 0, "cur": 0}
        total_chunks = NCH * NT

        def ws_get(kind, lag=1):
            j = wst["cur"]
            wst["cur"] += 1
            assert chunks[j % NCH][0] == kind, (chunks[j % NCH], kind)
            upto = min(total_chunks - 1, j + NSLOT - lag)
            while wst["next_dma"] <= upto:
                m = wst["next_dma"]
                wst["next_dma"] += 1
                k = m % NCH
                sl = slots[m % NSLOT]
                if chunks[k][0] == "gla_lr":
                    R.op("sp", lambda e, sl=sl, k=k: e.dma_start(out=sl.ap[:, 0:1024], in_=ws_d[k, :, 0:1024]),
                         writes=sl.bufs, dma_sem=slot_sems[m % NSLOT], dur=600, avail=4000)
                else:
                    R.op("sp", lambda e, sl=sl, k=k: e.dma_start(out=sl.ap, in_=ws_d[k]),
                         writes=sl.bufs, dma_sem=slot_sems[m % NSLOT], dur=600, avail=5500)
            return slots[j % NSLOT]

        s_c = R.new_dma_sem("ld_c")
        s_h = R.new_dma_sem("ld_h")
        s_o = R.new_dma_sem("st_o")
        R.op("sp", lambda e: e.dma_start(out=CS.ap, in_=cs_d), writes=CS.bufs, dma_sem=s_c, dur=600, avail=4000)
        onesf = tile("onesf", [128, 128])
        for ot, val in ((ones1024, 1.0 / 1024), (ones128, 1.0 / 128), (ones256, 1.0 / 256)):
            MEMSET("pool", onesf, val)
            CP("act", ot, onesf)
        for h in range(8):
            MEMSET("pool", S32hg[h], 0.0)
            MEMSET("pool", Sbfhg[h], 0.0)
        for h in range(4):
            MEMSET("pool", S32gl[h], 0.0)
            MEMSET("pool", Sbfgl[h], 0.0)
        tails_all = V(tails_t[:], [tails[l][ct].bufs[0] for l in range(2) for ct in range(44)])
        MEMSET("pool", tails_all, 0.0)
        CP("act", ident_bf, cst("ident", 0, 128))
        CP("act", W2r, cst("w2", 0, 512))
        ACT(lbe, cst("lbl", 0, 24), AF.Exp)
        TTo("pool", lb, lbe[:, 0:8], lbe[:, 8:16], ALU.add)
        TTo("pool", lb, lb, lbe[:, 16:24], ALU.add)
        RECIP("dve", lb, lb, exact=True)
        TTo("dve", lb, lb, lbe[:, 0:8], ALU.mult)
        ACT(clb, lb, AF.Ln, scale=-1.0, bias=cst("one"))
        TS("pool", nbgk, cst("bgk", 0, 4), -1.0, ALU.mult)
        one = cst("one")
        eps = cst("eps")
        cmask = cst("cmask", 0, 128)
        smask = cst("smask", 0, 512)

        def rmsnorm(wname, woff, dst):
            for dt in range(8):
                ACT(yT[dt], hT[dt], AF.Square)
            pb = pbank()
            for dt in range(8):
                mm(pb, ones1024, yT[dt], start=(dt == 0), stop=(dt == 7))
            ACT(rstd, pb, AF.Ln, bias=eps)
            ACT(rstd, rstd, AF.Exp, scale=-0.5)
            for dt in range(8):
                if dt % 2 == 0:
                    STT("dve", dst[dt], hT[dt], cst(wname, woff + dt), rstd, ALU.mult, ALU.mult)
                else:
                    ACT(dst[dt], hT[dt], AF.Identity, scale=cst(wname, woff + dt))
                    TTo("pool", dst[dt], dst[dt], rstd, ALU.mult)

        def out_proj(kind):
            so = [ws_get(kind, lag=i + 1) for i in range(4)]
            for ft in range(8):
                pw = pbank()
                for h in range(8):
                    w = so[h // 2].rearrange("p (a f) -> p a f", a=2)[:, h % 2, ft * 128:(ft + 1) * 128]
                    mm(pw, w, oT[h], start=(h == 0), stop=(h == 7))
                TTo("dve", hT[ft], hT[ft], pw, ALU.add)

        def recurrence(tm, po, n_vt, Sbf, S32, vcol0):
            kT, kkT, qT, kktok, vtok, ebl = tm["kT"], tm["kkT"], tm["qT"], tm["kktok"], tm["vtok"], tm["ebl"]
            nv = 128 * n_vt
            for blk in range(4):
                cols = slice(blk * 128, (blk + 1) * 128)
                psc = psc_region()
                mm(psc, kT[:, cols], qT[:, cols])
                scT = tm["scT"][blk % 2]
                TTo("dve", scT, psc, cmask, ALU.mult)
                for vt in range(n_vt):
                    mm(po[vt][:, cols], vtok[:, blk, vcol0 + vt * 128:vcol0 + (vt + 1) * 128], scT,
                       start=True, stop=False, signal=False)
                for ci in range(2):
                    c = blk * 2 + ci
                    ccols = slice(c * 64, (c + 1) * 64)
                    rows = slice(ci * 64, (ci + 1) * 64)
                    for vt in range(n_vt):
                        mm(po[vt][:, ccols], Sbf[:, vt * 128:(vt + 1) * 128], qT[:, ccols],
                           start=False, stop=True, signal=(vt == n_vt - 1))
                    pU = pU_region(nv)
                    mm(pU, kktok[rows, blk, :], vtok[rows, blk, vcol0:vcol0 + nv])
                    STT("dve", S32, S32, ebl[:, c:c + 1], pU, ALU.mult, ALU.add)
                    CP("act", Sbf, S32)
                yield

        def kk_transposes(tm):
            ptr = ptrans()
            for blk in range(4):
                tr(ptr[:, blk * 128:(blk + 1) * 128], tm["kkT"][:, blk * 128:(blk + 1) * 128])
            CP("act", tm["kktok"].rearrange("p a b -> p (a b)"), ptr)

        def silu_gate(dst, pg):
            ACT(dst, pg, AF.Exp, scale=-1.0)
            ACT(dst, dst, AF.Ln, bias=one)
            ACT(dst, dst, AF.Exp, scale=-1.0)
            TTo("dve", dst, pg, dst, ALU.mult)

        def mixer_hg():
            rmsnorm("nm", 0, yT)
            for hp in range(4):
                sq = ws_get("hg_in", 1).rearrange("p (dt f) -> p dt f", dt=8)
                sf = ws_get("hg_in", 2).rearrange("p (dt f) -> p dt f", dt=8)
                si = ws_get("hg_in", 3).rearrange("p (dt f) -> p dt f", dt=8)
                sg = ws_get("hg_in", 4).rearrange("p (dt f) -> p dt f", dt=8)
                tmv = TM[hp % 2]
                for blk in range(4):
                    pv = pbank()[:, 0:256]
                    for dt in range(8):
                        mm(pv, yT[dt][:, blk * 128:(blk + 1) * 128], si[:, dt, :], start=(dt == 0), stop=(dt == 7))
                    CP("act", tmv["vtok"][:, blk, :], pv)
                tms = []
                for a in range(2):
                    h = 2 * hp + a
                    tm = dict(TM[h % 2])
                    tm["vtok"] = tmv["vtok"]
                    tms.append(tm)
                    A_, B_, C_, D_ = tm["A"], tm["B"], tm["C"], tm["D"]
                    fc = slice(a * 128, (a + 1) * 128)
                    pf = pbank()
                    for dt in range(8):
                        mm(pf, sf[:, dt, fc], yT[dt], start=(dt == 0), stop=(dt == 7))
                    pq = pbank()
                    for dt in range(8):
                        mm(pq, sq[:, dt, fc], yT[dt], start=(dt == 0), stop=(dt == 7))
                    pg = pbank()
                    for dt in range(8):
                        mm(pg, sg[:, dt, fc], yT[dt], start=(dt == 0), stop=(dt == 7))
                    ACT(A_, pf, AF.Exp, scale=-1.0)
                    ACT(B_, A_, AF.Ln, bias=one)
                    ACT(C_, A_, AF.Ln, scale=lb[:, h:h + 1], bias=one)
                    TTo("pool", C_, C_, B_, ALU.subtract)
                    SCAN(D_, smask, C_)
                    TTo("dve", A_, pf, B_, ALU.add)
                    TTo("pool", A_, A_, D_, ALU.add)
                    ACT(tm["kT"], A_, AF.Exp, scale=-1.0, bias=clb[:, h:h + 1])
                    ACT(tm["ebl"], D_[:, 63::64], AF.Exp)
                    TTo("pool", tm["kkT"].rearrange("p (c k) -> p c k", k=64),
                        tm["kT"].rearrange("p (c k) -> p c k", k=64), tm["ebl"].bcast3(64), ALU.mult)
                    kk_transposes(tm)
                    ACT(C_, pq, AF.Exp, scale=-1.0)
                    ACT(C_, C_, AF.Ln, bias=one)
                    TTo("pool", C_, D_, C_, ALU.subtract)
                    ACT(C_, C_, AF.Exp, bias=cst("lnc"))
                    TTo("dve", tm["qT"], pq, C_, ALU.mult)
                    silu_gate(A_, pg)
                pos = [pbank(), pbank()]
                gens = [recurrence(tms[a], [pos[a]], 1, Sbfhg[2 * hp + a], S32hg[2 * hp + a], a * 128) for a in range(2)]
                for blk in range(4):
                    for g in gens:
                        next(g)
                for a in range(2):
                    h = 2 * hp + a
                    tm = tms[a]
                    A_, C_, D_ = tm["A"], tm["C"], tm["D"]
                    po = pos[a]
                    ACT(tm["OSQ0"], po, AF.Square)
                    pss = pbank()
                    mm(pss, ones128, tm["OSQ0"])
                    ACT(C_, pss, AF.Ln, bias=eps)
                    ACT(C_, C_, AF.Exp, scale=-0.5)
                    STT("dve", D_, po, cst("hgnw"), C_, ALU.mult, ALU.mult)
                    TTo("pool", oT[h], D_, A_, ALU.mult)
            out_proj("hg_out")

        def mixer_gla():
            rmsnorm("nm", 8, yT)
            slr = ws_get("gla_lr", 1)[:, 0:1024].rearrange("p (dt f) -> p dt f", dt=8)
            pG = pbank()
            for dt in range(8):
                mm(pG, slr[:, dt, :], yT[dt], start=(dt == 0), stop=(dt == 7))
            CP("act", G_sb, pG)
            for hd0 in (0, 2):
                st = []
                for hd in (hd0, hd0 + 1):
                    sqk = ws_get("gla_qk", 1).rearrange("p (dt two f) -> p dt two f", dt=8, two=2)
                    sv = ws_get("gla_v", 2).rearrange("p (dt f) -> p dt f", dt=8)
                    sg = ws_get("gla_g", 3).rearrange("p (dt f) -> p dt f", dt=8)
                    tm = TM[hd % 2]
                    A_, B_, C_, D_ = tm["A"], tm["B"], tm["C"], tm["D"]
                    SG = [tm["E"], tm["SG1"]]
                    for vt in range(2):
                        pg = pbank()
                        for dt in range(8):
                            mm(pg, sg[:, dt, vt * 128:(vt + 1) * 128], yT[dt], start=(dt == 0), stop=(dt == 7))
                        silu_gate(SG[vt], pg)
                    for blk in range(4):
                        pv = pbank()[:, 0:256]
                        for dt in range(8):
                            mm(pv, yT[dt][:, blk * 128:(blk + 1) * 128], sv[:, dt, :], start=(dt == 0), stop=(dt == 7))
                        CP("act", tm["vtok"][:, blk, :], pv)
                    pgk = pbank()
                    mm(pgk, W2r[:, hd * 128:(hd + 1) * 128], G_sb)
                    ACT(A_, pgk, AF.Exp, scale=-1.0, bias=nbgk[:, hd:hd + 1])
                    ACT(A_, A_, AF.Ln, bias=one)
                    SCAN(D_, smask, A_)
                    ACT(B_, D_, AF.Exp, scale=-1.0 / 16)
                    ACT(C_, D_, AF.Exp, scale=1.0 / 16)
                    ACT(tm["ebl"], D_[:, 63::64], AF.Exp, scale=-1.0 / 16)
                    pq = pbank()
                    for dt in range(8):
                        mm(pq, sqk[:, dt, 0, :], yT[dt], start=(dt == 0), stop=(dt == 7))
                    pk = pbank()
                    for dt in range(8):
                        mm(pk, sqk[:, dt, 1, :], yT[dt], start=(dt == 0), stop=(dt == 7))
                    STT("dve", tm["qT"], pq, 128 ** -0.5, B_, ALU.mult, ALU.mult)
                    TTo("dve", tm["kT"], pk, C_, ALU.mult)
                    TTo("pool", tm["kkT"].rearrange("p (c k) -> p c k", k=64),
                        tm["kT"].rearrange("p (c k) -> p c k", k=64), tm["ebl"].bcast3(64), ALU.mult)
                    kk_transposes(tm)
                    st.append((hd, tm, SG))
                pos = [[pbank(), pbank()] for _ in st]
                gens = [recurrence(tm, pos[i], 2, Sbfgl[hd], S32gl[hd], 0) for i, (hd, tm, SG) in enumerate(st)]
                for blk in range(4):
                    for g in gens:
                        next(g)
                for i, (hd, tm, SG) in enumerate(st):
                    C_, D_ = tm["C"], tm["D"]
                    po = pos[i]
                    OSQ = [tm["OSQ0"], tm["OSQ1"]]
                    for vt in range(2):
                        ACT(OSQ[vt], po[vt], AF.Square)
                    pss = pbank()
                    mm(pss, ones256, OSQ[0], start=True, stop=False)
                    mm(pss, ones256, OSQ[1], start=False, stop=True)
                    ACT(C_, pss, AF.Ln, bias=eps)
                    ACT(C_, C_, AF.Exp, scale=-0.5)
                    for vt in range(2):
                        STT("dve", D_, po[vt], cst("glanw", vt), C_, ALU.mult, ALU.mult)
                        TTo("pool", oT[hd * 2 + vt], D_, SG[vt], ALU.mult)
            out_proj("gla_out")

        def ffn(l):
            rmsnorm("nf", 8 * l, yT)
            pstate["nbig"] = 8
            for g in range(2):
                for ci in range(11):
                    c = g * 11 + ci
                    su = ws_get("up", 1).rearrange("p (dt two f) -> p dt two f", dt=8, two=2)
                    tm = TM[ci % 2]
                    pa = pbank()
                    for dt in range(8):
                        mm(pa, su[:, dt, 0, :], yT[dt], start=(dt == 0), stop=(dt == 7))
                    pg = pbank()
                    for dt in range(8):
                        mm(pg, su[:, dt, 1, :], yT[dt], start=(dt == 0), stop=(dt == 7))
                    for (ps, ct, xs, u, e1, e2) in ((pa, c, tm["XA"], tm["UA"], "dve", "dve"),
                                                     (pg, NCT + c, tm["XG"], tm["UG"], "dve", "dve")):
                        cw = lambda j, ct=ct: cst("cw", (l * 3 + j) * 44 + ct)
                        CP("act", xs[:, 2:TT + 2], ps)
                        CP("pool", xs[:, 0:2], tails[l][ct])
                        ACT(u, xs[:, 0:TT], AF.Identity, scale=cw(0), bias=cst("cb", l * 44 + ct))
                        STT(e1, u, xs[:, 1:TT + 1], cw(1), u, ALU.mult, ALU.add)
                        STT(e2, u, xs[:, 2:TT + 2], cw(2), u, ALU.mult, ALU.add)
                        CP("pool", tails[l][ct], xs[:, TT:TT + 2])
                    E = tm["E"]
                    ACT(E, tm["UG"], AF.Silu)
                    TTo("pool", aT[ci], E, tm["UA"], ALU.mult)
                for half in range(2):
                    accs = [pbank() for _ in range(4)]
                    for part in (range(0, 4), range(4, 8), range(8, 11)):
                        sd = ws_get("down", 1).rearrange("p (i f) -> p i f", i=4)
                        for i, ci in enumerate(part):
                            for j in range(4):
                                mm(accs[j], sd[:, i, j * 128:(j + 1) * 128], aT[ci], start=(ci == 0), stop=(ci == 10))
                    for j in range(4):
                        ft = half * 4 + j
                        TTo("dve", hT[ft], hT[ft], accs[j], ALU.add)
            pstate["nbig"] = 6
            pstate["big"] = 0

        out_toks = []
        for tI in range(NT):
            t0 = tI * TT
            R.op("sp", lambda e, tI=tI: e.dma_start(out=hT_t[:], in_=xT_d[tI].rearrange("p (dt t) -> p dt t", dt=8)),
                 writes=hT_b, dma_sem=s_h, dur=600, avail=9000)
            for l in layers:
                if l == 0:
                    mixer_hg()
                else:
                    mixer_gla()
                ffn(l)
            if final_norm:
                rmsnorm("nfin", 0, hT)
            tok = R.op("sp", lambda e, tI=tI: e.dma_start(out=out_d[tI].rearrange("p (dt t) -> p dt t", dt=8), in_=hT_t[:]),
                       reads=hT_b, dma_sem=s_o, dur=600, avail=6000)
        R.final_wait("sp", [(s_o, 16 * NT)])
        assert wst["cur"] == total_chunks
        R.emit(nc)
    return nc


_PROGS = {}


def _prog(NT, layers, final_norm):
    key = (NT, tuple(layers), final_norm)
    if key not in _PROGS:
        _PROGS[key] = build(NT, layers, final_norm)
    return _PROGS[key]


FUSED = True


def to_tiles(xb):
    nt = xb.shape[0] // TT
    return np.ascontiguousarray(xb.reshape(nt, TT, 8, 128).transpose(0, 3, 2, 1)).reshape(nt, 128, 8 * TT)


def from_tiles(o):
    nt = o.shape[0]
    return np.ascontiguousarray(o.reshape(nt, 128, 8, TT).transpose(0, 3, 2, 1)).reshape(nt * TT, D)


def kernel(**inputs):
    inp = {k: np.asarray(v) for k, v in inputs.items()}
    x = inp["x"].astype(np.float32, copy=False)
    B = x.shape[0]
    consts = pack_consts(inp)
    xT = [to_tiles(x[b]) for b in range(B)]
    if FUSED:
        stages = [((0, 1), True)]
    else:
        stages = [((0,), False), ((1,), True)]
    cur = xT
    for layers, fin in stages:
        ws = pack_wstream(inp, layers)
        nc = _prog(T // TT, layers, fin)
        in_maps = [{"xT": cur[b], "wstream": ws, "consts": consts} for b in range(B)]
        res = run_bass_kernel_spmd(nc, in_maps, core_ids=list(range(B)))
        cur = [np.asarray(res.results[b]["outT"]) for b in range(B)]
    out = np.stack([from_tiles(cur[b]) for b in range(B)], axis=0)
    return out.astype(np.float32, copy=False)
```

```python
import contextlib
import numpy as np
import concourse.bass as bass
import concourse.mybir as mybir
from concourse.bass_utils import run_bass_kernel_spmd

F32 = mybir.dt.float32
F32R = mybir.dt.float32r
BF16 = mybir.dt.bfloat16
AF = mybir.ActivationFunctionType
ALU = mybir.AluOpType

D = 1024
T = 4096
TT = 512
DFF = 2816
NCT = 22
NSLOT = 8
ENGS = ("pe", "act", "dve", "pool", "sp")


class Buf:
    __slots__ = ("name", "last_w", "readers")

    def __init__(self, name=""):
        self.name = name
        self.last_w = None
        self.readers = {}


class Rec:
    def __init__(self, same_engine_sync=("act", "dve", "pool")):
        self.ops = {e: [] for e in ENGS}
        self.cnt = {e: 0 for e in ENGS}
        self.waited = {}
        self.same_sync = set(same_engine_sync)
        self.dma_sems = []
        self.prog = []
        self.finals = []

    def new_dma_sem(self, name):
        self.cnt[name] = 0
        self.dma_sems.append(name)
        return name

    def op(self, eng, issue, reads=(), writes=(), signal=True, dma_sem=None, force=False, dur=500.0, avail=None):
        self.prog.append(dict(eng=eng, issue=issue, reads=list(reads), writes=list(writes), signal=signal,
                              dma_sem=dma_sem, dur=float(dur), avail=float(avail if avail is not None else dur)))
        return None

    def _op(self, eng, issue, reads=(), writes=(), signal=True, dma_sem=None):
        need = {}

        def add(tok):
            if tok is None:
                return
            k, v = tok
            if need.get(k, 0) < v:
                need[k] = v

        for b in reads:
            add(b.last_w)
        for b in writes:
            add(b.last_w)
            for k, v in b.readers.items():
                add((k, v))
        waits = []
        for k, v in need.items():
            if k == eng and eng not in self.same_sync:
                continue
            if k == eng and v > self.cnt[eng]:
                continue
            if self.waited.get((eng, k), 0) >= v:
                continue
            self.waited[(eng, k)] = v
            waits.append((k, v))
        if dma_sem is not None:
            self.cnt[dma_sem] += 16
            tok = (dma_sem, self.cnt[dma_sem])
            inc = (dma_sem, 16)
        elif signal:
            self.cnt[eng] += 1
            tok = (eng, self.cnt[eng])
            inc = (eng, 1)
        else:
            tok = (eng, self.cnt[eng] + 1)
            inc = None
        for b in reads:
            if b.readers.get(tok[0], 0) < tok[1]:
                b.readers[tok[0]] = tok[1]
        for b in writes:
            b.last_w = tok
            b.readers = {}
        self.ops[eng].append((waits, issue, inc))
        return tok

    def final_wait(self, eng, toks):
        self.finals.append((eng, [(k, v) for (k, v) in toks]))

    def schedule(self, window=96, lat=350.0):
        prog = self.prog
        units = []
        open_pe = None
        for o in prog:
            if o["eng"] == "pe":
                if open_pe is None:
                    open_pe = dict(eng="pe", ops=[], deps=set(), idx=len(units))
                    units.append(open_pe)
                open_pe["ops"].append(o)
                o["unit"] = open_pe["idx"]
                if o["signal"]:
                    open_pe = None
            else:
                u = dict(eng=o["eng"], ops=[o], deps=set(), idx=len(units))
                units.append(u)
                o["unit"] = u["idx"]
        assert open_pe is None
        lastw = {}
        rdrs = {}
        for o in prog:
            u = o["unit"]
            d = units[u]["deps"]
            for b in o["reads"]:
                w = lastw.get(id(b))
                if w is not None and w != u:
                    d.add(w)
            for b in o["writes"]:
                w = lastw.get(id(b))
                if w is not None and w != u:
                    d.add(w)
                for r in rdrs.get(id(b), ()):
                    if r != u:
                        d.add(r)
            for b in o["reads"]:
                rdrs.setdefault(id(b), set()).add(u)
            for b in o["writes"]:
                lastw[id(b)] = u
                rdrs[id(b)] = set()
        n = len(units)
        succ = [[] for _ in range(n)]
        ndep = [0] * n
        for u in units:
            ndep[u["idx"]] = len(u["deps"])
            for d in u["deps"]:
                succ[d].append(u["idx"])
        per_eng = {e: [u["idx"] for u in units if u["eng"] == e] for e in ENGS}
        pos = {e: 0 for e in ENGS}
        done = [False] * n
        fin = [0.0] * n
        free = {e: 0.0 for e in ENGS}
        order = []
        remaining = n
        while remaining:
            best = None
            for e in ENGS:
                lst = per_eng[e]
                p = pos[e]
                while p < len(lst) and done[lst[p]]:
                    p += 1
                pos[e] = p
                cnt = 0
                q = p
                win = 1 if e == "pe" else window
                while q < len(lst) and cnt < win:
                    ui = lst[q]
                    q += 1
                    if done[ui]:
                        continue
                    cnt += 1
                    if ndep[ui]:
                        continue
                    u = units[ui]
                    st = free[e]
                    for d in u["deps"]:
                        t = fin[d] + lat
                        if t > st:
                            st = t
                    if best is None or st < best[0] or (st == best[0] and ui < best[1]):
                        best = (st, ui)
                    if st <= free[e]:
                        break
            st, ui = best
            u = units[ui]
            e = u["eng"]
            t = st
            for o in u["ops"]:
                t += o["dur"]
            free[e] = t
            fin[ui] = st + sum(o["dur"] for o in u["ops"][:-1]) + u["ops"][-1]["avail"]
            done[ui] = True
            remaining -= 1
            for sidx in succ[ui]:
                ndep[sidx] -= 1
            order.append(ui)
        self.est_ns = max(fin)
        for ui in order:
            for o in units[ui]["ops"]:
                self._op(o["eng"], o["issue"], o["reads"], o["writes"], o["signal"], o["dma_sem"])

    def emit(self, nc):
        import os as _os
        self.schedule(window=int(_os.environ.get("KWINDOW", "96")))
        for eng, toks in self.finals:
            self.ops[eng].append((toks, None, None))
        names = [e for e in ENGS if e != "sp"] + self.dma_sems
        with contextlib.ExitStack() as st:
            sems = {n: st.enter_context(nc.semaphore("s_" + n)) for n in names}
            block = st.enter_context(nc.Block())

            def replay(name, e):
                for waits, issue, inc in self.ops[name]:
                    for k, v in waits:
                        e.wait_ge(sems[k], v)
                    if issue is None:
                        continue
                    ins = issue(e)
                    if inc is not None:
                        ins.then_inc(sems[inc[0]], inc[1])

            @block.tensor
            def _(e):
                replay("pe", e)

            @block.scalar
            def _(e):
                replay("act", e)

            @block.vector
            def _(e):
                replay("dve", e)

            @block.gpsimd
            def _(e):
                replay("pool", e)

            @block.sync
            def _(e):
                replay("sp", e)
                for n_ in names:
                    if self.cnt[n_] > 0:
                        e.wait_ge(sems[n_], self.cnt[n_])
                for n_ in names:
                    e.sem_clear(sems[n_])


class V:
    __slots__ = ("ap", "bufs")

    def __init__(self, ap, bufs):
        self.ap = ap
        self.bufs = list(bufs)

    def __getitem__(self, k):
        return V(self.ap[k], self.bufs)

    def bitcast(self, dt):
        return V(self.ap.bitcast(dt), self.bufs)

    def rearrange(self, s, **kw):
        return V(self.ap.rearrange(s, **kw), self.bufs)

    def bcast3(self, n):
        return V(self.ap.unsqueeze(2).broadcast_to(list(self.ap.shape) + [n]), self.bufs)


def plan(layers):
    ch = []
    for l in layers:
        if l == 0:
            for hp in range(4):
                for k in ("q", "f", "i", "g"):
                    ch.append(("hg_in", k, hp))
            for hp in range(4):
                ch.append(("hg_out", hp))
        else:
            ch.append(("gla_lr",))
            for hd in range(4):
                ch.append(("gla_qk", hd))
                ch.append(("gla_v", hd))
                ch.append(("gla_g", hd))
            for hd in range(4):
                ch.append(("gla_out", hd))
        for g in range(2):
            cs = list(range(g * 11, (g + 1) * 11))
            for c in cs:
                ch.append(("up", l, c))
            for half in range(2):
                for part in (cs[0:4], cs[4:8], cs[8:11]):
                    ch.append(("down", l, half, tuple(part)))
    return ch


def _cols(W, cols):
    sub = W[:, cols].reshape(8, 128, len(cols)).transpose(1, 0, 2)
    return sub.reshape(128, -1)


def pack_wstream(inp, layers):
    chunks = plan(layers)
    out = np.zeros((len(chunks), 128, 2048), np.float32)
    hg_in = inp["hg_w_in"][0]
    hg_out = inp["hg_w_out"][0]
    gl_in = inp["gla_w_in"][0]
    gl_out = inp["gla_w_out"][0]
    for i, c in enumerate(chunks):
        kind = c[0]
        if kind == "hg_in":
            base = {"q": 0, "f": 1024, "i": 2048, "g": 3072}[c[1]] + c[2] * 256
            out[i] = _cols(hg_in, np.arange(base, base + 256))
        elif kind == "hg_out":
            r0 = c[1] * 256
            out[i] = hg_out[r0:r0 + 256].reshape(2, 128, 1024).transpose(1, 0, 2).reshape(128, 2048)
        elif kind == "gla_lr":
            out[i, :, :1024] = _cols(gl_in, np.arange(2960, 3088))
        elif kind == "gla_qk":
            hd = c[1]
            cols = np.concatenate([np.arange(hd * 128, hd * 128 + 128), np.arange(512 + hd * 128, 512 + hd * 128 + 128)])
            out[i] = _cols(gl_in, cols)
        elif kind == "gla_v":
            out[i] = _cols(gl_in, np.arange(1024 + c[1] * 256, 1024 + c[1] * 256 + 256))
        elif kind == "gla_g":
            out[i] = _cols(gl_in, np.arange(2048 + c[1] * 256, 2048 + c[1] * 256 + 256))
        elif kind == "gla_out":
            r0 = c[1] * 256
            out[i] = gl_out[r0:r0 + 256].reshape(2, 128, 1024).transpose(1, 0, 2).reshape(128, 2048)
        elif kind == "up":
            l, ct = c[1], c[2]
            cols = np.concatenate([np.arange(ct * 128, ct * 128 + 128), np.arange(DFF + ct * 128, DFF + ct * 128 + 128)])
            out[i] = _cols(inp["ffn_w_up"][l], cols)
        elif kind == "down":
            l, half, part = c[1], c[2], c[3]
            wd = inp["ffn_w_down"][l]
            for j, ct in enumerate(part):
                out[i, :, j * 512:(j + 1) * 512] = wd[ct * 128:(ct + 1) * 128, half * 512:(half + 1) * 512]
    return out


_CO = {}
_off = 0
for _n, _w in (("nm", 16), ("nf", 16), ("nfin", 8), ("lbl", 24), ("hgnw", 1), ("bgk", 4), ("glanw", 2),
               ("cw", 264), ("cb", 88), ("eps", 1), ("one", 1), ("lnc", 1), ("ident", 128), ("cmask", 128),
               ("smask", 512), ("w2", 512)):
    _CO[_n] = _off
    _off += _w
NCONST = _off


def pack_consts(inp):
    c = np.zeros((128, NCONST), np.float32)

    def put(name, arr):
        arr = np.asarray(arr, np.float32)
        c[:, _CO[name]:_CO[name] + arr.shape[1]] = arr

    pd = lambda v: np.asarray(v, np.float32).reshape(-1, 128).T
    put("nm", np.concatenate([pd(inp["norm_mixer_w"][l]) for l in range(2)], axis=1))
    put("nf", np.concatenate([pd(inp["norm_ffn_w"][l]) for l in range(2)], axis=1))
    put("nfin", pd(inp["norm_final_w"]))
    put("lbl", np.concatenate([pd(inp["lb_logits"][k]) for k in range(3)], axis=1))
    put("hgnw", pd(inp["hg_norm_w"][0]))
    put("bgk", pd(inp["gla_b_gk_up"][0]))
    put("glanw", pd(inp["gla_norm_w"][0]))
    cw = np.asarray(inp["ffn_conv_w"], np.float32)
    put("cw", np.concatenate([pd(cw[l, j]) for l in range(2) for j in range(3)], axis=1))
    cb = np.asarray(inp["ffn_conv_b"], np.float32)
    put("cb", np.concatenate([pd(cb[l]) for l in range(2)], axis=1))
    put("eps", np.full((128, 1), 1e-6, np.float32))
    put("one", np.ones((128, 1), np.float32))
    put("lnc", np.full((128, 1), np.log(128.0 ** -0.5), np.float32))
    put("ident", np.eye(128, dtype=np.float32))
    s = np.arange(128)[:, None]
    t = np.arange(128)[None, :]
    put("cmask", ((s // 64 == t // 64) & (s <= t)).astype(np.float32))
    sm = np.ones((128, 512), np.float32)
    sm[:, ::64] = 0.0
    put("smask", sm)
    w2 = np.zeros((128, 512), np.float32)
    w2[112:128, :] = np.asarray(inp["gla_w_gk_up"][0], np.float32)
    put("w2", w2)
    return c


def build(NT, layers=(0, 1), final_norm=True, same_sync=("act", "dve", "pool")):
    nc = bass.Bass("TRN2", target_bir_lowering=False)
    nc.dge_precook = False
    chunks = plan(layers)
    NCH = len(chunks)
    xT_d = nc.dram_tensor("xT", [NT, 128, 8 * TT], F32, kind="ExternalInput").ap()
    ws_d = nc.dram_tensor("wstream", [NCH, 128, 2048], F32R, kind="ExternalInput").ap()
    cs_d = nc.dram_tensor("consts", [128, NCONST], F32, kind="ExternalInput").ap()
    out_d = nc.dram_tensor("outT", [NT, 128, 8 * TT], F32, kind="ExternalOutput").ap()

    R = Rec(same_engine_sync=same_sync)
    with contextlib.ExitStack() as st:
        def sbt(name, shape, dt=F32, nbuf=1):
            t = st.enter_context(nc.sbuf_tensor(name, shape, dt))
            return t, [Buf(f"{name}{i}") for i in range(nbuf)]

        def tile(name, shape, dt=F32):
            t, b = sbt(name, shape, dt)
            return V(t[:], b)

        hT_t, hT_b = sbt("hT", [128, 8, TT], F32, 8)
        yT_t, yT_b = sbt("yT", [128, 8, TT], F32R, 8)
        oT_t, oT_b = sbt("oT", [128, 8, TT], F32R, 8)
        aT_t, aT_b = sbt("aT", [128, 11, TT], F32R, 11)
        hT = [V(hT_t[:, i, :], [hT_b[i]]) for i in range(8)]
        yT = [V(yT_t[:, i, :], [yT_b[i]]) for i in range(8)]
        oT = [V(oT_t[:, i, :], [oT_b[i]]) for i in range(8)]
        aT = [V(aT_t[:, i, :], [aT_b[i]]) for i in range(11)]
        hT_all = V(hT_t[:], hT_b)
        CS = tile("consts_sb", [128, NCONST])
        slots = [tile(f"slot{i}", [128, 2048], F32R) for i in range(NSLOT)]
        slot_sems = [R.new_dma_sem(f"ws{i}") for i in range(NSLOT)]

        def cst(name, j=0, w=1):
            o = _CO[name] + j
            return CS[:, o:o + w]

        ones1024 = tile("ones1024", [128, 128], F32R)
        ones128 = tile("ones128", [128, 128], F32R)
        ones256 = tile("ones256", [128, 128], F32R)
        ident_bf = tile("ident_bf", [128, 128], BF16)
        lb = tile("lb", [128, 8])
        clb = tile("clb", [128, 8])
        lbe = tile("lbe", [128, 24])
        nbgk = tile("nbgk", [128, 4])
        W2r = tile("W2r", [128, 512], F32R)
        G_sb = tile("G_sb", [128, TT], F32R)
        rstd = tile("rstd", [128, TT])
        S32hg = [tile(f"S32hg{h}", [128, 128]) for h in range(8)]
        Sbfhg = [tile(f"Sbfhg{h}", [128, 128], BF16) for h in range(8)]
        S32gl = [tile(f"S32gl{h}", [128, 256]) for h in range(4)]
        Sbfgl = [tile(f"Sbfgl{h}", [128, 256], BF16) for h in range(4)]
        tails_t = st.enter_context(nc.sbuf_tensor("tails", [128, 2, 44, 2], F32))
        tails = [[V(tails_t[:, l, ct, :], [Buf(f"tail{l}_{ct}")]) for ct in range(44)] for l in range(2)]
        TM = []
        for s in range(2):
            d = {}
            for n in ("C", "D", "SG1", "E"):
                d[n] = tile(f"t{n}{s}", [128, TT])
            d["XA"] = tile(f"tXA{s}", [128, TT + 2])
            d["XG"] = tile(f"tXG{s}", [128, TT + 2])
            d["A"] = d["XA"][:, 0:TT]
            d["B"] = d["XG"][:, 0:TT]
            d["UA"] = d["C"]
            d["UG"] = d["D"]
            d["OSQ0"] = tile(f"tOSQ0{s}", [128, TT], F32R)
            d["OSQ1"] = tile(f"tOSQ1{s}", [128, TT], F32R)
            for n in ("kT", "kkT", "qT"):
                d[n] = tile(f"t{n}{s}", [128, TT], BF16)
            d["kktok"] = tile(f"tkktok{s}", [128, 4, 128], BF16)
            d["vtok"] = tile(f"tvtok{s}", [128, 4, 256], BF16)
            d["scT"] = [tile(f"tscT{s}{i}", [128, 128], BF16) for i in range(2)]
            d["ebl"] = tile(f"tebl{s}", [128, 8])
            TM.append(d)
        banks = []
        for i in range(8):
            t = st.enter_context(nc.psum_tensor(f"bank{i}", [128, TT], F32))
            banks.append((t, [Buf(f"bk{i}a"), Buf(f"bk{i}b")]))
        pstate = {"big": 0, "sc": 0, "tr": 0, "nbig": 6}

        def pbank():
            n = pstate["nbig"]
            i = pstate["big"] % n
            pstate["big"] += 1
            t, b = banks[i]
            return V(t[:], b)

        def psc_region():
            h = pstate["sc"] % 2
            pstate["sc"] += 1
            t, b = banks[6]
            return V(t[:, h * 128:(h + 1) * 128], [b[0]])

        def pU_region(nv):
            t, b = banks[6]
            return V(t[:, 256:256 + nv], [b[1]])

        def ptrans():
            h = pstate["tr"] % 2
            pstate["tr"] += 1
            t, b = banks[7]
            return V(t[:, h * 256:(h + 1) * 256].bitcast(BF16), [b[h]])

        def bufs_of(*vs):
            r = []
            for v in vs:
                if isinstance(v, V):
                    r += v.bufs
            return r

        def apof(v):
            return v.ap if isinstance(v, V) else v

        def mm(out, lhsT, rhs, start=True, stop=True, signal=None):
            sig = stop if signal is None else signal
            R.op("pe", lambda e: e.matmul(out.ap, lhsT.ap, rhs.ap, start=start, stop=stop),
                 reads=lhsT.bufs + rhs.bufs, writes=out.bufs, signal=sig,
                 dur=max(64, rhs.ap.shape[-1]) / 2.2 + 20)

        def tr(out, in_):
            R.op("pe", lambda e: e.transpose(out.ap, in_.ap, ident_bf.ap),
                 reads=in_.bufs + ident_bf.bufs, writes=out.bufs, signal=True, dur=90)

        def nfree(v):
            n = 1
            for d in v.ap.shape[1:]:
                n *= d
            return n

        def edur(eng, v, mult=1.0):
            n = nfree(v) * mult
            if eng == "act":
                return 220 + 0.85 * n
            if eng == "dve":
                return 120 + 1.0 * n
            return 150 + 2.3 * n

        def ACT(out, in_, func, bias=None, scale=None):
            kw = {}
            if bias is not None:
                kw["bias"] = apof(bias)
            if scale is not None:
                kw["scale"] = apof(scale)
            R.op("act", lambda e: e.activation(out=out.ap, in_=in_.ap, func=func, **kw),
                 reads=bufs_of(in_, bias, scale), writes=out.bufs, dur=edur("act", out))

        def TTo(eng, out, in0, in1, op):
            R.op(eng, lambda e: e.tensor_tensor(out=out.ap, in0=in0.ap, in1=in1.ap, op=op),
                 reads=bufs_of(in0, in1), writes=out.bufs, dur=edur(eng, out))

        def TS(eng, out, in0, s1, op0, s2=None, op1=None):
            kw = {}
            if op1 is None and eng == "pool":
                if op0 == ALU.add:
                    s2, op1 = 1.0, ALU.mult
                elif op0 == ALU.mult:
                    s2, op1 = 0.0, ALU.add
            if op1 is not None:
                kw["op1"] = op1
            R.op(eng, lambda e: e.tensor_scalar(out=out.ap, in0=in0.ap, scalar1=apof(s1), scalar2=apof(s2), op0=op0, **kw),
                 reads=bufs_of(in0, s1, s2), writes=out.bufs, dur=edur(eng, out))

        def STT(eng, out, in0, scalar, in1, op0, op1):
            R.op(eng, lambda e: e.scalar_tensor_tensor(out=out.ap, in0=in0.ap, scalar=apof(scalar), in1=in1.ap, op0=op0, op1=op1),
                 reads=bufs_of(in0, scalar, in1), writes=out.bufs, dur=edur(eng, out))

        def SCAN(out, d0, d1):
            R.op("dve", lambda e: e.tensor_tensor_scan(out=out.ap, data0=d0.ap, data1=d1.ap, initial=0.0, op0=ALU.mult, op1=ALU.add),
                 reads=bufs_of(d0, d1), writes=out.bufs, dur=edur("dve", out, 2.0))

        def RECIP(eng, out, in_, exact=True):
            R.op(eng, lambda e: e.reciprocal(out=out.ap, in_=in_.ap), reads=in_.bufs, writes=out.bufs, dur=edur(eng, out, 6.0))

        def CP(eng, out, in_):
            if eng == "act":
                R.op("act", lambda e: e.copy(out=out.ap, in_=in_.ap), reads=in_.bufs, writes=out.bufs, dur=edur("act", out))
            else:
                R.op(eng, lambda e: e.tensor_copy(out=out.ap, in_=in_.ap), reads=in_.bufs, writes=out.bufs, dur=edur(eng, out))

        def MEMSET(eng, out, val):
            R.op(eng, lambda e: e.memset(out.ap, val), writes=out.bufs, dur=edur(eng, out, 0.5))

        wst = {"next_dma": 0, "cur": 0}
        total_chunks = NCH * NT

        def ws_get(kind, lag=1):
            j = wst["cur"]
            wst["cur"] += 1
            assert chunks[j % NCH][0] == kind, (chunks[j % NCH], kind)
            upto = min(total_chunks - 1, j + NSLOT - lag)
            while wst["next_dma"] <= upto:
                m = wst["next_dma"]
                wst["next_dma"] += 1
                k = m % NCH
                sl = slots[m % NSLOT]
                if chunks[k][0] == "gla_lr":
                    R.op("sp", lambda e, sl=sl, k=k: e.dma_start(out=sl.ap[:, 0:1024], in_=ws_d[k, :, 0:1024]),
                         writes=sl.bufs, dma_sem=slot_sems[m % NSLOT], dur=600, avail=4000)
                else:
                    R.op("sp", lambda e, sl=sl, k=k: e.dma_start(out=sl.ap, in_=ws_d[k]),
                         writes=sl.bufs, dma_sem=slot_sems[m % NSLOT], dur=600, avail=5500)
            return slots[j % NSLOT]

        s_c = R.new_dma_sem("ld_c")
        s_h = R.new_dma_sem("ld_h")
        s_o = R.new_dma_sem("st_o")
        R.op("sp", lambda e: e.dma_start(out=CS.ap, in_=cs_d), writes=CS.bufs, dma_sem=s_c, dur=600, avail=4000)
        onesf = tile("onesf", [128, 128])
        for ot, val in ((ones1024, 1.0 / 1024), (ones128, 1.0 / 128), (ones256, 1.0 / 256)):
            MEMSET("pool", onesf, val)
            CP("act", ot, onesf)
        for h in range(8):
            MEMSET("pool", S32hg[h], 0.0)
            MEMSET("pool", Sbfhg[h], 0.0)
        for h in range(4):
            MEMSET("pool", S32gl[h], 0.0)
            MEMSET("pool", Sbfgl[h], 0.0)
        tails_all = V(tails_t[:], [tails[l][ct].bufs[0] for l in range(2) for ct in range(44)])
        MEMSET("pool", tails_all, 0.0)
        CP("act", ident_bf, cst("ident", 0, 128))
        CP("act", W2r, cst("w2", 0, 512))
        ACT(lbe, cst("lbl", 0, 24), AF.Exp)
        TTo("pool", lb, lbe[:, 0:8], lbe[:, 8:16], ALU.add)
        TTo("pool", lb, lb, lbe[:, 16:24], ALU.add)
        RECIP("dve", lb, lb, exact=True)
        TTo("dve", lb, lb, lbe[:, 0:8], ALU.mult)
        ACT(clb, lb, AF.Ln, scale=-1.0, bias=cst("one"))
        TS("pool", nbgk, cst("bgk", 0, 4), -1.0, ALU.mult)
        one = cst("one")
        eps = cst("eps")
        cmask = cst("cmask", 0, 128)
        smask = cst("smask", 0, 512)

        def rmsnorm(wname, woff, dst):
            for dt in range(8):
                ACT(yT[dt], hT[dt], AF.Square)
            pb = pbank()
            for dt in range(8):
                mm(pb, ones1024, yT[dt], start=(dt == 0), stop=(dt == 7))
            ACT(rstd, pb, AF.Ln, bias=eps)
            ACT(rstd, rstd, AF.Exp, scale=-0.5)
            for dt in range(8):
                if dt % 2 == 0:
                    STT("dve", dst[dt], hT[dt], cst(wname, woff + dt), rstd, ALU.mult, ALU.mult)
                else:
                    ACT(dst[dt], hT[dt], AF.Identity, scale=cst(wname, woff + dt))
                    TTo("pool", dst[dt], dst[dt], rstd, ALU.mult)

        def out_proj(kind):
            so = [ws_get(kind, lag=i + 1) for i in range(4)]
            for ft in range(8):
                pw = pbank()
                for h in range(8):
                    w = so[h // 2].rearrange("p (a f) -> p a f", a=2)[:, h % 2, ft * 128:(ft + 1) * 128]
                    mm(pw, w, oT[h], start=(h == 0), stop=(h == 7))
                TTo("dve", hT[ft], hT[ft], pw, ALU.add)

        def recurrence(tm, po, n_vt, Sbf, S32, vcol0):
            kT, kkT, qT, kktok, vtok, ebl = tm["kT"], tm["kkT"], tm["qT"], tm["kktok"], tm["vtok"], tm["ebl"]
            nv = 128 * n_vt
            for blk in range(4):
                cols = slice(blk * 128, (blk + 1) * 128)
                psc = psc_region()
                mm(psc, kT[:, cols], qT[:, cols])
                scT = tm["scT"][blk % 2]
                TTo("dve", scT, psc, cmask, ALU.mult)
                for vt in range(n_vt):
                    mm(po[vt][:, cols], vtok[:, blk, vcol0 + vt * 128:vcol0 + (vt + 1) * 128], scT,
                       start=True, stop=False, signal=False)
                for ci in range(2):
                    c = blk * 2 + ci
                    ccols = slice(c * 64, (c + 1) * 64)
                    rows = slice(ci * 64, (ci + 1) * 64)
                    for vt in range(n_vt):
                        mm(po[vt][:, ccols], Sbf[:, vt * 128:(vt + 1) * 128], qT[:, ccols],
                           start=False, stop=True, signal=(vt == n_vt - 1))
                    pU = pU_region(nv)
                    mm(pU, kktok[rows, blk, :], vtok[rows, blk, vcol0:vcol0 + nv])
                    STT("dve", S32, S32, ebl[:, c:c + 1], pU, ALU.mult, ALU.add)
                    CP("act", Sbf, S32)
                yield

        def kk_transposes(tm):
            ptr = ptrans()
            for blk in range(4):
                tr(ptr[:, blk * 128:(blk + 1) * 128], tm["kkT"][:, blk * 128:(blk + 1) * 128])
            CP("act", tm["kktok"].rearrange("p a b -> p (a b)"), ptr)

        def silu_gate(dst, pg):
            ACT(dst, pg, AF.Exp, scale=-1.0)
            ACT(dst, dst, AF.Ln, bias=one)
            ACT(dst, dst, AF.Exp, scale=-1.0)
            TTo("dve", dst, pg, dst, ALU.mult)

        def mixer_hg():
            rmsnorm("nm", 0, yT)
            for hp in range(4):
                sq = ws_get("hg_in", 1).rearrange("p (dt f) -> p dt f", dt=8)
                sf = ws_get("hg_in", 2).rearrange("p (dt f) -> p dt f", dt=8)
                si = ws_get("hg_in", 3).rearrange("p (dt f) -> p dt f", dt=8)
                sg = ws_get("hg_in", 4).rearrange("p (dt f) -> p dt f", dt=8)
                tmv = TM[hp % 2]
                for blk in range(4):
                    pv = pbank()[:, 0:256]
                    for dt in range(8):
                        mm(pv, yT[dt][:, blk * 128:(blk + 1) * 128], si[:, dt, :], start=(dt == 0), stop=(dt == 7))
                    CP("act", tmv["vtok"][:, blk, :], pv)
                tms = []
                for a in range(2):
                    h = 2 * hp + a
                    tm = dict(TM[h % 2])
                    tm["vtok"] = tmv["vtok"]
                    tms.append(tm)
                    A_, B_, C_, D_ = tm["A"], tm["B"], tm["C"], tm["D"]
                    fc = slice(a * 128, (a + 1) * 128)
                    pf = pbank()
                    for dt in range(8):
                        mm(pf, sf[:, dt, fc], yT[dt], start=(dt == 0), stop=(dt == 7))
                    pq = pbank()
                    for dt in range(8):
                        mm(pq, sq[:, dt, fc], yT[dt], start=(dt == 0), stop=(dt == 7))
                    pg = pbank()
                    for dt in range(8):
                        mm(pg, sg[:, dt, fc], yT[dt], start=(dt == 0), stop=(dt == 7))
                    ACT(A_, pf, AF.Exp, scale=-1.0)
                    ACT(B_, A_, AF.Ln, bias=one)
                    ACT(C_, A_, AF.Ln, scale=lb[:, h:h + 1], bias=one)
                    TTo("pool", C_, C_, B_, ALU.subtract)
                    SCAN(D_, smask, C_)
                    TTo("dve", A_, pf, B_, ALU.add)
                    TTo("pool", A_, A_, D_, ALU.add)
                    ACT(tm["kT"], A_, AF.Exp, scale=-1.0, bias=clb[:, h:h + 1])
                    ACT(tm["ebl"], D_[:, 63::64], AF.Exp)
                    TTo("pool", tm["kkT"].rearrange("p (c k) -> p c k", k=64),
                        tm["kT"].rearrange("p (c k) -> p c k", k=64), tm["ebl"].bcast3(64), ALU.mult)
                    kk_transposes(tm)
                    ACT(C_, pq, AF.Exp, scale=-1.0)
                    ACT(C_, C_, AF.Ln, bias=one)
                    TTo("pool", C_, D_, C_, ALU.subtract)
                    ACT(C_, C_, AF.Exp, bias=cst("lnc"))
                    TTo("dve", tm["qT"], pq, C_, ALU.mult)
                    silu_gate(A_, pg)
                pos = [pbank(), pbank()]
                gens = [recurrence(tms[a], [pos[a]], 1, Sbfhg[2 * hp + a], S32hg[2 * hp + a], a * 128) for a in range(2)]
                for blk in range(4):
                    for g in gens:
                        next(g)
                for a in range(2):
                    h = 2 * hp + a
                    tm = tms[a]
                    A_, C_, D_ = tm["A"], tm["C"], tm["D"]
                    po = pos[a]
                    ACT(tm["OSQ0"], po, AF.Square)
                    pss = pbank()
                    mm(pss, ones128, tm["OSQ0"])
                    ACT(C_, pss, AF.Ln, bias=eps)
                    ACT(C_, C_, AF.Exp, scale=-0.5)
                    STT("dve", D_, po, cst("hgnw"), C_, ALU.mult, ALU.mult)
                    TTo("pool", oT[h], D_, A_, ALU.mult)
            out_proj("hg_out")

        def mixer_gla():
            rmsnorm("nm", 8, yT)
            slr = ws_get("gla_lr", 1)[:, 0:1024].rearrange("p (dt f) -> p dt f", dt=8)
            pG = pbank()
            for dt in range(8):
                mm(pG, slr[:, dt, :], yT[dt], start=(dt == 0), stop=(dt == 7))
            CP("act", G_sb, pG)
            for hd0 in (0, 2):
                st = []
                for hd in (hd0, hd0 + 1):
                    sqk = ws_get("gla_qk", 1).rearrange("p (dt two f) -> p dt two f", dt=8, two=2)
                    sv = ws_get("gla_v", 2).rearrange("p (dt f) -> p dt f", dt=8)
                    sg = ws_get("gla_g", 3).rearrange("p (dt f) -> p dt f", dt=8)
                    tm = TM[hd % 2]
                    A_, B_, C_, D_ = tm["A"], tm["B"], tm["C"], tm["D"]
                    SG = [tm["E"], tm["SG1"]]
                    for vt in range(2):
                        pg = pbank()
                        for dt in range(8):
                            mm(pg, sg[:, dt, vt * 128:(vt + 1) * 128], yT[dt], start=(dt == 0), stop=(dt == 7))
                        silu_gate(SG[vt], pg)
                    for blk in range(4):
                        pv = pbank()[:, 0:256]
                        for dt in range(8):
                            mm(pv, yT[dt][:, blk * 128:(blk + 1) * 128], sv[:, dt, :], start=(dt == 0), stop=(dt == 7))
                        CP("act", tm["vtok"][:, blk, :], pv)
                    pgk = pbank()
                    mm(pgk, W2r[:, hd * 128:(hd + 1) * 128], G_sb)
                    ACT(A_, pgk, AF.Exp, scale=-1.0, bias=nbgk[:, hd:hd + 1])
                    ACT(A_, A_, AF.Ln, bias=one)
                    SCAN(D_, smask, A_)
                    ACT(B_, D_, AF.Exp, scale=-1.0 / 16)
                    ACT(C_, D_, AF.Exp, scale=1.0 / 16)
                    ACT(tm["ebl"], D_[:, 63::64], AF.Exp, scale=-1.0 / 16)
                    pq = pbank()
                    for dt in range(8):
                        mm(pq, sqk[:, dt, 0, :], yT[dt], start=(dt == 0), stop=(dt == 7))
                    pk = pbank()
                    for dt in range(8):
                        mm(pk, sqk[:, dt, 1, :], yT[dt], start=(dt == 0), stop=(dt == 7))
                    STT("dve", tm["qT"], pq, 128 ** -0.5, B_, ALU.mult, ALU.mult)
                    TTo("dve", tm["kT"], pk, C_, ALU.mult)
                    TTo("pool", tm["kkT"].rearrange("p (c k) -> p c k", k=64),
                        tm["kT"].rearrange("p (c k) -> p c k", k=64), tm["ebl"].bcast3(64), ALU.mult)
                    kk_transposes(tm)
                    st.append((hd, tm, SG))
                pos = [[pbank(), pbank()] for _ in st]
                gens = [recurrence(tm, pos[i], 2, Sbfgl[hd], S32gl[hd], 0) for i, (hd, tm, SG) in enumerate(st)]
                for blk in range(4):
                    for g in gens:
                        next(g)
                for i, (hd, tm, SG) in enumerate(st):
                    C_, D_ = tm["C"], tm["D"]
                    po = pos[i]
                    OSQ = [tm["OSQ0"], tm["OSQ1"]]
                    for vt in range(2):
                        ACT(OSQ[vt], po[vt], AF.Square)
                    pss = pbank()
                    mm(pss, ones256, OSQ[0], start=True, stop=False)
                    mm(pss, ones256, OSQ[1], start=False, stop=True)
                    ACT(C_, pss, AF.Ln, bias=eps)
                    ACT(C_, C_, AF.Exp, scale=-0.5)
                    for vt in range(2):
                        STT("dve", D_, po[vt], cst("glanw", vt), C_, ALU.mult, ALU.mult)
                        TTo("pool", oT[hd * 2 + vt], D_, SG[vt], ALU.mult)
            out_proj("gla_out")

        def ffn(l):
            rmsnorm("nf", 8 * l, yT)
            pstate["nbig"] = 8
            for g in range(2):
                for ci in range(11):
                    c = g * 11 + ci
                    su = ws_get("up", 1).rearrange("p (dt two f) -> p dt two f", dt=8, two=2)
                    tm = TM[ci % 2]
                    pa = pbank()
                    for dt in range(8):
                        mm(pa, su[:, dt, 0, :], yT[dt], start=(dt == 0), stop=(dt == 7))
                    pg = pbank()
                    for dt in range(8):
                        mm(pg, su[:, dt, 1, :], yT[dt], start=(dt == 0), stop=(dt == 7))
                    for (ps, ct, xs, u, e1, e2) in ((pa, c, tm["XA"], tm["UA"], "dve", "dve"),
                                                     (pg, NCT + c, tm["XG"], tm["UG"], "dve", "dve")):
                        cw = lambda j, ct=ct: cst("cw", (l * 3 + j) * 44 + ct)
                        CP("act", xs[:, 2:TT + 2], ps)
                        CP("pool", xs[:, 0:2], tails[l][ct])
                        ACT(u, xs[:, 0:TT], AF.Identity, scale=cw(0), bias=cst("cb", l * 44 + ct))
                        STT(e1, u, xs[:, 1:TT + 1], cw(1), u, ALU.mult, ALU.add)
                        STT(e2, u, xs[:, 2:TT + 2], cw(2), u, ALU.mult, ALU.add)
                        CP("pool", tails[l][ct], xs[:, TT:TT + 2])
                    E = tm["E"]
                    ACT(E, tm["UG"], AF.Silu)
                    TTo("pool", aT[ci], E, tm["UA"], ALU.mult)
                for half in range(2):
                    accs = [pbank() for _ in range(4)]
                    for part in (range(0, 4), range(4, 8), range(8, 11)):
                        sd = ws_get("down", 1).rearrange("p (i f) -> p i f", i=4)
                        for i, ci in enumerate(part):
                            for j in range(4):
                                mm(accs[j], sd[:, i, j * 128:(j + 1) * 128], aT[ci], start=(ci == 0), stop=(ci == 10))
                    for j in range(4):
                        ft = half * 4 + j
                        TTo("dve", hT[ft], hT[ft], accs[j], ALU.add)
            pstate["nbig"] = 6
            pstate["big"] = 0

        out_toks = []
        for tI in range(NT):
            t0 = tI * TT
            R.op("sp", lambda e, tI=tI: e.dma_start(out=hT_t[:], in_=xT_d[tI].rearrange("p (dt t) -> p dt t", dt=8)),
                 writes=hT_b, dma_sem=s_h, dur=600, avail=9000)
            for l in layers:
                if l == 0:
                    mixer_hg()
                else:
                    mixer_gla()
                ffn(l)
            if final_norm:
                rmsnorm("nfin", 0, hT)
            tok = R.op("sp", lambda e, tI=tI: e.dma_start(out=out_d[tI].rearrange("p (dt t) -> p dt t", dt=8), in_=hT_t[:]),
                       reads=hT_b, dma_sem=s_o, dur=600, avail=6000)
        R.final_wait("sp", [(s_o, 16 * NT)])
        assert wst["cur"] == total_chunks
        R.emit(nc)
    return nc


_PROGS = {}


def _prog(NT, layers, final_norm):
    key = (NT, tuple(layers), final_norm)
    if key not in _PROGS:
        _PROGS[key] = build(NT, layers, final_norm)
    return _PROGS[key]


FUSED = True


def to_tiles(xb):
    nt = xb.shape[0] // TT
    return np.ascontiguousarray(xb.reshape(nt, TT, 8, 128).transpose(0, 3, 2, 1)).reshape(nt, 128, 8 * TT)


def from_tiles(o):
    nt = o.shape[0]
    return np.ascontiguousarray(o.reshape(nt, 128, 8, TT).transpose(0, 3, 2, 1)).reshape(nt * TT, D)


def kernel(**inputs):
    inp = {k: np.asarray(v) for k, v in inputs.items()}
    x = inp["x"].astype(np.float32, copy=False)
    B = x.shape[0]
    consts = pack_consts(inp)
    xT = [to_tiles(x[b]) for b in range(B)]
    if FUSED:
        stages = [((0, 1), True)]
    else:
        stages = [((0,), False), ((1,), True)]
    cur = xT
    for layers, fin in stages:
        ws = pack_wstream(inp, layers)
        nc = _prog(T // TT, layers, fin)
        in_maps = [{"xT": cur[b], "wstream": ws, "consts": consts} for b in range(B)]
        res = run_bass_kernel_spmd(nc, in_maps, core_ids=list(range(B)))
        cur = [np.asarray(res.results[b]["outT"]) for b in range(B)]
    out = np.stack([from_tiles(cur[b]) for b in range(B)], axis=0)
    return out.astype(np.float32, copy=False)
```

```python
import contextlib
import numpy as np
import concourse.bass as bass
import concourse.mybir as mybir
from concourse.bass_utils import run_bass_kernel_spmd

F32 = mybir.dt.float32
F32R = mybir.dt.float32r
BF16 = mybir.dt.bfloat16
AF = mybir.ActivationFunctionType
ALU = mybir.AluOpType

D = 1024
T = 4096
TT = 512
DFF = 2816
NCT = 22
NSLOT = 8
ENGS = ("pe", "act", "dve", "pool", "sp")


class Buf:
    __slots__ = ("name", "last_w", "readers")

    def __init__(self, name=""):
        self.name = name
        self.last_w = None
        self.readers = {}


class Rec:
    def __init__(self, same_engine_sync=("act", "dve", "pool")):
        self.ops = {e: [] for e in ENGS}
        self.cnt = {e: 0 for e in ENGS}
        self.waited = {}
        self.same_sync = set(same_engine_sync)
        self.dma_sems = []
        self.prog = []
        self.finals = []

    def new_dma_sem(self, name):
        self.cnt[name] = 0
        self.dma_sems.append(name)
        return name

    def op(self, eng, issue, reads=(), writes=(), signal=True, dma_sem=None, force=False, dur=500.0, avail=None):
        self.prog.append(dict(eng=eng, issue=issue, reads=list(reads), writes=list(writes), signal=signal,
                              dma_sem=dma_sem, dur=float(dur), avail=float(avail if avail is not None else dur)))
        return None

    def _op(self, eng, issue, reads=(), writes=(), signal=True, dma_sem=None):
        need = {}

        def add(tok):
            if tok is None:
                return
            k, v = tok
            if need.get(k, 0) < v:
                need[k] = v

        for b in reads:
            add(b.last_w)
        for b in writes:
            add(b.last_w)
            for k, v in b.readers.items():
                add((k, v))
        waits = []
        for k, v in need.items():
            if k == eng and eng not in self.same_sync:
                continue
            if k == eng and v > self.cnt[eng]:
                continue
            if self.waited.get((eng, k), 0) >= v:
                continue
            self.waited[(eng, k)] = v
            waits.append((k, v))
        if dma_sem is not None:
            self.cnt[dma_sem] += 16
            tok = (dma_sem, self.cnt[dma_sem])
            inc = (dma_sem, 16)
        elif signal:
            self.cnt[eng] += 1
            tok = (eng, self.cnt[eng])
            inc = (eng, 1)
        else:
            tok = (eng, self.cnt[eng] + 1)
            inc = None
        for b in reads:
            if b.readers.get(tok[0], 0) < tok[1]:
                b.readers[tok[0]] = tok[1]
        for b in writes:
            b.last_w = tok
            b.readers = {}
        self.ops[eng].append((waits, issue, inc))
        return tok

    def final_wait(self, eng, toks):
        self.finals.append((eng, [(k, v) for (k, v) in toks]))

    def schedule(self, window=96, lat=350.0):
        prog = self.prog
        units = []
        open_pe = None
        for o in prog:
            if o["eng"] == "pe":
                if open_pe is None:
                    open_pe = dict(eng="pe", ops=[], deps=set(), idx=len(units))
                    units.append(open_pe)
                open_pe["ops"].append(o)
                o["unit"] = open_pe["idx"]
                if o["signal"]:
                    open_pe = None
            else:
                u = dict(eng=o["eng"], ops=[o], deps=set(), idx=len(units))
                units.append(u)
                o["unit"] = u["idx"]
        assert open_pe is None
        lastw = {}
        rdrs = {}
        for o in prog:
            u = o["unit"]
            d = units[u]["deps"]
            for b in o["reads"]:
                w = lastw.get(id(b))
                if w is not None and w != u:
                    d.add(w)
            for b in o["writes"]:
                w = lastw.get(id(b))
                if w is not None and w != u:
                    d.add(w)
                for r in rdrs.get(id(b), ()):
                    if r != u:
                        d.add(r)
            for b in o["reads"]:
                rdrs.setdefault(id(b), set()).add(u)
            for b in o["writes"]:
                lastw[id(b)] = u
                rdrs[id(b)] = set()
        n = len(units)
        succ = [[] for _ in range(n)]
        ndep = [0] * n
        for u in units:
            ndep[u["idx"]] = len(u["deps"])
            for d in u["deps"]:
                succ[d].append(u["idx"])
        per_eng = {e: [u["idx"] for u in units if u["eng"] == e] for e in ENGS}
        pos = {e: 0 for e in ENGS}
        done = [False] * n
        fin = [0.0] * n
        free = {e: 0.0 for e in ENGS}
        order = []
        remaining = n
        while remaining:
            best = None
            for e in ENGS:
                lst = per_eng[e]
                p = pos[e]
                while p < len(lst) and done[lst[p]]:
                    p += 1
                pos[e] = p
                cnt = 0
                q = p
                win = 1 if e == "pe" else window
                while q < len(lst) and cnt < win:
                    ui = lst[q]
                    q += 1
                    if done[ui]:
                        continue
                    cnt += 1
                    if ndep[ui]:
                        continue
                    u = units[ui]
                    st = free[e]
                    for d in u["deps"]:
                        t = fin[d] + lat
                        if t > st:
                            st = t
                    if best is None or st < best[0] or (st == best[0] and ui < best[1]):
                        best = (st, ui)
                    if st <= free[e]:
                        break
            st, ui = best
            u = units[ui]
            e = u["eng"]
            t = st
            for o in u["ops"]:
                t += o["dur"]
            free[e] = t
            fin[ui] = st + sum(o["dur"] for o in u["ops"][:-1]) + u["ops"][-1]["avail"]
            done[ui] = True
            remaining -= 1
            for sidx in succ[ui]:
                ndep[sidx] -= 1
            order.append(ui)
        self.est_ns = max(fin)
        for ui in order:
            for o in units[ui]["ops"]:
                self._op(o["eng"], o["issue"], o["reads"], o["writes"], o["signal"], o["dma_sem"])

    def emit(self, nc):
        import os as _os
        self.schedule(window=int(_os.environ.get("KWINDOW", "96")))
        for eng, toks in self.finals:
            self.ops[eng].append((toks, None, None))
        names = [e for e in ENGS if e != "sp"] + self.dma_sems
        with contextlib.ExitStack() as st:
            sems = {n: st.enter_context(nc.semaphore("s_" + n)) for n in names}
            block = st.enter_context(nc.Block())

            def replay(name, e):
                for waits, issue, inc in self.ops[name]:
                    for k, v in waits:
                        e.wait_ge(sems[k], v)
                    if issue is None:
                        continue
                    ins = issue(e)
                    if inc is not None:
                        ins.then_inc(sems[inc[0]], inc[1])

            @block.tensor
            def _(e):
                replay("pe", e)

            @block.scalar
            def _(e):
                replay("act", e)

            @block.vector
            def _(e):
                replay("dve", e)

            @block.gpsimd
            def _(e):
                replay("pool", e)

            @block.sync
            def _(e):
                replay("sp", e)
                for n_ in names:
                    if self.cnt[n_] > 0:
                        e.wait_ge(sems[n_], self.cnt[n_])
                for n_ in names:
                    e.sem_clear(sems[n_])


class V:
    __slots__ = ("ap", "bufs")

    def __init__(self, ap, bufs):
        self.ap = ap
        self.bufs = list(bufs)

    def __getitem__(self, k):
        return V(self.ap[k], self.bufs)

    def bitcast(self, dt):
        return V(self.ap.bitcast(dt), self.bufs)

    def rearrange(self, s, **kw):
        return V(self.ap.rearrange(s, **kw), self.bufs)

    def bcast3(self, n):
        return V(self.ap.unsqueeze(2).broadcast_to(list(self.ap.shape) + [n]), self.bufs)


def plan(layers):
    ch = []
    for l in layers:
        if l == 0:
            for hp in range(4):
                for k in ("q", "f", "i", "g"):
                    ch.append(("hg_in", k, hp))
            for hp in range(4):
                ch.append(("hg_out", hp))
        else:
            ch.append(("gla_lr",))
            for hd in range(4):
                ch.append(("gla_qk", hd))
                ch.append(("gla_v", hd))
                ch.append(("gla_g", hd))
            for hd in range(4):
                ch.append(("gla_out", hd))
        for g in range(2):
            cs = list(range(g * 11, (g + 1) * 11))
            for c in cs:
                ch.append(("up", l, c))
            for half in range(2):
                for part in (cs[0:4], cs[4:8], cs[8:11]):
                    ch.append(("down", l, half, tuple(part)))
    return ch


def _cols(W, cols):
    sub = W[:, cols].reshape(8, 128, len(cols)).transpose(1, 0, 2)
    return sub.reshape(128, -1)


def pack_wstream(inp, layers):
    chunks = plan(layers)
    out = np.zeros((len(chunks), 128, 2048), np.float32)
    hg_in = inp["hg_w_in"][0]
    hg_out = inp["hg_w_out"][0]
    gl_in = inp["gla_w_in"][0]
    gl_out = inp["gla_w_out"][0]
    for i, c in enumerate(chunks):
        kind = c[0]
        if kind == "hg_in":
            base = {"q": 0, "f": 1024, "i": 2048, "g": 3072}[c[1]] + c[2] * 256
            out[i] = _cols(hg_in, np.arange(base, base + 256))
        elif kind == "hg_out":
            r0 = c[1] * 256
            out[i] = hg_out[r0:r0 + 256].reshape(2, 128, 1024).transpose(1, 0, 2).reshape(128, 2048)
        elif kind == "gla_lr":
            out[i, :, :1024] = _cols(gl_in, np.arange(2960, 3088))
        elif kind == "gla_qk":
            hd = c[1]
            cols = np.concatenate([np.arange(hd * 128, hd * 128 + 128), np.arange(512 + hd * 128, 512 + hd * 128 + 128)])
            out[i] = _cols(gl_in, cols)
        elif kind == "gla_v":
            out[i] = _cols(gl_in, np.arange(1024 + c[1] * 256, 1024 + c[1] * 256 + 256))
        elif kind == "gla_g":
            out[i] = _cols(gl_in, np.arange(2048 + c[1] * 256, 2048 + c[1] * 256 + 256))
        elif kind == "gla_out":
            r0 = c[1] * 256
            out[i] = gl_out[r0:r0 + 256].reshape(2, 128, 1024).transpose(1, 0, 2).reshape(128, 2048)
        elif kind == "up":
            l, ct = c[1], c[2]
            cols = np.concatenate([np.arange(ct * 128, ct * 128 + 128), np.arange(DFF + ct * 128, DFF + ct * 128 + 128)])
            out[i] = _cols(inp["ffn_w_up"][l], cols)
        elif kind == "down":
            l, half, part = c[1], c[2], c[3]
            wd = inp["ffn_w_down"][l]
            for j, ct in enumerate(part):
                out[i, :, j * 512:(j + 1) * 512] = wd[ct * 128:(ct + 1) * 128, half * 512:(half + 1) * 512]
    return out


_CO = {}
_off = 0
for _n, _w in (("nm", 16), ("nf", 16), ("nfin", 8), ("lbl", 24), ("hgnw", 1), ("bgk", 4), ("glanw", 2),
               ("cw", 264), ("cb", 88), ("eps", 1), ("one", 1), ("lnc", 1), ("ident", 128), ("cmask", 128),
               ("smask", 512), ("w2", 512)):
    _CO[_n] = _off
    _off += _w
NCONST = _off


def pack_consts(inp):
    c = np.zeros((128, NCONST), np.float32)

    def put(name, arr):
        arr = np.asarray(arr, np.float32)
        c[:, _CO[name]:_CO[name] + arr.shape[1]] = arr

    pd = lambda v: np.asarray(v, np.float32).reshape(-1, 128).T
    put("nm", np.concatenate([pd(inp["norm_mixer_w"][l]) for l in range(2)], axis=1))
    put("nf", np.concatenate([pd(inp["norm_ffn_w"][l]) for l in range(2)], axis=1))
    put("nfin", pd(inp["norm_final_w"]))
    put("lbl", np.concatenate([pd(inp["lb_logits"][k]) for k in range(3)], axis=1))
    put("hgnw", pd(inp["hg_norm_w"][0]))
    put("bgk", pd(inp["gla_b_gk_up"][0]))
    put("glanw", pd(inp["gla_norm_w"][0]))
    cw = np.asarray(inp["ffn_conv_w"], np.float32)
    put("cw", np.concatenate([pd(cw[l, j]) for l in range(2) for j in range(3)], axis=1))
    cb = np.asarray(inp["ffn_conv_b"], np.float32)
    put("cb", np.concatenate([pd(cb[l]) for l in range(2)], axis=1))
    put("eps", np.full((128, 1), 1e-6, np.float32))
    put("one", np.ones((128, 1), np.float32))
    put("lnc", np.full((128, 1), np.log(128.0 ** -0.5), np.float32))
    put("ident", np.eye(128, dtype=np.float32))
    s = np.arange(128)[:, None]
    t = np.arange(128)[None, :]
    put("cmask", ((s // 64 == t // 64) & (s <= t)).astype(np.float32))
    sm = np.ones((128, 512), np.float32)
    sm[:, ::64] = 0.0
    put("smask", sm)
    w2 = np.zeros((128, 512), np.float32)
    w2[112:128, :] = np.asarray(inp["gla_w_gk_up"][0], np.float32)
    put("w2", w2)
    return c


def build(NT, layers=(0, 1), final_norm=True, same_sync=("act", "dve", "pool")):
    nc = bass.Bass("TRN2", target_bir_lowering=False)
    nc.dge_precook = False
    chunks = plan(layers)
    NCH = len(chunks)
    xT_d = nc.dram_tensor("xT", [NT, 128, 8 * TT], F32, kind="ExternalInput").ap()
    ws_d = nc.dram_tensor("wstream", [NCH, 128, 2048], F32R, kind="ExternalInput").ap()
    cs_d = nc.dram_tensor("consts", [128, NCONST], F32, kind="ExternalInput").ap()
    out_d = nc.dram_tensor("outT", [NT, 128, 8 * TT], F32, kind="ExternalOutput").ap()

    R = Rec(same_engine_sync=same_sync)
    with contextlib.ExitStack() as st:
        def sbt(name, shape, dt=F32, nbuf=1):
            t = st.enter_context(nc.sbuf_tensor(name, shape, dt))
            return t, [Buf(f"{name}{i}") for i in range(nbuf)]

        def tile(name, shape, dt=F32):
            t, b = sbt(name, shape, dt)
            return V(t[:], b)

        hT_t, hT_b = sbt("hT", [128, 8, TT], F32, 8)
        yT_t, yT_b = sbt("yT", [128, 8, TT], F32R, 8)
        oT_t, oT_b = sbt("oT", [128, 8, TT], F32R, 8)
        aT_t, aT_b = sbt("aT", [128, 11, TT], F32R, 11)
        hT = [V(hT_t[:, i, :], [hT_b[i]]) for i in range(8)]
        yT = [V(yT_t[:, i, :], [yT_b[i]]) for i in range(8)]
        oT = [V(oT_t[:, i, :], [oT_b[i]]) for i in range(8)]
        aT = [V(aT_t[:, i, :], [aT_b[i]]) for i in range(11)]
        hT_all = V(hT_t[:], hT_b)
        CS = tile("consts_sb", [128, NCONST])
        slots = [tile(f"slot{i}", [128, 2048], F32R) for i in range(NSLOT)]
        slot_sems = [R.new_dma_sem(f"ws{i}") for i in range(NSLOT)]

        def cst(name, j=0, w=1):
            o = _CO[name] + j
            return CS[:, o:o + w]

        ones1024 = tile("ones1024", [128, 128], F32R)
        ones128 = tile("ones128", [128, 128], F32R)
        ones256 = tile("ones256", [128, 128], F32R)
        ident_bf = tile("ident_bf", [128, 128], BF16)
        lb = tile("lb", [128, 8])
        clb = tile("clb", [128, 8])
        lbe = tile("lbe", [128, 24])
        nbgk = tile("nbgk", [128, 4])
        W2r = tile("W2r", [128, 512], F32R)
        G_sb = tile("G_sb", [128, TT], F32R)
        rstd = tile("rstd", [128, TT])
        S32hg = [tile(f"S32hg{h}", [128, 128]) for h in range(8)]
        Sbfhg = [tile(f"Sbfhg{h}", [128, 128], BF16) for h in range(8)]
        S32gl = [tile(f"S32gl{h}", [128, 256]) for h in range(4)]
        Sbfgl = [tile(f"Sbfgl{h}", [128, 256], BF16) for h in range(4)]
        tails_t = st.enter_context(nc.sbuf_tensor("tails", [128, 2, 44, 2], F32))
        tails = [[V(tails_t[:, l, ct, :], [Buf(f"tail{l}_{ct}")]) for ct in range(44)] for l in range(2)]
        TM = []
        for s in range(2):
            d = {}
            for n in ("C", "D", "SG1", "E"):
                d[n] = tile(f"t{n}{s}", [128, TT])
            d["XA"] = tile(f"tXA{s}", [128, TT + 2])
            d["XG"] = tile(f"tXG{s}", [128, TT + 2])
            d["A"] = d["XA"][:, 0:TT]
            d["B"] = d["XG"][:, 0:TT]
            d["UA"] = d["C"]
            d["UG"] = d["D"]
            d["OSQ0"] = tile(f"tOSQ0{s}", [128, TT], F32R)
            d["OSQ1"] = tile(f"tOSQ1{s}", [128, TT], F32R)
            for n in ("kT", "kkT", "qT"):
                d[n] = tile(f"t{n}{s}", [128, TT], BF16)
            d["kktok"] = tile(f"tkktok{s}", [128, 4, 128], BF16)
            d["vtok"] = tile(f"tvtok{s}", [128, 4, 256], BF16)
            d["scT"] = [tile(f"tscT{s}{i}", [128, 128], BF16) for i in range(2)]
            d["ebl"] = tile(f"tebl{s}", [128, 8])
            TM.append(d)
        banks = []
        for i in range(8):
            t = st.enter_context(nc.psum_tensor(f"bank{i}", [128, TT], F32))
            banks.append((t, [Buf(f"bk{i}a"), Buf(f"bk{i}b")]))
        pstate = {"big": 0, "sc": 0, "tr": 0, "nbig": 6}

        def pbank():
            n = pstate["nbig"]
            i = pstate["big"] % n
            pstate["big"] += 1
            t, b = banks[i]
            return V(t[:], b)

        def psc_region():
            h = pstate["sc"] % 2
            pstate["sc"] += 1
            t, b = banks[6]
            return V(t[:, h * 128:(h + 1) * 128], [b[0]])

        def pU_region(nv):
            t, b = banks[6]
            return V(t[:, 256:256 + nv], [b[1]])

        def ptrans():
            h = pstate["tr"] % 2
            pstate["tr"] += 1
            t, b = banks[7]
            return V(t[:, h * 256:(h + 1) * 256].bitcast(BF16), [b[h]])

        def bufs_of(*vs):
            r = []
            for v in vs:
                if isinstance(v, V):
                    r += v.bufs
            return r

        def apof(v):
            return v.ap if isinstance(v, V) else v

        def mm(out, lhsT, rhs, start=True, stop=True, signal=None):
            sig = stop if signal is None else signal
            R.op("pe", lambda e: e.matmul(out.ap, lhsT.ap, rhs.ap, start=start, stop=stop),
                 reads=lhsT.bufs + rhs.bufs, writes=out.bufs, signal=sig,
                 dur=max(64, rhs.ap.shape[-1]) / 2.2 + 20)

        def tr(out, in_):
            R.op("pe", lambda e: e.transpose(out.ap, in_.ap, ident_bf.ap),
                 reads=in_.bufs + ident_bf.bufs, writes=out.bufs, signal=True, dur=90)

        def nfree(v):
            n = 1
            for d in v.ap.shape[1:]:
                n *= d
            return n

        def edur(eng, v, mult=1.0):
            n = nfree(v) * mult
            if eng == "act":
                return 220 + 0.85 * n
            if eng == "dve":
                return 120 + 1.0 * n
            return 150 + 2.3 * n

        def ACT(out, in_, func, bias=None, scale=None):
            kw = {}
            if bias is not None:
                kw["bias"] = apof(bias)
            if scale is not None:
                kw["scale"] = apof(scale)
            R.op("act", lambda e: e.activation(out=out.ap, in_=in_.ap, func=func, **kw),
                 reads=bufs_of(in_, bias, scale), writes=out.bufs, dur=edur("act", out))

        def TTo(eng, out, in0, in1, op):
            R.op(eng, lambda e: e.tensor_tensor(out=out.ap, in0=in0.ap, in1=in1.ap, op=op),
                 reads=bufs_of(in0, in1), writes=out.bufs, dur=edur(eng, out))

        def TS(eng, out, in0, s1, op0, s2=None, op1=None):
            kw = {}
            if op1 is None and eng == "pool":
                if op0 == ALU.add:
                    s2, op1 = 1.0, ALU.mult
                elif op0 == ALU.mult:
                    s2, op1 = 0.0, ALU.add
            if op1 is not None:
                kw["op1"] = op1
            R.op(eng, lambda e: e.tensor_scalar(out=out.ap, in0=in0.ap, scalar1=apof(s1), scalar2=apof(s2), op0=op0, **kw),
                 reads=bufs_of(in0, s1, s2), writes=out.bufs, dur=edur(eng, out))

        def STT(eng, out, in0, scalar, in1, op0, op1):
            R.op(eng, lambda e: e.scalar_tensor_tensor(out=out.ap, in0=in0.ap, scalar=apof(scalar), in1=in1.ap, op0=op0, op1=op1),
                 reads=bufs_of(in0, scalar, in1), writes=out.bufs, dur=edur(eng, out))

        def SCAN(out, d0, d1):
            R.op("dve", lambda e: e.tensor_tensor_scan(out=out.ap, data0=d0.ap, data1=d1.ap, initial=0.0, op0=ALU.mult, op1=ALU.add),
                 reads=bufs_of(d0, d1), writes=out.bufs, dur=edur("dve", out, 2.0))

        def RECIP(eng, out, in_, exact=True):
            R.op(eng, lambda e: e.reciprocal(out=out.ap, in_=in_.ap), reads=in_.bufs, writes=out.bufs, dur=edur(eng, out, 6.0))

        def CP(eng, out, in_):
            if eng == "act":
                R.op("act", lambda e: e.copy(out=out.ap, in_=in_.ap), reads=in_.bufs, writes=out.bufs, dur=edur("act", out))
            else:
                R.op(eng, lambda e: e.tensor_copy(out=out.ap, in_=in_.ap), reads=in_.bufs, writes=out.bufs, dur=edur(eng, out))

        def MEMSET(eng, out, val):
            R.op(eng, lambda e: e.memset(out.ap, val), writes=out.bufs, dur=edur(eng, out, 0.5))

        wst = {"next_dma": 0, "cur": 0}
        total_chunks = NCH * NT

        def ws_get(kind, lag=1):
            j = wst["cur"]
            wst["cur"] += 1
            assert chunks[j % NCH][0] == kind, (chunks[j % NCH], kind)
            upto = min(total_chunks - 1, j + NSLOT - lag)
            while wst["next_dma"] <= upto:
                m = wst["next_dma"]
                wst["next_dma"] += 1
                k = m % NCH
                sl = slots[m % NSLOT]
                if chunks[k][0] == "gla_lr":
                    R.op("sp", lambda e, sl=sl, k=k: e.dma_start(out=sl.ap[:, 0:1024], in_=ws_d[k, :, 0:1024]),
                         writes=sl.bufs, dma_sem=slot_sems[m % NSLOT], dur=600, avail=4000)
                else:
                    R.op("sp", lambda e, sl=sl, k=k: e.dma_start(out=sl.ap, in_=ws_d[k]),
                         writes=sl.bufs, dma_sem=slot_sems[m % NSLOT], dur=600, avail=5500)
            return slots[j % NSLOT]

        s_c = R.new_dma_sem("ld_c")
        s_h = R.new_dma_sem("ld_h")
        s_o = R.new_dma_sem("st_o")
        R.op("sp", lambda e: e.dma_start(out=CS.ap, in_=cs_d), writes=CS.bufs, dma_sem=s_c, dur=600, avail=4000)
        onesf = tile("onesf", [128, 128])
        for ot, val in ((ones1024, 1.0 / 1024), (ones128, 1.0 / 128), (ones256, 1.0 / 256)):
            MEMSET("pool", onesf, val)
            CP("act", ot, onesf)
        for h in range(8):
            MEMSET("pool", S32hg[h], 0.0)
            MEMSET("pool", Sbfhg[h], 0.0)
        for h in range(4):
            MEMSET("pool", S32gl[h], 0.0)
            MEMSET("pool", Sbfgl[h], 0.0)
        tails_all = V(tails_t[:], [tails[l][ct].bufs[0] for l in range(2) for ct in range(44)])
        MEMSET("pool", tails_all, 0.0)
        CP("act", ident_bf, cst("ident", 0, 128))
        CP("act", W2r, cst("w2", 0, 512))
        ACT(lbe, cst("lbl", 0, 24), AF.Exp)
        TTo("pool", lb, lbe[:, 0:8], lbe[:, 8:16], ALU.add)
        TTo("pool", lb, lb, lbe[:, 16:24], ALU.add)
        RECIP("dve", lb, lb, exact=True)
        TTo("dve", lb, lb, lbe[:, 0:8], ALU.mult)
        ACT(clb, lb, AF.Ln, scale=-1.0, bias=cst("one"))
        TS("pool", nbgk, cst("bgk", 0, 4), -1.0, ALU.mult)
        one = cst("one")
        eps = cst("eps")
        cmask = cst("cmask", 0, 128)
        smask = cst("smask", 0, 512)

        def rmsnorm(wname, woff, dst):
            for dt in range(8):
                ACT(yT[dt], hT[dt], AF.Square)
            pb = pbank()
            for dt in range(8):
                mm(pb, ones1024, yT[dt], start=(dt == 0), stop=(dt == 7))
            ACT(rstd, pb, AF.Ln, bias=eps)
            ACT(rstd, rstd, AF.Exp, scale=-0.5)
            for dt in range(8):
                if dt % 2 == 0:
                    STT("dve", dst[dt], hT[dt], cst(wname, woff + dt), rstd, ALU.mult, ALU.mult)
                else:
                    ACT(dst[dt], hT[dt], AF.Identity, scale=cst(wname, woff + dt))
                    TTo("pool", dst[dt], dst[dt], rstd, ALU.mult)

        def out_proj(kind):
            so = [ws_get(kind, lag=i + 1) for i in range(4)]
            for ft in range(8):
                pw = pbank()
                for h in range(8):
                    w = so[h // 2].rearrange("p (a f) -> p a f", a=2)[:, h % 2, ft * 128:(ft + 1) * 128]
                    mm(pw, w, oT[h], start=(h == 0), stop=(h == 7))
                TTo("dve", hT[ft], hT[ft], pw, ALU.add)

        def recurrence(tm, po, n_vt, Sbf, S32, vcol0):
            kT, kkT, qT, kktok, vtok, ebl = tm["kT"], tm["kkT"], tm["qT"], tm["kktok"], tm["vtok"], tm["ebl"]
            nv = 128 * n_vt
            for blk in range(4):
                cols = slice(blk * 128, (blk + 1) * 128)
                psc = psc_region()
                mm(psc, kT[:, cols], qT[:, cols])
                scT = tm["scT"][blk % 2]
                TTo("dve", scT, psc, cmask, ALU.mult)
                for vt in range(n_vt):
                    mm(po[vt][:, cols], vtok[:, blk, vcol0 + vt * 128:vcol0 + (vt + 1) * 128], scT,
                       start=True, stop=False, signal=False)
                for ci in range(2):
                    c = blk * 2 + ci
                    ccols = slice(c * 64, (c + 1) * 64)
                    rows = slice(ci * 64, (ci + 1) * 64)
                    for vt in range(n_vt):
                        mm(po[vt][:, ccols], Sbf[:, vt * 128:(vt + 1) * 128], qT[:, ccols],
                           start=False, stop=(ci == 1), signal=(vt == n_vt - 1))
                    pU = pU_region(nv)
                    mm(pU, kktok[rows, blk, :], vtok[rows, blk, vcol0:vcol0 + nv])
                    STT("dve", S32, S32, ebl[:, c:c + 1], pU, ALU.mult, ALU.add)
                    CP("act", Sbf, S32)
                yield

        def kk_transposes(tm):
            ptr = ptrans()
            for blk in range(4):
                tr(ptr[:, blk * 128:(blk + 1) * 128], tm["kkT"][:, blk * 128:(blk + 1) * 128])
            CP("act", tm["kktok"].rearrange("p a b -> p (a b)"), ptr)

        def silu_gate(dst, pg):
            ACT(dst, pg, AF.Exp, scale=-1.0)
            ACT(dst, dst, AF.Ln, bias=one)
            ACT(dst, dst, AF.Exp, scale=-1.0)
            TTo("dve", dst, pg, dst, ALU.mult)

        def mixer_hg():
            rmsnorm("nm", 0, yT)
            for hp in range(4):
                sq = ws_get("hg_in", 1).rearrange("p (dt f) -> p dt f", dt=8)
                sf = ws_get("hg_in", 2).rearrange("p (dt f) -> p dt f", dt=8)
                si = ws_get("hg_in", 3).rearrange("p (dt f) -> p dt f", dt=8)
                sg = ws_get("hg_in", 4).rearrange("p (dt f) -> p dt f", dt=8)
                tmv = TM[hp % 2]
                for blk in range(4):
                    pv = pbank()[:, 0:256]
                    for dt in range(8):
                        mm(pv, yT[dt][:, blk * 128:(blk + 1) * 128], si[:, dt, :], start=(dt == 0), stop=(dt == 7))
                    CP("act", tmv["vtok"][:, blk, :], pv)
                tms = []
                for a in range(2):
                    h = 2 * hp + a
                    tm = dict(TM[h % 2])
                    tm["vtok"] = tmv["vtok"]
                    tms.append(tm)
                    A_, B_, C_, D_ = tm["A"], tm["B"], tm["C"], tm["D"]
                    fc = slice(a * 128, (a + 1) * 128)
                    pf = pbank()
                    for dt in range(8):
                        mm(pf, sf[:, dt, fc], yT[dt], start=(dt == 0), stop=(dt == 7))
                    pq = pbank()
                    for dt in range(8):
                        mm(pq, sq[:, dt, fc], yT[dt], start=(dt == 0), stop=(dt == 7))
                    pg = pbank()
                    for dt in range(8):
                        mm(pg, sg[:, dt, fc], yT[dt], start=(dt == 0), stop=(dt == 7))
                    ACT(A_, pf, AF.Exp, scale=-1.0)
                    ACT(B_, A_, AF.Ln, bias=one)
                    ACT(C_, A_, AF.Ln, scale=lb[:, h:h + 1], bias=one)
                    TTo("pool", C_, C_, B_, ALU.subtract)
                    SCAN(D_, smask, C_)
                    TTo("dve", A_, pf, B_, ALU.add)
                    TTo("pool", A_, A_, D_, ALU.add)
                    ACT(tm["kT"], A_, AF.Exp, scale=-1.0, bias=clb[:, h:h + 1])
                    ACT(tm["ebl"], D_[:, 63::64], AF.Exp)
                    TTo("pool", tm["kkT"].rearrange("p (c k) -> p c k", k=64),
                        tm["kT"].rearrange("p (c k) -> p c k", k=64), tm["ebl"].bcast3(64), ALU.mult)
                    kk_transposes(tm)
                    ACT(C_, pq, AF.Exp, scale=-1.0)
                    ACT(C_, C_, AF.Ln, bias=one)
                    TTo("pool", C_, D_, C_, ALU.subtract)
                    ACT(C_, C_, AF.Exp, bias=cst("lnc"))
                    TTo("dve", tm["qT"], pq, C_, ALU.mult)
                    silu_gate(A_, pg)
                pos = [pbank(), pbank()]
                gens = [recurrence(tms[a], [pos[a]], 1, Sbfhg[2 * hp + a], S32hg[2 * hp + a], a * 128) for a in range(2)]
                for blk in range(4):
                    for g in gens:
                        next(g)
                for a in range(2):
                    h = 2 * hp + a
                    tm = tms[a]
                    A_, C_, D_ = tm["A"], tm["C"], tm["D"]
                    po = pos[a]
                    ACT(tm["OSQ0"], po, AF.Square)
                    pss = pbank()
                    mm(pss, ones128, tm["OSQ0"])
                    ACT(C_, pss, AF.Ln, bias=eps)
                    ACT(C_, C_, AF.Exp, scale=-0.5)
                    STT("dve", D_, po, cst("hgnw"), C_, ALU.mult, ALU.mult)
                    TTo("pool", oT[h], D_, A_, ALU.mult)
            out_proj("hg_out")

        def mixer_gla():
            rmsnorm("nm", 8, yT)
            slr = ws_get("gla_lr", 1)[:, 0:1024].rearrange("p (dt f) -> p dt f", dt=8)
            pG = pbank()
            for dt in range(8):
                mm(pG, slr[:, dt, :], yT[dt], start=(dt == 0), stop=(dt == 7))
            CP("act", G_sb, pG)
            for hd0 in (0, 2):
                st = []
                for hd in (hd0, hd0 + 1):
                    sqk = ws_get("gla_qk", 1).rearrange("p (dt two f) -> p dt two f", dt=8, two=2)
                    sv = ws_get("gla_v", 2).rearrange("p (dt f) -> p dt f", dt=8)
                    sg = ws_get("gla_g", 3).rearrange("p (dt f) -> p dt f", dt=8)
                    tm = TM[hd % 2]
                    A_, B_, C_, D_ = tm["A"], tm["B"], tm["C"], tm["D"]
                    SG = [tm["E"], tm["SG1"]]
                    for vt in range(2):
                        pg = pbank()
                        for dt in range(8):
                            mm(pg, sg[:, dt, vt * 128:(vt + 1) * 128], yT[dt], start=(dt == 0), stop=(dt == 7))
                        silu_gate(SG[vt], pg)
                    for blk in range(4):
                        pv = pbank()[:, 0:256]
                        for dt in range(8):
                            mm(pv, yT[dt][:, blk * 128:(blk + 1) * 128], sv[:, dt, :], start=(dt == 0), stop=(dt == 7))
                        CP("act", tm["vtok"][:, blk, :], pv)
                    pgk = pbank()
                    mm(pgk, W2r[:, hd * 128:(hd + 1) * 128], G_sb)
                    ACT(A_, pgk, AF.Exp, scale=-1.0, bias=nbgk[:, hd:hd + 1])
                    ACT(A_, A_, AF.Ln, bias=one)
                    SCAN(D_, smask, A_)
                    ACT(B_, D_, AF.Exp, scale=-1.0 / 16)
                    ACT(C_, D_, AF.Exp, scale=1.0 / 16)
                    ACT(tm["ebl"], D_[:, 63::64], AF.Exp, scale=-1.0 / 16)
                    pq = pbank()
                    for dt in range(8):
                        mm(pq, sqk[:, dt, 0, :], yT[dt], start=(dt == 0), stop=(dt == 7))
                    pk = pbank()
                    for dt in range(8):
                        mm(pk, sqk[:, dt, 1, :], yT[dt], start=(dt == 0), stop=(dt == 7))
                    STT("dve", tm["qT"], pq, 128 ** -0.5, B_, ALU.mult, ALU.mult)
                    TTo("dve", tm["kT"], pk, C_, ALU.mult)
                    TTo("pool", tm["kkT"].rearrange("p (c k) -> p c k", k=64),
                        tm["kT"].rearrange("p (c k) -> p c k", k=64), tm["ebl"].bcast3(64), ALU.mult)
                    kk_transposes(tm)
                    st.append((hd, tm, SG))
                pos = [[pbank(), pbank()] for _ in st]
                gens = [recurrence(tm, pos[i], 2, Sbfgl[hd], S32gl[hd], 0) for i, (hd, tm, SG) in enumerate(st)]
                for blk in range(4):
                    for g in gens:
                        next(g)
                for i, (hd, tm, SG) in enumerate(st):
                    C_, D_ = tm["C"], tm["D"]
                    po = pos[i]
                    OSQ = [tm["OSQ0"], tm["OSQ1"]]
                    for vt in range(2):
                        ACT(OSQ[vt], po[vt], AF.Square)
                    pss = pbank()
                    mm(pss, ones256, OSQ[0], start=True, stop=False)
                    mm(pss, ones256, OSQ[1], start=False, stop=True)
                    ACT(C_, pss, AF.Ln, bias=eps)
                    ACT(C_, C_, AF.Exp, scale=-0.5)
                    for vt in range(2):
                        STT("dve", D_, po[vt], cst("glanw", vt), C_, ALU.mult, ALU.mult)
                        TTo("pool", oT[hd * 2 + vt], D_, SG[vt], ALU.mult)
            out_proj("gla_out")

        def ffn(l):
            rmsnorm("nf", 8 * l, yT)
            pstate["nbig"] = 8
            for g in range(2):
                for ci in range(11):
                    c = g * 11 + ci
                    su = ws_get("up", 1).rearrange("p (dt two f) -> p dt two f", dt=8, two=2)
                    tm = TM[ci % 2]
                    pa = pbank()
                    for dt in range(8):
                        mm(pa, su[:, dt, 0, :], yT[dt], start=(dt == 0), stop=(dt == 7))
                    pg = pbank()
                    for dt in range(8):
                        mm(pg, su[:, dt, 1, :], yT[dt], start=(dt == 0), stop=(dt == 7))
                    for (ps, ct, xs, u, e1, e2) in ((pa, c, tm["XA"], tm["UA"], "dve", "dve"),
                                                     (pg, NCT + c, tm["XG"], tm["UG"], "dve", "dve")):
                        cw = lambda j, ct=ct: cst("cw", (l * 3 + j) * 44 + ct)
                        CP("act", xs[:, 2:TT + 2], ps)
                        CP("pool", xs[:, 0:2], tails[l][ct])
                        ACT(u, xs[:, 0:TT], AF.Identity, scale=cw(0), bias=cst("cb", l * 44 + ct))
                        STT(e1, u, xs[:, 1:TT + 1], cw(1), u, ALU.mult, ALU.add)
                        STT(e2, u, xs[:, 2:TT + 2], cw(2), u, ALU.mult, ALU.add)
                        CP("pool", tails[l][ct], xs[:, TT:TT + 2])
                    E = tm["E"]
                    ACT(E, tm["UG"], AF.Silu)
                    TTo("pool", aT[ci], E, tm["UA"], ALU.mult)
                for half in range(2):
                    accs = [pbank() for _ in range(4)]
                    for part in (range(0, 4), range(4, 8), range(8, 11)):
                        sd = ws_get("down", 1).rearrange("p (i f) -> p i f", i=4)
                        for i, ci in enumerate(part):
                            for j in range(4):
                                mm(accs[j], sd[:, i, j * 128:(j + 1) * 128], aT[ci], start=(ci == 0), stop=(ci == 10))
                    for j in range(4):
                        ft = half * 4 + j
                        TTo("dve", hT[ft], hT[ft], accs[j], ALU.add)
            pstate["nbig"] = 6
            pstate["big"] = 0

        out_toks = []
        for tI in range(NT):
            t0 = tI * TT
            R.op("sp", lambda e, tI=tI: e.dma_start(out=hT_t[:], in_=xT_d[tI].rearrange("p (dt t) -> p dt t", dt=8)),
                 writes=hT_b, dma_sem=s_h, dur=600, avail=9000)
            for l in layers:
                if l == 0:
                    mixer_hg()
                else:
                    mixer_gla()
                ffn(l)
            if final_norm:
                rmsnorm("nfin", 0, hT)
            tok = R.op("sp", lambda e, tI=tI: e.dma_start(out=out_d[tI].rearrange("p (dt t) -> p dt t", dt=8), in_=hT_t[:]),
                       reads=hT_b, dma_sem=s_o, dur=600, avail=6000)
        R.final_wait("sp", [(s_o, 16 * NT)])
        assert wst["cur"] == total_chunks
        R.emit(nc)
    return nc


_PROGS = {}


def _prog(NT, layers, final_norm):
    key = (NT, tuple(layers), final_norm)
    if key not in _PROGS:
        _PROGS[key] = build(NT, layers, final_norm)
    return _PROGS[key]


FUSED = True


def to_tiles(xb):
    nt = xb.shape[0] // TT
    return np.ascontiguousarray(xb.reshape(nt, TT, 8, 128).transpose(0, 3, 2, 1)).reshape(nt, 128, 8 * TT)


def from_tiles(o):
    nt = o.shape[0]
    return np.ascontiguousarray(o.reshape(nt, 128, 8, TT).transpose(0, 3, 2, 1)).reshape(nt * TT, D)


def kernel(**inputs):
    inp = {k: np.asarray(v) for k, v in inputs.items()}
    x = inp["x"].astype(np.float32, copy=False)
    B = x.shape[0]
    consts = pack_consts(inp)
    xT = [to_tiles(x[b]) for b in range(B)]
    if FUSED:
        stages = [((0, 1), True)]
    else:
        stages = [((0,), False), ((1,), True)]
    cur = xT
    for layers, fin in stages:
        ws = pack_wstream(inp, layers)
        nc = _prog(T // TT, layers, fin)
        in_maps = [{"xT": cur[b], "wstream": ws, "consts": consts} for b in range(B)]
        res = run_bass_kernel_spmd(nc, in_maps, core_ids=list(range(B)))
        cur = [np.asarray(res.results[b]["outT"]) for b in range(B)]
    out = np.stack([from_tiles(cur[b]) for b in range(B)], axis=0)
    return out.astype(np.float32, copy=False)
```
